# Optimizing a Trainium2 kernel written in Bass

```python
import jax, jax.numpy as jnp
from jax import lax
import numpy as np

D_MODEL = 4096
BATCH = 2
SEQ = 8192
DEPTH = 2

N_BRANCH = 4
BRANCH_WIDTH = D_MODEL // 4
HEAD_DIM = 128
MOBA_HEADS = BRANCH_WIDTH // HEAD_DIM
MOBA_BLOCK = 256
MOBA_TOPK = 3
MOBA_Q_CHUNK = 32
FOX_HEADS = BRANCH_WIDTH // HEAD_DIM
FOX_Q_BLOCK = 128
POOL_WINDOWS = (2, 4, 8, 16)
POOL_GROUPS = len(POOL_WINDOWS)
POOL_GROUP_WIDTH = BRANCH_WIDTH // POOL_GROUPS
CONV_K = 3
RMS_EPS = 1e-6
NEG_INF = -1e30

_W = BRANCH_WIDTH
_IN_SEGMENTS = (3 * _W, _W,
                3 * _W, _W, FOX_HEADS,
                _W, _W,
                3 * _W, _W,
                N_BRANCH * D_MODEL)
IN_SPLITS = tuple(int(v) for v in np.cumsum(_IN_SEGMENTS)[:-1])
N_IN = int(sum(_IN_SEGMENTS))

kernel_name = "hybrid_moba_fox_pool_conv_gated_merge"


def rms_norm(x, g):
    x32 = x.astype(jnp.float32)
    y = x32 * lax.rsqrt(jnp.mean(x32 * x32, axis=-1, keepdims=True) + RMS_EPS)
    return (y * g.astype(jnp.float32)).astype(x.dtype)


def split_heads(t, n_heads):
    b, s, _ = t.shape
    return t.reshape(b, s, n_heads, HEAD_DIM).transpose(0, 2, 1, 3)


def merge_heads(t):
    b, h, s, d = t.shape
    return t.transpose(0, 2, 1, 3).reshape(b, s, h * d)


def moba_attention(q, k, v):
    b, h, s, d = q.shape
    nb = -(-s // MOBA_BLOCK)
    s_pad = nb * MOBA_BLOCK
    pad = ((0, 0), (0, 0), (0, s_pad - s), (0, 0))
    kb = jnp.pad(k, pad).reshape(b, h, nb, MOBA_BLOCK, d)
    vb = jnp.pad(v, pad).reshape(b, h, nb, MOBA_BLOCK, d)
    scale = HEAD_DIM ** -0.5
    k_mean = jnp.mean(kb.astype(jnp.float32), axis=3)
    gate = jnp.einsum("bhsd,bhnd->bhsn", q.astype(jnp.float32), k_mean)
    q_blk = jnp.arange(s) // MOBA_BLOCK
    past = jnp.arange(nb)[None, :] < q_blk[:, None]
    gate = jnp.where(past, gate, NEG_INF)
    topk = min(MOBA_TOPK, nb)
    _, idx = lax.top_k(gate, topk)
    valid = idx < q_blk[None, None, :, None]

    nc = s // MOBA_Q_CHUNK

    def to_chunks(t):
        t = t.reshape((b, h, nc, MOBA_Q_CHUNK) + t.shape[3:])
        return jnp.moveaxis(t, 2, 0)

    gather = jax.vmap(jax.vmap(lambda blocks, i: blocks[i]))

    def chunk(args):
        qc, ic, vc, c = args
        t_pos = c * MOBA_Q_CHUNK + jnp.arange(MOBA_Q_CHUNK)
        blk = (c * MOBA_Q_CHUNK) // MOBA_BLOCK
        k_sel = gather(kb, ic).reshape(b, h, MOBA_Q_CHUNK, topk * MOBA_BLOCK, d)
        v_sel = gather(vb, ic).reshape(b, h, MOBA_Q_CHUNK, topk * MOBA_BLOCK, d)
        s_sel = jnp.einsum("bhqd,bhqkd->bhqk", qc, k_sel).astype(jnp.float32) * scale
        s_sel = jnp.where(jnp.repeat(vc, MOBA_BLOCK, axis=-1), s_sel, NEG_INF)
        k_own = lax.dynamic_index_in_dim(kb, blk, axis=2, keepdims=False)
        v_own = lax.dynamic_index_in_dim(vb, blk, axis=2, keepdims=False)
        s_own = jnp.einsum("bhqd,bhkd->bhqk", qc, k_own).astype(jnp.float32) * scale
        own_pos = blk * MOBA_BLOCK + jnp.arange(MOBA_BLOCK)
        s_own = jnp.where(own_pos[None, :] <= t_pos[:, None], s_own, NEG_INF)
        p = jax.nn.softmax(jnp.concatenate([s_sel, s_own], axis=-1), axis=-1).astype(v.dtype)
        n_sel = topk * MOBA_BLOCK
        return (jnp.einsum("bhqk,bhqkd->bhqd", p[..., :n_sel], v_sel)
                + jnp.einsum("bhqk,bhkd->bhqd", p[..., n_sel:], v_own))

    out = lax.map(chunk, (to_chunks(q), to_chunks(idx), to_chunks(valid), jnp.arange(nc)))
    return jnp.moveaxis(out, 0, 2).reshape(b, h, s, d)


def fox_attention(q, k, v, log_f):
    b, h, s, d = q.shape
    scale = HEAD_DIM ** -0.5
    c = jnp.cumsum(log_f, axis=-1)
    nq = s // FOX_Q_BLOCK
    k_pos = jnp.arange(s)
    q_blocks = jnp.moveaxis(q.reshape(b, h, nq, FOX_Q_BLOCK, d), 2, 0)
    c_blocks = jnp.moveaxis(c.reshape(b, h, nq, FOX_Q_BLOCK), 2, 0)

    def block(args):
        qb, cb, i = args
        t_pos = i * FOX_Q_BLOCK + jnp.arange(FOX_Q_BLOCK)
        logits = (jnp.einsum("bhqd,bhkd->bhqk", qb, k).astype(jnp.float32) * scale
                  + cb[..., None] - c[:, :, None, :])
        logits = jnp.where(k_pos[None, :] <= t_pos[:, None], logits, NEG_INF)
        p = jax.nn.softmax(logits, axis=-1).astype(v.dtype)
        return jnp.einsum("bhqk,bhkd->bhqd", p, v)

    out = lax.map(block, (q_blocks, c_blocks, jnp.arange(nq)))
    return jnp.moveaxis(out, 0, 2).reshape(b, h, s, d)


def pool_mixer(u, w_pool, pool_scale):
    b, s, w_total = u.shape
    u32 = u.astype(jnp.float32)
    csum = jnp.concatenate([jnp.zeros((b, 1, w_total), jnp.float32), jnp.cumsum(u32, axis=1)], axis=1)
    t = jnp.arange(s)
    outs = []
    for g, win in enumerate(POOL_WINDOWS):
        sl = slice(g * POOL_GROUP_WIDTH, (g + 1) * POOL_GROUP_WIDTH)
        cg = csum[..., sl]
        lag = jnp.concatenate([jnp.zeros((b, win - 1, POOL_GROUP_WIDTH), jnp.float32),
                               cg[:, :s + 1 - win]], axis=1)
        cnt = jnp.minimum(t + 1, win).astype(jnp.float32)[None, :, None]
        outs.append((cg[:, 1:] - lag) / cnt - u32[..., sl])
    pooled = jnp.stack(outs, axis=2).astype(u.dtype)
    y = jnp.einsum("bsgc,gcd->bsgd", pooled, w_pool).reshape(b, s, w_total)
    return y * pool_scale


def short_conv_mixer(bcx, w_conv):
    b_g, c_g, xs = jnp.split(bcx, 3, axis=-1)
    z = c_g * xs
    width = z.shape[-1]
    y = lax.conv_general_dilated(z, w_conv[:, None, :].astype(z.dtype), window_strides=(1,),
                                 padding=[(CONV_K - 1, 0)],
                                 dimension_numbers=("NWC", "WIO", "NWC"),
                                 feature_group_count=width)
    return b_g * y


def hybrid_layer(x, g_norm, w_in, b_forget, w_pool, pool_scale, w_conv, w_branch, b_merge, w_out):
    b, s, d = x.shape
    hn = rms_norm(x, g_norm)
    proj = jnp.einsum("bsd,dn->bsn", hn, w_in)
    (a_qkv, a_gate, f_qkv, f_gate, f_forget, p_in, p_gate, c_bcx, c_gate,
     m_gate) = jnp.split(proj, IN_SPLITS, axis=-1)
    qa, ka, va = [split_heads(t, MOBA_HEADS) for t in jnp.split(a_qkv, 3, axis=-1)]
    o_a = merge_heads(moba_attention(qa, ka, va))
    qf, kf, vf = [split_heads(t, FOX_HEADS) for t in jnp.split(f_qkv, 3, axis=-1)]
    log_f = jax.nn.log_sigmoid((f_forget + b_forget).astype(jnp.float32)).transpose(0, 2, 1)
    o_b = merge_heads(fox_attention(qf, kf, vf, log_f))
    o_c = pool_mixer(p_in, w_pool, pool_scale)
    o_d = short_conv_mixer(c_bcx, w_conv)
    branches = (o_a * jax.nn.silu(a_gate), o_b * jax.nn.silu(f_gate),
                o_c * jax.nn.silu(p_gate), o_d * jax.nn.silu(c_gate))
    m_gate = m_gate.reshape(b, s, N_BRANCH, d) + b_merge
    merged = jnp.zeros_like(x)
    for i in range(N_BRANCH):
        up = jnp.einsum("bsw,wd->bsd", branches[i], w_branch[i])
        merged = merged + jax.nn.sigmoid(m_gate[:, :, i]) * up
    return x + jnp.einsum("bsd,de->bse", merged, w_out)


def setup_inputs(seed: int = 0) -> dict:
    key = jax.random.key(seed)
    ks = jax.random.split(key, 12)
    f32 = jnp.float32
    x = jax.random.normal(ks[0], (BATCH, SEQ, D_MODEL), f32)
    norm_g = 1.0 + 0.02 * jax.random.normal(ks[1], (DEPTH, D_MODEL), f32)
    w_in = jax.random.normal(ks[2], (DEPTH, D_MODEL, N_IN), f32) * D_MODEL ** -0.5
    b_forget = jax.random.uniform(ks[3], (DEPTH, FOX_HEADS), f32, minval=1.0, maxval=6.0)
    w_pool = jax.random.normal(ks[4], (DEPTH, POOL_GROUPS, POOL_GROUP_WIDTH, POOL_GROUP_WIDTH), f32) * POOL_GROUP_WIDTH ** -0.5
    pool_scale = 1.0 + 0.02 * jax.random.normal(ks[5], (DEPTH, BRANCH_WIDTH), f32)
    w_conv = jax.random.normal(ks[6], (DEPTH, CONV_K, BRANCH_WIDTH), f32) * CONV_K ** -0.5
    w_branch = jax.random.normal(ks[7], (DEPTH, N_BRANCH, BRANCH_WIDTH, D_MODEL), f32) * BRANCH_WIDTH ** -0.5
    b_merge = 0.02 * jax.random.normal(ks[8], (DEPTH, N_BRANCH, D_MODEL), f32)
    w_out = jax.random.normal(ks[9], (DEPTH, D_MODEL, D_MODEL), f32) * D_MODEL ** -0.5
    final_g = 1.0 + 0.02 * jax.random.normal(ks[10], (D_MODEL,), f32)
    return {"x": x, "norm_g": norm_g, "w_in": w_in, "b_forget": b_forget, "w_pool": w_pool,
            "pool_scale": pool_scale, "w_conv": w_conv, "w_branch": w_branch,
            "b_merge": b_merge, "w_out": w_out, "final_g": final_g}


def reference(x, norm_g, w_in, b_forget, w_pool, pool_scale, w_conv, w_branch, b_merge, w_out, final_g):
    h = x
    for layer in range(DEPTH):
        h = hybrid_layer(h, norm_g[layer], w_in[layer], b_forget[layer], w_pool[layer],
                         pool_scale[layer], w_conv[layer], w_branch[layer], b_merge[layer],
                         w_out[layer])
    return rms_norm(h, final_g)
```

```python
import numpy as np
from contextlib import ExitStack
import concourse.bass as bass
import concourse.mybir as mybir
from concourse.bass_utils import run_bass_kernel_spmd

F32 = mybir.dt.float32
BF16 = mybir.dt.bfloat16
AF = mybir.ActivationFunctionType
ALU = mybir.AluOpType
AX = mybir.AxisListType

NCORES = 8
D = 4096
NCH = D // 128
SEQ = 8192
BATCH = 2
TOK = BATCH * SEQ
TPC = TOK // NCORES
TT = 512
W = 1024
NH = 8
EPS = 1e-6
SCALE = 128 ** -0.5
NEG = -60000.0


class KB:
    def __init__(self, nc, es):
        self.nc = nc
        self.es = es
        self.eng = {"pe": nc.tensor, "act": nc.scalar, "dve": nc.vector, "pool": nc.gpsimd, "sp": nc.sync}
        self.psem = {}
        self.pcnt = {}
        for e in ("pe", "act", "dve", "pool"):
            self.psem[e] = es.enter_context(nc.semaphore("prog_" + e))
            self.pcnt[e] = 0
        self.waited = {}
        self.nsem = 0

    def sb(self, name, shape, dt):
        return self.es.enter_context(self.nc.sbuf_tensor(name, shape, dt))

    def ps(self, name, shape, dt):
        return self.es.enter_context(self.nc.psum_tensor(name, shape, dt))

    def sem(self, name):
        self.nsem += 1
        return self.es.enter_context(self.nc.semaphore(name))

    def mark(self, instr, e):
        self.pcnt[e] += 1
        instr.then_inc(self.psem[e], 1)
        return (self.psem[e], self.pcnt[e], e)

    def wait(self, e, *toks):
        flat = []
        for tok in toks:
            if isinstance(tok, list):
                flat.extend(tok)
            else:
                flat.append(tok)
        for tok in flat:
            if tok is None:
                continue
            sem, val, src = tok
            key = (e, id(sem))
            if self.waited.get(key, 0) >= val:
                continue
            self.waited[key] = val
            self.eng[e].wait_ge(sem, val)


class DmaSem:
    def __init__(self, kb, name):
        self.sem = kb.sem(name)
        self.val = 0

    def inc(self, instr, n=1):
        self.val += 16
        instr.then_inc(self.sem, 16)
        return (self.sem, self.val, "dma")

    def tok(self):
        return (self.sem, self.val, "dma")


def emit_norm_tile(kb, x_rows, grep, xbuf, xn, ss, rs, hT, tpps, tok0, st):
    nc = kb.nc
    kb.wait("sp", st.get("xbuf_free"))
    ld = st["xsem"].inc(nc.sync.dma_start(out=xbuf[:], in_=x_rows))
    kb.wait("act", ld, st.get("xn_free"))
    t_sq = kb.mark(nc.scalar.activation(out=xn[:], in_=xbuf[:], func=AF.Square, accum_out=ss[:]), "act")
    kb.wait("act", t_sq)
    t_act = kb.mark(nc.scalar.activation(out=rs[:], in_=ss[:], func=AF.Sqrt, scale=1.0 / D, bias=st["eps"][:]), "act")
    kb.wait("dve", t_act)
    t_rc = kb.mark(nc.vector.reciprocal(out=rs[:], in_=rs[:]), "dve")
    kb.wait("dve", t_rc)
    t_xn = kb.mark(nc.vector.scalar_tensor_tensor(out=xn[:], in0=xbuf[:], scalar=rs[:, 0:1], in1=grep[:],
                                                  op0=ALU.mult, op1=ALU.mult), "dve")
    st["xbuf_free"] = t_xn
    kb.wait("pe", t_xn)
    last_pe = None
    for c8 in range(NCH // 8):
        slot = st["tp_i"] % len(tpps)
        st["tp_i"] += 1
        kb.wait("pe", st["tp_free"][slot])
        for j in range(8):
            c = c8 * 8 + j
            ins = nc.tensor.transpose(out=tpps[slot][:, j, :], in_=xn[:, c * 128:(c + 1) * 128], identity=st["ident"][:])
        t_pe = kb.mark(ins, "pe")
        last_pe = t_pe
        kb.wait("dve", t_pe, st.get("hT_free"))
        t_ev = kb.mark(nc.vector.tensor_copy(out=hT[:, c8 * 8:(c8 + 1) * 8, tok0:tok0 + 128], in_=tpps[slot][:, :, :]), "dve")
        st["tp_free"][slot] = t_ev
        st["hT_ready"] = t_ev
    st["xn_free"] = last_pe


def make_ident(kb, ident_f, ident):
    nc = kb.nc
    t = kb.mark(nc.gpsimd.memset(ident_f[:], 0.0), "pool")
    kb.wait("pool", t)
    t = kb.mark(nc.gpsimd.affine_select(out=ident_f[:], in_=ident_f[:], pattern=[[-1, 128]], compare_op=ALU.not_equal,
                                        fill=1.0, base=0, channel_multiplier=1), "pool")
    kb.wait("pool", t)
    return kb.mark(nc.gpsimd.tensor_copy(out=ident[:], in_=ident_f[:]), "pool")


def build_phase_a(TPC=TPC):
    nc = bass.Bass("TRN2", target_bir_lowering=False)
    x = nc.dram_tensor("x", [TPC, D], F32, kind="ExternalInput").ap()
    g = nc.dram_tensor("g", [128, D], F32, kind="ExternalInput").ap()
    wa = nc.dram_tensor("wa", [D, 6 * W], F32, kind="ExternalInput").ap()
    wf = nc.dram_tensor("wf", [D, NH], F32, kind="ExternalInput").ap()
    qk = nc.dram_tensor("qk", [4, NH, 128, TPC], F32, kind="ExternalOutput").ap()
    v = nc.dram_tensor("v", [2, TPC, W], F32, kind="ExternalOutput").ap()
    fo = nc.dram_tensor("fo", [NH, TPC], F32, kind="ExternalOutput").ap()
    wa_v = wa.rearrange("(c p) n -> p c n", p=128)
    wf_v = wf.rearrange("(c p) n -> p c n", p=128)
    with ExitStack() as es:
        kb = KB(nc, es)
        grep = kb.sb("grep", [128, D], F32)
        xbuf = kb.sb("xbuf", [128, D], F32)
        xn = kb.sb("xn", [128, D], BF16)
        ss = kb.sb("ss", [128, 1], F32)
        rs = kb.sb("rs", [128, 1], F32)
        epsT = kb.sb("epsT", [128, 1], F32)
        ident_f = kb.sb("ident_f", [128, 128], F32)
        ident = kb.sb("ident", [128, 128], BF16)
        hT = kb.sb("hT", [128, NCH, TT], BF16)
        NST = 3
        stage = [kb.sb(f"stage{i}", [128, 4, 512], F32) for i in range(NST)]
        wblk = [kb.sb(f"wblk{i}", [128, NCH, 512], BF16) for i in range(2)]
        wf_f = kb.sb("wf_f", [128, NCH, NH], F32)
        wf_b = kb.sb("wf_b", [128, NCH, NH], BF16)
        NOST = 4
        ost = [kb.sb(f"ost{i}", [128, 512], F32) for i in range(NOST)]
        tpps = [kb.ps(f"tpps{i}", [128, 8, 128], BF16) for i in range(2)]
        NPS = 4
        mps = [kb.ps(f"mps{i}", [128, 512], F32) for i in range(NPS)]

        st = {"xsem": DmaSem(kb, "xsem"), "tp_i": 0, "tp_free": [None, None], "ident": ident, "eps": epsT}
        csem = DmaSem(kb, "csem")
        stage_sem = [DmaSem(kb, f"stsem{i}") for i in range(NST)]
        stage_free = [None] * NST
        wblk_free = [None] * 2
        ost_sem = [DmaSem(kb, f"ostsem{i}") for i in range(NOST)]
        mps_free = [None] * NPS
        fin = []

        t_id = make_ident(kb, ident_f, ident)
        t_eps = kb.mark(nc.vector.memset(epsT[:], EPS), "dve")
        c1 = csem.inc(nc.sync.dma_start(out=grep[:], in_=g[:, :]))
        c2 = csem.inc(nc.sync.dma_start(out=wf_f[:], in_=wf_v))
        kb.wait("dve", c2)
        t_wf = kb.mark(nc.vector.tensor_copy(out=wf_b[:], in_=wf_f[:]), "dve")
        kb.wait("dve", c1)
        kb.wait("pe", t_id, t_wf)
        kb.wait("act", t_eps)

        piece_i = 0
        blk_i = 0
        mps_i = 0
        ost_i = 0
        last_mm = None
        for tt in range(TPC // TT):
            st["hT_free"] = last_mm
            for s in range(TT // 128):
                r0 = tt * TT + s * 128
                emit_norm_tile(kb, x[r0:r0 + 128, :], grep, xbuf, xn, ss, rs, hT, tpps, s * 128, st)
            kb.wait("pe", st["hT_ready"])
            ms = mps_i % NPS
            mps_i += 1
            kb.wait("pe", mps_free[ms])
            for c in range(NCH):
                ins = nc.tensor.matmul(mps[ms][0:NH, :], lhsT=wf_b[:, c, :], rhs=hT[:, c, :], start=(c == 0), stop=(c == NCH - 1))
            t_mm = kb.mark(ins, "pe")
            os_ = ost_i % NOST
            ost_i += 1
            kb.wait("act", t_mm, ost_sem[os_].tok())
            t_ev = kb.mark(nc.scalar.copy(out=ost[os_][0:NH, :], in_=mps[ms][0:NH, :]), "act")
            mps_free[ms] = t_ev
            kb.wait("act", t_ev)
            ost_sem[os_].inc(nc.scalar.dma_start(out=fo[:, tt * TT:(tt + 1) * TT], in_=ost[os_][0:NH, :]))
            for blk in range(12):
                kind = blk // 2
                half = blk % 2
                wslot = blk_i % 2
                blk_i += 1
                t_cast = None
                for p in range(8):
                    ss_ = piece_i % NST
                    piece_i += 1
                    kb.wait("sp", stage_free[ss_])
                    ld = stage_sem[ss_].inc(nc.sync.dma_start(out=stage[ss_][:], in_=wa_v[:, 4 * p:4 * p + 4, blk * 512:(blk + 1) * 512]))
                    kb.wait("pool", ld)
                    if p == 0:
                        kb.wait("pool", wblk_free[wslot])
                    t_cast = kb.mark(nc.gpsimd.tensor_copy(out=wblk[wslot][:, 4 * p:4 * p + 4, :], in_=stage[ss_][:]), "pool")
                    stage_free[ss_] = t_cast
                kb.wait("pe", t_cast)
                for j in range(4):
                    ms = mps_i % NPS
                    mps_i += 1
                    kb.wait("pe", mps_free[ms])
                    for c in range(NCH):
                        if kind in (2, 5):
                            ins = nc.tensor.matmul(mps[ms][:], lhsT=hT[:, c, j * 128:(j + 1) * 128], rhs=wblk[wslot][:, c, :],
                                                   start=(c == 0), stop=(c == NCH - 1))
                        else:
                            ins = nc.tensor.matmul(mps[ms][:], lhsT=wblk[wslot][:, c, j * 128:(j + 1) * 128], rhs=hT[:, c, :],
                                                   start=(c == 0), stop=(c == NCH - 1))
                    t_mm = kb.mark(ins, "pe")
                    last_mm = t_mm
                    os_ = ost_i % NOST
                    ost_i += 1
                    kb.wait("act", t_mm, ost_sem[os_].tok())
                    t_ev = kb.mark(nc.scalar.copy(out=ost[os_][:], in_=mps[ms][:]), "act")
                    mps_free[ms] = t_ev
                    if kind in (2, 5):
                        dst = v[kind // 3, tt * TT + j * 128: tt * TT + (j + 1) * 128, half * 512:(half + 1) * 512]
                    else:
                        which = {0: 0, 1: 1, 3: 2, 4: 3}[kind]
                        dst = qk[which, half * 4 + j, :, tt * TT:(tt + 1) * TT]
                    kb.wait("act", t_ev)
                    ost_sem[os_].inc(nc.scalar.dma_start(out=dst, in_=ost[os_][:]))
                wblk_free[wslot] = last_mm
        for s_ in ost_sem:
            kb.wait("act", s_.tok())
    return nc


_CACHE = {}


def get_prog(name):
    if name not in _CACHE:
        _CACHE[name] = {"a": build_phase_a, "b": build_phase_b, "c": build_phase_c, "d": build_phase_d}[name]()
    return _CACHE[name]


def run_phase_a(x_flat, g, w_in):
    nc = get_prog("a")
    cols = np.concatenate([np.arange(0, 3 * W), np.arange(4 * W, 7 * W)])
    wa = np.ascontiguousarray(w_in[:, cols])
    wf = np.ascontiguousarray(w_in[:, 8 * W:8 * W + NH])
    grep = np.ascontiguousarray(np.broadcast_to(g[None, :], (128, D)))
    in_maps = [{"x": np.ascontiguousarray(x_flat[c * TPC:(c + 1) * TPC]), "g": grep, "wa": wa, "wf": wf} for c in range(NCORES)]
    res = run_bass_kernel_spmd(nc, in_maps, core_ids=list(range(NCORES)))
    qk = np.concatenate([r["qk"] for r in res.results], axis=3)
    v = np.concatenate([r["v"] for r in res.results], axis=1)
    fo = np.concatenate([r["fo"] for r in res.results], axis=1)
    return qk, v, fo


def build_phase_b(SEQ=SEQ, NB=BATCH):
    nc = bass.Bass("TRN2", target_bir_lowering=False)
    NKC = SEQ // 128
    NQT = SEQ // 512
    NBLK = SEQ // 256
    NPC = SEQ // 2048
    qk = nc.dram_tensor("qk", [NB, 4, 128, SEQ], F32, kind="ExternalInput").ap()
    vv = nc.dram_tensor("vv", [NB, 2, SEQ, 128], F32, kind="ExternalInput").ap()
    ff = nc.dram_tensor("ff", [NB, 1, SEQ], F32, kind="ExternalInput").ap()
    bf = nc.dram_tensor("bf", [1, 1], F32, kind="ExternalInput").ap()
    oo = nc.dram_tensor("oo", [NB, 2, 128, SEQ], F32, kind="ExternalOutput").ap()
    with ExitStack() as es:
        kb = KB(nc, es)
        qT = kb.sb("qT", [128, SEQ], BF16)
        kT = kb.sb("kT", [128, SEQ], BF16)
        vS = kb.sb("vS", [128, NKC, 128], BF16)
        crep = kb.sb("crep", [128, SEQ], F32)
        negc = kb.sb("negc", [128, NKC], F32)
        NSTG = 2
        stg = [kb.sb(f"stg{i}", [128, 2048], F32) for i in range(NSTG)]
        biasT = kb.sb("biasT", [32, SEQ], BF16)
        gate_all = kb.sb("gate_all", [128, SEQ // 128, 32], F32)
        ksum = kb.sb("ksum", [128, 32], F32)
        gate_m = kb.sb("gate_m", [128, 2, 32], F32)
        top8 = kb.sb("top8", [128, 2, 8], F32)
        selb = kb.sb("selb", [128, 2, 32], F32)
        E_all = kb.sb("E_all", [32, 32, 128], BF16)
        ident_f = kb.sb("ident_f", [128, 128], F32)
        ident = kb.sb("ident", [128, 128], BF16)
        tri_f = kb.sb("tri_f", [128, 128], F32)
        tri = kb.sb("tri", [128, 128], BF16)
        ones_b = kb.sb("ones_b", [128, 128], BF16)
        onesrow = kb.sb("onesrow", [1, 128], F32)
        negone = kb.sb("negone", [1, 2], F32)
        one1 = kb.sb("one1", [1, 1], F32)
        negb = kb.sb("negb", [1, 1], F32)
        lrow = kb.sb("lrow", [1, 2048], F32)
        carry = kb.sb("carry", [1, 1], F32)
        NPT = 3
        PT = [kb.sb(f"PT{i}", [128, 512], BF16) for i in range(NPT)]
        tmpS = [kb.sb(f"tmpS{i}", [128, 512], F32) for i in range(NPT)]
        rden = kb.sb("rden", [128, 512], F32)
        NOS = 2
        oS = [kb.sb(f"oS{i}", [128, 512], F32) for i in range(NOS)]
        Sps = [kb.ps(f"Sps{i}", [128, 512], F32) for i in range(NPT)]
        accps = kb.ps("accps", [128, 512], F32)
        denps = kb.ps("denps", [128, 512], F32)
        mscps = [kb.ps(f"mscps{i}", [128, 512], F32) for i in range(2)]

        ldsem = [DmaSem(kb, f"ldsem{i}") for i in range(NSTG)]
        csem = DmaSem(kb, "csem")
        osem = [DmaSem(kb, f"osem{i}") for i in range(NOS)]
        stg_free = [None] * NSTG
        st = {"stg_i": 0, "pt_i": 0, "os_i": 0, "msc_i": 0}
        S_free = [None] * NPT
        PT_free = [None] * NPT
        tmp_free = [None] * NPT
        msc_free = [None, None]
        last = {"pe": None, "act": None, "dve": None, "pool": None}

        def chain(e, ins, *deps):
            kb.wait(e, last[e], *deps)
            t = kb.mark(ins(), e)
            last[e] = t
            return t

        t = chain("pool", lambda: nc.gpsimd.memset(ident_f[:], 0.0))
        t = chain("pool", lambda: nc.gpsimd.affine_select(out=ident_f[:], in_=ident_f[:], pattern=[[-1, 128]],
                                                          compare_op=ALU.not_equal, fill=1.0, base=0, channel_multiplier=1))
        t = chain("pool", lambda: nc.gpsimd.tensor_copy(out=ident[:], in_=ident_f[:]))
        t = chain("pool", lambda: nc.gpsimd.memset(tri_f[:], 0.0))
        t = chain("pool", lambda: nc.gpsimd.affine_select(out=tri_f[:], in_=tri_f[:], pattern=[[1, 128]],
                                                          compare_op=ALU.is_ge, fill=NEG, base=0, channel_multiplier=-1))
        t = chain("pool", lambda: nc.gpsimd.tensor_copy(out=tri[:], in_=tri_f[:]))
        t = chain("pool", lambda: nc.gpsimd.memset(E_all[:], 0.0))
        t = chain("pool", lambda: nc.gpsimd.affine_select(out=E_all[:], in_=E_all[:], pattern=[[-1, 32], [0, 128]],
                                                          compare_op=ALU.not_equal, fill=1.0, base=0, channel_multiplier=1))
        t = chain("pool", lambda: nc.gpsimd.memset(ones_b[:], 1.0))
        t = chain("pool", lambda: nc.gpsimd.memset(onesrow[:], 1.0))
        t = chain("pool", lambda: nc.gpsimd.memset(negone[:], -1.0))
        t = chain("pool", lambda: nc.gpsimd.memset(one1[:], 1.0))
        t_const = chain("pool", lambda: nc.gpsimd.memset(carry[:], 0.0))
        cb = csem.inc(nc.sync.dma_start(out=negb[:], in_=bf[:, :]))
        t_negb = chain("dve", lambda: nc.vector.tensor_scalar(out=negb[:], in0=negb[:], scalar1=-1.0, scalar2=None, op0=ALU.mult), cb)
        chain("dve", lambda: nc.vector.memset(ksum[:], 0.0))
        chain("dve", lambda: nc.vector.memset(gate_all[:], 0.0))
        kb.wait("pe", t_const)
        kb.wait("act", t_const)
        kb.wait("dve", t_const)

        def load_piece(src_ap, shape_view=None):
            s_ = st["stg_i"] % NSTG
            st["stg_i"] += 1
            kb.wait("sp", stg_free[s_])
            dst = stg[s_][:] if shape_view is None else shape_view(stg[s_])
            tok = ldsem[s_].inc(nc.sync.dma_start(out=dst, in_=src_ap))
            return s_, tok

        def attention(b, which, attn_done):
            moba = (which == 0)
            t_k = None
            for p in range(NPC):
                s_, tok = load_piece(qk[b, 2 * which + 1, :, p * 2048:(p + 1) * 2048])
                kb.wait("pool", tok, attn_done)
                t_k = kb.mark(nc.gpsimd.tensor_copy(out=kT[:, p * 2048:(p + 1) * 2048], in_=stg[s_][:]), "pool")
                fr = [t_k]
                if moba:
                    t_ks = chain("dve", lambda: nc.vector.tensor_reduce(
                        out=ksum[:, p * 8:(p + 1) * 8], in_=stg[s_][:].rearrange("p (n t) -> p n t", t=256), axis=AX.X, op=ALU.add),
                        tok, attn_done)
                    fr.append(t_ks)
                stg_free[s_] = fr
            t_v = None
            for p in range(NPC):
                s_, tok = load_piece(vv[b, which, p * 2048:(p + 1) * 2048, :].rearrange("(c p) d -> p c d", p=128),
                                     lambda tl: tl[:].rearrange("p (c d) -> p c d", d=128))
                kb.wait("pool", tok, attn_done)
                t_v = kb.mark(nc.gpsimd.tensor_copy(out=vS[:, p * 16:(p + 1) * 16, :],
                                                   in_=stg[s_][:].rearrange("p (c d) -> p c d", d=128)), "pool")
                stg_free[s_] = [t_v]
            t_q = None
            t_gate = None
            for p in range(NPC):
                s_, tok = load_piece(qk[b, 2 * which, :, p * 2048:(p + 1) * 2048])
                kb.wait("pool", tok, attn_done)
                t_q = kb.mark(nc.gpsimd.tensor_copy(out=qT[:, p * 2048:(p + 1) * 2048], in_=stg[s_][:]), "pool")
                fr = [t_q]
                if moba:
                    m_ = st["msc_i"] % 2
                    st["msc_i"] += 1
                    kb.wait("pe", tok, last["dve"], msc_free[m_])
                    gv = mscps[m_][:].rearrange("p (t n) -> p t n", n=32)
                    for j in range(16):
                        ins = nc.tensor.matmul(gv[:, j, :], lhsT=stg[s_][:, j * 128:(j + 1) * 128], rhs=ksum[:, :], start=True, stop=True)
                    t_g = kb.mark(ins, "pe")
                    last["pe"] = t_g
                    fr.append(t_g)
                    t_gate = chain("dve", lambda: nc.vector.tensor_copy(out=gate_all[:, p * 16:(p + 1) * 16, :], in_=gv), t_g)
                    msc_free[m_] = t_gate
                stg_free[s_] = fr
            t_bias = None
            if moba:
                for qb in range(NBLK):
                    chain("dve", lambda: nc.vector.memset(gate_m[:], -1e30))
                    if qb > 0:
                        chain("dve", lambda: nc.vector.tensor_copy(out=gate_m[:, :, 0:qb], in_=gate_all[:, 2 * qb:2 * qb + 2, 0:qb]))
                    for j in range(2):
                        chain("dve", lambda: nc.vector.max(out=top8[:, j, :], in_=gate_m[:, j, :]))
                        chain("dve", lambda: nc.vector.tensor_scalar(out=selb[:, j, :], in0=gate_m[:, j, :], scalar1=top8[:, j, 2:3],
                                                                    scalar2=None, op0=ALU.is_ge))
                    chain("dve", lambda: nc.vector.tensor_scalar(out=selb[:], in0=selb[:], scalar1=-1.0, scalar2=-NEG,
                                                                op0=ALU.add, op1=ALU.mult))
                    if qb + 1 < 32:
                        chain("dve", lambda: nc.vector.memset(selb[:, :, qb + 1:32], NEG))
                    t_sel = chain("dve", lambda: nc.vector.memset(selb[:, :, qb:qb + 1], 0.0))
                    m_ = st["msc_i"] % 2
                    st["msc_i"] += 1
                    kb.wait("pe", t_sel, msc_free[m_])
                    for j in range(2):
                        ins = nc.tensor.transpose(out=mscps[m_][0:32, j * 128:(j + 1) * 128], in_=selb[:, j, :], identity=ident_f[:])
                    t_tr = kb.mark(ins, "pe")
                    last["pe"] = t_tr
                    t_bias = chain("dve", lambda: nc.vector.tensor_copy(out=biasT[:, qb * 256:(qb + 1) * 256], in_=mscps[m_][0:32, 0:256]),
                                   t_tr, attn_done)
                    msc_free[m_] = t_bias
            else:
                for p in range(NPC):
                    s_, tok = load_piece(ff[b, :, p * 2048:(p + 1) * 2048], lambda tl: tl[0:1, :])
                    fr_ = stg[s_][0:1, :]
                    chain("act", lambda: nc.scalar.activation(out=fr_, in_=fr_, func=AF.Exp, scale=-1.0, bias=negb[:]), tok, t_negb)
                    t_l = chain("act", lambda: nc.scalar.activation(out=fr_, in_=fr_, func=AF.Ln, scale=1.0, bias=one1[:]))
                    chain("dve", lambda: nc.vector.tensor_scalar(out=fr_, in0=fr_, scalar1=-1.0 / SCALE, scalar2=None, op0=ALU.mult), t_l)
                    init = 0.0 if p == 0 else carry[:, 0:1]
                    kb.wait("dve", last["pe"])
                    t_sc = chain("dve", lambda: nc.vector.tensor_tensor_scan(out=lrow[:], data0=one1[:, 0:1].to_broadcast([1, 2048]), data1=fr_,
                                                                           initial=init, op0=ALU.mult, op1=ALU.add))
                    stg_free[s_] = [t_sc]
                    t_sc = chain("dve", lambda: nc.vector.tensor_copy(out=carry[:], in_=lrow[:, 2047:2048]))
                    for i in range(4):
                        m_ = st["msc_i"] % 2
                        st["msc_i"] += 1
                        kb.wait("pe", t_sc, msc_free[m_])
                        t_mm = kb.mark(nc.tensor.matmul(mscps[m_][:], lhsT=onesrow[:], rhs=lrow[:, i * 512:(i + 1) * 512], start=True, stop=True), "pe")
                        last["pe"] = t_mm
                        t_cr = chain("dve", lambda: nc.vector.tensor_copy(out=crep[:, p * 2048 + i * 512:p * 2048 + (i + 1) * 512], in_=mscps[m_][:]),
                                     t_mm, attn_done)
                        msc_free[m_] = t_cr
                    m_ = st["msc_i"] % 2
                    st["msc_i"] += 1
                    kb.wait("pe", t_sc, msc_free[m_])
                    for kc in range(16):
                        ins = nc.tensor.matmul(mscps[m_][:, 2 * kc:2 * kc + 2], lhsT=lrow[:, kc * 128:(kc + 1) * 128], rhs=negone[:], start=True, stop=True)
                    t_mm = kb.mark(ins, "pe")
                    last["pe"] = t_mm
                    t_bias = chain("dve", lambda: nc.vector.tensor_copy(
                        out=negc[:, p * 16:(p + 1) * 16], in_=mscps[m_][:, 0:32].rearrange("p (c two) -> p c two", two=2)[:, :, 0]), t_mm, attn_done)
                    msc_free[m_] = t_bias

            kb.wait("pe", t_k, t_v, t_q, t_bias)
            t_last_pv = None
            for Q in range(NQT):
                q0 = Q * 512
                nkc = 4 * Q + 4
                acc_first = True
                for kc in range(nkc):
                    c0 = max(0, kc * 128 - q0)
                    diag = kc * 128 >= q0
                    n = kc // 2
                    sl = st["pt_i"] % NPT
                    st["pt_i"] += 1
                    kb.wait("pe", S_free[sl])
                    ins = nc.tensor.matmul(Sps[sl][:, c0:512], lhsT=kT[:, kc * 128:(kc + 1) * 128], rhs=qT[:, q0 + c0:q0 + 512],
                                           start=True, stop=not (moba or diag))
                    if moba:
                        ins = nc.tensor.matmul(Sps[sl][:, c0:512], lhsT=E_all[:, n, :], rhs=biasT[:, q0 + c0:q0 + 512],
                                               start=False, stop=not diag)
                    if diag:
                        ins = nc.tensor.matmul(Sps[sl][:, c0:c0 + 128], lhsT=ident[:], rhs=tri[:], start=False, stop=True)
                    t_s = kb.mark(ins, "pe")
                    if moba:
                        kb.wait("act", t_s, PT_free[sl])
                        t_p = kb.mark(nc.scalar.activation(out=PT[sl][:, c0:512], in_=Sps[sl][:, c0:512], func=AF.Exp, scale=SCALE), "act")
                        S_free[sl] = t_p
                    else:
                        kb.wait("dve", t_s, tmp_free[sl])
                        t_t = kb.mark(nc.vector.scalar_tensor_tensor(out=tmpS[sl][:, c0:512], in0=Sps[sl][:, c0:512], scalar=negc[:, kc:kc + 1],
                                                                     in1=crep[:, q0 + c0:q0 + 512], op0=ALU.add, op1=ALU.add), "dve")
                        S_free[sl] = t_t
                        kb.wait("act", t_t, PT_free[sl])
                        t_p = kb.mark(nc.scalar.activation(out=PT[sl][:, c0:512], in_=tmpS[sl][:, c0:512], func=AF.Exp, scale=SCALE), "act")
                        tmp_free[sl] = t_p
                    kb.wait("pe", t_p)
                    if acc_first:
                        kb.wait("pe", st.get("acc_free"))
                    nc.tensor.matmul(accps[:, c0:512], lhsT=vS[:, kc, :], rhs=PT[sl][:, c0:512], start=acc_first, stop=(kc == nkc - 1))
                    ins = nc.tensor.matmul(denps[:, c0:512], lhsT=ones_b[:], rhs=PT[sl][:, c0:512], start=acc_first, stop=(kc == nkc - 1))
                    t_pv = kb.mark(ins, "pe")
                    PT_free[sl] = t_pv
                    acc_first = False
                t_last_pv = t_pv
                t_r = chain("dve", lambda: nc.vector.reciprocal(out=rden[:], in_=denps[:]), t_pv)
                o_ = st["os_i"] % NOS
                st["os_i"] += 1
                t_o = chain("dve", lambda: nc.vector.tensor_tensor(out=oS[o_][:], in0=accps[:], in1=rden[:], op=ALU.mult), osem[o_].tok())
                st["acc_free"] = t_o
                kb.wait("sp", t_o)
                osem[o_].inc(nc.sync.dma_start(out=oo[b, which, :, q0:q0 + 512], in_=oS[o_][:]))
            return t_last_pv

        done = None
        for b in range(NB):
            for which in range(2):
                done = attention(b, which, done)
        for s_ in osem:
            kb.wait("sp", s_.tok())
    return nc


def run_phase_b(qk, v, fo, b_forget):
    nc = get_prog("b")
    in_maps = []
    for h in range(NCORES):
        qk_h = np.ascontiguousarray(qk[:, h].reshape(4, 128, BATCH, SEQ).transpose(2, 0, 1, 3))
        v_h = np.ascontiguousarray(v[:, :, h * 128:(h + 1) * 128].reshape(2, BATCH, SEQ, 128).transpose(1, 0, 2, 3))
        f_h = np.ascontiguousarray(fo[h].reshape(BATCH, 1, SEQ))
        in_maps.append({"qk": qk_h, "vv": v_h, "ff": f_h, "bf": np.ascontiguousarray(b_forget[h].reshape(1, 1))})
    res = run_bass_kernel_spmd(nc, in_maps, core_ids=list(range(NCORES)))
    return np.stack([r["oo"] for r in res.results], axis=0)


HAL = 16


def build_phase_c(TPC=TPC, T=256):
    nc = bass.Bass("TRN2", target_bir_lowering=False)
    NT = TPC // T
    x = nc.dram_tensor("x", [TPC, D], F32, kind="ExternalInput").ap()
    xh = nc.dram_tensor("xh", [128, D], F32, kind="ExternalInput").ap()
    gcol = nc.dram_tensor("gcol", [128, NCH], F32, kind="ExternalInput").ap()
    wc = nc.dram_tensor("wc", [D, 24576], F32, kind="ExternalInput").ap()
    wbr = nc.dram_tensor("wbr", [D, D], F32, kind="ExternalInput").ap()
    wo = nc.dram_tensor("wo", [D, D], F32, kind="ExternalInput").ap()
    oa = nc.dram_tensor("oa", [2, W, TPC], F32, kind="ExternalInput").ap()
    icnt = nc.dram_tensor("icnt", [128, 4, TPC], F32, kind="ExternalInput").ap()
    wpool = nc.dram_tensor("wpool", [128, 8, 256], F32, kind="ExternalInput").ap()
    pscale = nc.dram_tensor("pscale", [128, 8], F32, kind="ExternalInput").ap()
    wconv = nc.dram_tensor("wconv", [128, 3, 8], F32, kind="ExternalInput").ap()
    bmerge = nc.dram_tensor("bmerge", [128, 4, NCH], F32, kind="ExternalInput").ap()
    y = nc.dram_tensor("y", [TPC, D], F32, kind="ExternalOutput").ap()
    wc_v = wc.rearrange("(c p) n -> p c n", p=128)
    wbr_v = wbr.rearrange("(c p) n -> p c n", p=128)
    wo_v = wo.rearrange("(c p) n -> p c n", p=128)
    TE = T + HAL
    with ExitStack() as es:
        kb = KB(nc, es)
        hT = kb.sb("hT", [128, NCH, TE], BF16)
        brT = kb.sb("brT", [128, NCH, T], BF16)
        mgT = kb.sb("mgT", [128, NCH, T], BF16)
        NW = 3
        wblk = [kb.sb(f"wblk{i}", [128, NCH, 256], BF16) for i in range(NW)]
        NST = 2
        stage = [kb.sb(f"stage{i}", [128, 8, 256], F32) for i in range(NST)]
        xbuf = kb.sb("xbuf", [128, D], F32)
        xn = kb.sb("xn", [128, D], BF16)
        ss = kb.sb("ss", [128, 1], F32)
        rs = kb.sb("rs", [128, 1], F32)
        epsT = kb.sb("epsT", [128, 1], F32)
        gcolS = kb.sb("gcolS", [128, NCH], F32)
        ident_f = kb.sb("ident_f", [128, 128], F32)
        ident = kb.sb("ident", [128, 128], BF16)
        icS = kb.sb("icS", [128, 4, T], F32)
        wpool_f = kb.sb("wpool_f", [128, 8, 256], F32)
        wpool_b = kb.sb("wpool_b", [128, 8, 256], BF16)
        pscS = kb.sb("pscS", [128, 8], F32)
        wcvS = kb.sb("wcvS", [128, 3, 8], F32)
        bmS = kb.sb("bmS", [128, 4, NCH], F32)
        NTF = 2
        tmpf = [kb.sb(f"tmpf{i}", [128, T], F32) for i in range(NTF)]
        otile = [kb.sb(f"otile{i}", [128, T], F32) for i in range(2)]
        uext = kb.sb("uext", [128, 2, TE], F32)
        pa = kb.sb("pa", [128, 2, TE], F32)
        pb = kb.sb("pb", [128, 2, TE], F32)
        pooledT = kb.sb("pooledT", [128, 2, T], BF16)
        yS = kb.sb("yS", [128, 2, T], F32)
        bS = kb.sb("bS", [128, T], F32)
        cext = kb.sb("cext", [128, TE], F32)
        zext = kb.sb("zext", [128, TE], F32)
        ycv = kb.sb("ycv", [128, T], F32)
        sgT = kb.sb("sgT", [128, 4, 2, T], BF16)
        accm = kb.sb("accm", [128, T], F32)
        xres = [kb.sb(f"xres{i}", [128, 256], F32) for i in range(2)]
        orow = [kb.sb(f"orow{i}", [128, 256], F32) for i in range(2)]
        tpps = [kb.ps(f"tpps{i}", [128, 8, 128], BF16) for i in range(2)]
        NPS = 4
        mps = [kb.ps(f"mps{i}", [128, 512], F32) for i in range(NPS)]
        hps = kb.ps("hps", [128, 512], F32)

        last = {"pe": None, "act": None, "dve": None, "pool": None}

        def chain(e, ins, *deps):
            kb.wait(e, last[e], *deps)
            t = kb.mark(ins(), e)
            last[e] = t
            return t

        def pe_mark(ins):
            t = kb.mark(ins, "pe")
            last["pe"] = t
            return t

        csem = DmaSem(kb, "csem")
        xsem = DmaSem(kb, "xsem")
        stage_sem = [DmaSem(kb, f"stsem{i}") for i in range(NST)]
        stage_free = [None] * NST
        wblk_free = [None] * NW
        osem = [DmaSem(kb, f"osem{i}") for i in range(2)]
        otile_free = [None, None]
        xrsem = [DmaSem(kb, f"xrsem{i}") for i in range(2)]
        xres_free = [None, None]
        orsem = [DmaSem(kb, f"orsem{i}") for i in range(2)]
        icsem = DmaSem(kb, "icsem")
        mps_free = [None] * NPS
        hps_free = [None]
        tp_free = [None, None]
        cnt = {"piece": 0, "blk": 0, "mps": 0, "tp": 0, "tf": 0, "ot": 0, "xr": 0}

        t = chain("pool", lambda: nc.gpsimd.memset(ident_f[:], 0.0))
        t = chain("pool", lambda: nc.gpsimd.affine_select(out=ident_f[:], in_=ident_f[:], pattern=[[-1, 128]],
                                                          compare_op=ALU.not_equal, fill=1.0, base=0, channel_multiplier=1))
        t_id = chain("pool", lambda: nc.gpsimd.tensor_copy(out=ident[:], in_=ident_f[:]))
        c_all = None
        for dst, src in ((gcolS, gcol), (wpool_f, wpool), (pscS, pscale), (wcvS, wconv), (bmS, bmerge)):
            c_all = csem.inc(nc.sync.dma_start(out=dst[:], in_=src))
        chain("dve", lambda: nc.vector.memset(epsT[:], EPS))
        t_c = chain("dve", lambda: nc.vector.tensor_copy(out=wpool_b[:], in_=wpool_f[:]), c_all)
        kb.wait("pe", t_id, t_c)
        kb.wait("act", t_c)

        def norm_tile(x_rows, dst_col0, src_c0, ncols):
            kb.wait("sp", last["dve"])
            ld = xsem.inc(nc.sync.dma_start(out=xbuf[:], in_=x_rows))
            chain("act", lambda: nc.scalar.activation(out=xn[:], in_=xbuf[:], func=AF.Square, accum_out=ss[:]), ld, last["pe"], last["dve"])
            t_a = chain("act", lambda: nc.scalar.activation(out=rs[:], in_=ss[:], func=AF.Sqrt, scale=1.0 / D, bias=epsT[:]))
            chain("dve", lambda: nc.vector.reciprocal(out=rs[:], in_=rs[:]), t_a)
            t_xn = chain("dve", lambda: nc.vector.tensor_scalar(out=xn[:], in0=xbuf[:], scalar1=rs[:, 0:1], scalar2=None, op0=ALU.mult))
            for c8 in range(NCH // 8):
                sl = cnt["tp"] % 2
                cnt["tp"] += 1
                kb.wait("pe", t_xn, tp_free[sl])
                for j in range(8):
                    c = c8 * 8 + j
                    ins = nc.tensor.transpose(out=tpps[sl][:, j, :], in_=xn[:, c * 128:(c + 1) * 128], identity=ident[:])
                t_pe = pe_mark(ins)
                for j in range(8):
                    c = c8 * 8 + j
                    t_ev = chain("dve", lambda: nc.vector.tensor_scalar(out=hT[:, c, dst_col0:dst_col0 + ncols],
                                                                       in0=tpps[sl][:, j, src_c0:src_c0 + ncols],
                                                                       scalar1=gcolS[:, c:c + 1], scalar2=None, op0=ALU.mult), t_pe)
                tp_free[sl] = t_ev

        def load_block(view, col0):
            slot = cnt["blk"] % NW
            cnt["blk"] += 1
            t_cast = None
            for p in range(4):
                s_ = cnt["piece"] % NST
                cnt["piece"] += 1
                kb.wait("sp", stage_free[s_])
                ld = stage_sem[s_].inc(nc.sync.dma_start(out=stage[s_][:], in_=view[:, 8 * p:8 * p + 8, col0:col0 + 256]))
                deps = [ld]
                if p == 0:
                    deps.append(wblk_free[slot])
                t_cast = chain("pool", lambda: nc.gpsimd.tensor_copy(out=wblk[slot][:, 8 * p:8 * p + 8, :], in_=stage[s_][:]), *deps)
                stage_free[s_] = t_cast
            kb.wait("pe", t_cast)
            return slot

        def mm_feat(slot, j, halo=False):
            ms = cnt["mps"] % NPS
            cnt["mps"] += 1
            kb.wait("pe", mps_free[ms])
            for c in range(NCH):
                ins = nc.tensor.matmul(mps[ms][:, 0:T], lhsT=wblk[slot][:, c, j * 128:(j + 1) * 128], rhs=hT[:, c, HAL:TE],
                                       start=(c == 0), stop=(c == NCH - 1))
            if halo:
                kb.wait("pe", hps_free[0])
                for c in range(NCH):
                    ins = nc.tensor.matmul(hps[:, 0:HAL], lhsT=wblk[slot][:, c, j * 128:(j + 1) * 128], rhs=hT[:, c, 0:HAL],
                                           start=(c == 0), stop=(c == NCH - 1))
            return ms, pe_mark(ins)

        def get_tmpf():
            i = cnt["tf"] % NTF
            cnt["tf"] += 1
            return tmpf[i]

        pending = []

        for tt in range(NT):
            t0 = tt * T
            if tt == 0:
                norm_tile(xh[:, :], 0, 128 - HAL, HAL)
            else:
                chain("dve", lambda: nc.vector.tensor_copy(out=hT[:, :, 0:HAL], in_=hT[:, :, T:TE]), last["pe"])
            for s in range(T // 128):
                norm_tile(x[t0 + s * 128:t0 + (s + 1) * 128, :], HAL + s * 128, 0, 128)
            kb.wait("pe", last["dve"])
            kb.wait("sp", last["dve"])
            t_ic = icsem.inc(nc.sync.dma_start(out=icS[:], in_=icnt[:, :, t0:t0 + T]))
            kb.wait("dve", t_ic)

            for blk in range(32):
                slot = load_block(wc_v, blk * 256)
                if blk < 8:
                    br = blk // 4
                    for j in range(2):
                        ch = 2 * (blk % 4) + j
                        ms, t_mm = mm_feat(slot, j)
                        tf = get_tmpf()
                        t_a = chain("act", lambda: nc.scalar.activation(out=tf[:], in_=mps[ms][:, 0:T], func=AF.Silu), t_mm, last["dve"])
                        mps_free[ms] = t_a
                        oi = cnt["ot"] % 2
                        cnt["ot"] += 1
                        kb.wait("sp", otile_free[oi])
                        t_o = osem[oi].inc(nc.sync.dma_start(out=otile[oi][:], in_=oa[br, ch * 128:(ch + 1) * 128, t0:t0 + T]))
                        t_d = chain("dve", lambda: nc.vector.tensor_tensor(out=brT[:, br * 8 + ch, :], in0=tf[:], in1=otile[oi][:], op=ALU.mult), t_a, t_o)
                        otile_free[oi] = t_d
                elif blk < 16:
                    g = (blk - 8) // 2
                    if (blk - 8) % 2 == 0:
                        for j in range(2):
                            ms, t_mm = mm_feat(slot, j, halo=True)
                            chain("act", lambda: nc.scalar.copy(out=uext[:, j, HAL:TE], in_=mps[ms][:, 0:T]), t_mm, last["dve"])
                            t_a = chain("act", lambda: nc.scalar.copy(out=uext[:, j, 0:HAL], in_=hps[:, 0:HAL]))
                            mps_free[ms] = t_a
                            hps_free[0] = t_a
                        src = uext
                        bufs = [pa, pb]
                        for k in range(g + 1):
                            sh = 2 ** k
                            dst = bufs[k % 2]
                            lo = 2 * sh - 1
                            chain("dve", lambda: nc.vector.tensor_tensor(out=dst[:, :, lo:TE], in0=src[:, :, lo:TE], in1=src[:, :, lo - sh:TE - sh], op=ALU.add), last["act"])
                            src = dst
                        for j in range(2):
                            tf = get_tmpf()
                            chain("dve", lambda: nc.vector.tensor_tensor(out=tf[:], in0=src[:, j, HAL:TE], in1=icS[:, g, :], op=ALU.mult), last["act"])
                            t_p = chain("dve", lambda: nc.vector.tensor_tensor(out=pooledT[:, j, :], in0=tf[:], in1=uext[:, j, HAL:TE], op=ALU.subtract), last["pe"])
                        for oc in range(2):
                            ms = cnt["mps"] % NPS
                            cnt["mps"] += 1
                            kb.wait("pe", mps_free[ms], t_p)
                            for j in range(2):
                                ins = nc.tensor.matmul(mps[ms][:, 0:T], lhsT=wpool_b[:, 2 * g + j, oc * 128:(oc + 1) * 128], rhs=pooledT[:, j, :],
                                                       start=(j == 0), stop=(j == 1))
                            t_mm = pe_mark(ins)
                            t_y = chain("dve", lambda: nc.vector.tensor_scalar(out=yS[:, oc, :], in0=mps[ms][:, 0:T], scalar1=pscS[:, 2 * g + oc:2 * g + oc + 1],
                                                                              scalar2=None, op0=ALU.mult), t_mm)
                            mps_free[ms] = t_y
                    else:
                        for j in range(2):
                            ms, t_mm = mm_feat(slot, j)
                            tf = get_tmpf()
                            t_a = chain("act", lambda: nc.scalar.activation(out=tf[:], in_=mps[ms][:, 0:T], func=AF.Silu), t_mm, last["dve"])
                            mps_free[ms] = t_a
                            chain("dve", lambda: nc.vector.tensor_tensor(out=brT[:, 16 + 2 * g + j, :], in0=tf[:], in1=yS[:, j, :], op=ALU.mult), t_a)
                else:
                    i = (blk - 16) // 2
                    if (blk - 16) % 2 == 0:
                        ms, t_mm = mm_feat(slot, 0)
                        t_a = chain("act", lambda: nc.scalar.copy(out=bS[:], in_=mps[ms][:, 0:T]), t_mm, last["dve"])
                        mps_free[ms] = t_a
                        ms, t_mm = mm_feat(slot, 1, halo=True)
                        chain("act", lambda: nc.scalar.copy(out=cext[:, HAL:TE], in_=mps[ms][:, 0:T]), t_mm, last["dve"])
                        t_a = chain("act", lambda: nc.scalar.copy(out=cext[:, 0:HAL], in_=hps[:, 0:HAL]))
                        mps_free[ms] = t_a
                        hps_free[0] = t_a
                    else:
                        ms, t_mm = mm_feat(slot, 0, halo=True)
                        chain("dve", lambda: nc.vector.tensor_tensor(out=zext[:, HAL:TE], in0=mps[ms][:, 0:T], in1=cext[:, HAL:TE], op=ALU.mult), t_mm, last["act"])
                        t_d = chain("dve", lambda: nc.vector.tensor_tensor(out=zext[:, 0:HAL], in0=hps[:, 0:HAL], in1=cext[:, 0:HAL], op=ALU.mult))
                        mps_free[ms] = t_d
                        hps_free[0] = t_d
                        chain("dve", lambda: nc.vector.tensor_scalar(out=ycv[:], in0=zext[:, HAL - 2:TE - 2], scalar1=wcvS[:, 0, i:i + 1], scalar2=None, op0=ALU.mult))
                        chain("dve", lambda: nc.vector.scalar_tensor_tensor(out=ycv[:], in0=zext[:, HAL - 1:TE - 1], scalar=wcvS[:, 1, i:i + 1], in1=ycv[:],
                                                                           op0=ALU.mult, op1=ALU.add))
                        chain("dve", lambda: nc.vector.scalar_tensor_tensor(out=ycv[:], in0=zext[:, HAL:TE], scalar=wcvS[:, 2, i:i + 1], in1=ycv[:],
                                                                           op0=ALU.mult, op1=ALU.add))
                        chain("dve", lambda: nc.vector.tensor_tensor(out=ycv[:], in0=ycv[:], in1=bS[:], op=ALU.mult))
                        ms, t_mm = mm_feat(slot, 1)
                        tf = get_tmpf()
                        t_a = chain("act", lambda: nc.scalar.activation(out=tf[:], in_=mps[ms][:, 0:T], func=AF.Silu), t_mm, last["dve"])
                        mps_free[ms] = t_a
                        chain("dve", lambda: nc.vector.tensor_tensor(out=brT[:, 24 + i, :], in0=tf[:], in1=ycv[:], op=ALU.mult), t_a)
                wblk_free[slot] = last["pe"]

            for dp in range(NCH // 2):
                for mb in range(4):
                    slot = load_block(wc_v, 8192 + (dp * 4 + mb) * 256)
                    dcl = mb // 2
                    dc = 2 * dp + dcl
                    for j in range(2):
                        i = 2 * (mb % 2) + j
                        ms, t_mm = mm_feat(slot, j)
                        t_a = chain("act", lambda: nc.scalar.activation(out=sgT[:, i, dcl, :], in_=mps[ms][:, 0:T], func=AF.Sigmoid,
                                                                       bias=bmS[:, i, dc:dc + 1]), t_mm, last["dve"])
                        mps_free[ms] = t_a
                    wblk_free[slot] = last["pe"]
                slot = load_block(wbr_v, dp * 256)
                kb.wait("pe", last["dve"])
                for dcl in range(2):
                    dc = 2 * dp + dcl
                    for i in range(4):
                        ms = cnt["mps"] % NPS
                        cnt["mps"] += 1
                        kb.wait("pe", mps_free[ms])
                        for wcn in range(8):
                            ins = nc.tensor.matmul(mps[ms][:, 0:T], lhsT=wblk[slot][:, 8 * i + wcn, dcl * 128:(dcl + 1) * 128], rhs=brT[:, 8 * i + wcn, :],
                                                   start=(wcn == 0), stop=(wcn == 7))
                        t_mm = pe_mark(ins)
                        if i == 0:
                            t_d = chain("dve", lambda: nc.vector.tensor_tensor(out=accm[:], in0=mps[ms][:, 0:T], in1=sgT[:, i, dcl, :], op=ALU.mult), t_mm, last["act"])
                        else:
                            tf = get_tmpf()
                            t_d = chain("dve", lambda: nc.vector.tensor_tensor(out=tf[:], in0=mps[ms][:, 0:T], in1=sgT[:, i, dcl, :], op=ALU.mult), t_mm, last["act"])
                            if i < 3:
                                chain("dve", lambda: nc.vector.tensor_tensor(out=accm[:], in0=accm[:], in1=tf[:], op=ALU.add))
                            else:
                                chain("dve", lambda: nc.vector.tensor_tensor(out=mgT[:, dc, :], in0=accm[:], in1=tf[:], op=ALU.add), last["pe"])
                        mps_free[ms] = t_d
                wblk_free[slot] = last["pe"]

            kb.wait("pe", last["dve"])
            for ob in range(16):
                slot = load_block(wo_v, ob * 256)
                for st_ in pending:
                    st_()
                pending = []
                for s in range(T // 128):
                    ms = cnt["mps"] % NPS
                    cnt["mps"] += 1
                    kb.wait("pe", mps_free[ms])
                    for c in range(NCH):
                        ins = nc.tensor.matmul(mps[ms][:, 0:256], lhsT=mgT[:, c, s * 128:(s + 1) * 128], rhs=wblk[slot][:, c, :],
                                               start=(c == 0), stop=(c == NCH - 1))
                    t_mm = pe_mark(ins)
                    xi = cnt["xr"] % 2
                    cnt["xr"] += 1
                    kb.wait("sp", xres_free[xi])
                    r0 = t0 + s * 128
                    t_x = xrsem[xi].inc(nc.sync.dma_start(out=xres[xi][:], in_=x[r0:r0 + 128, ob * 256:(ob + 1) * 256]))
                    t_d = chain("dve", lambda: nc.vector.tensor_tensor(out=orow[xi][:], in0=mps[ms][:, 0:256], in1=xres[xi][:], op=ALU.add),
                                t_mm, t_x, orsem[xi].tok())
                    mps_free[ms] = t_d
                    xres_free[xi] = t_d

                    def mk_store(xi=xi, r0=r0, ob=ob, t_d=t_d):
                        kb.wait("sp", t_d)
                        orsem[xi].inc(nc.sync.dma_start(out=y[r0:r0 + 128, ob * 256:(ob + 1) * 256], in_=orow[xi][:]))
                    mk_store()
                wblk_free[slot] = last["pe"]
        for s_ in orsem:
            kb.wait("sp", s_.tok())
    return nc


def prep_c_consts(w_in, w_pool, pool_scale, w_conv, w_branch, b_merge, w_out, g):
    cols = []
    for blk in range(4):
        cols.append(np.arange(3072 + blk * 256, 3072 + (blk + 1) * 256))
    for blk in range(4):
        cols.append(np.arange(7168 + blk * 256, 7168 + (blk + 1) * 256))
    for gg in range(4):
        cols.append(np.arange(8200 + gg * 256, 8200 + (gg + 1) * 256))
        cols.append(np.arange(9224 + gg * 256, 9224 + (gg + 1) * 256))
    for i in range(8):
        cols.append(np.arange(10248 + i * 128, 10248 + (i + 1) * 128))
        cols.append(np.arange(11272 + i * 128, 11272 + (i + 1) * 128))
        cols.append(np.arange(12296 + i * 128, 12296 + (i + 1) * 128))
        cols.append(np.arange(13320 + i * 128, 13320 + (i + 1) * 128))
    for dp in range(16):
        for mb in range(4):
            dc = 2 * dp + mb // 2
            for j in range(2):
                i = 2 * (mb % 2) + j
                cols.append(np.arange(14344 + i * 4096 + dc * 128, 14344 + i * 4096 + (dc + 1) * 128))
    cols = np.concatenate(cols)
    assert cols.shape[0] == 24576
    return {
        "wc": np.ascontiguousarray(w_in[:, cols]),
        "wbr": np.ascontiguousarray(w_branch.reshape(D, D)),
        "wo": np.ascontiguousarray(w_out),
        "gcol": np.ascontiguousarray(g.reshape(NCH, 128).T),
        "wpool": np.ascontiguousarray(w_pool.reshape(4, 2, 128, 256).transpose(2, 0, 1, 3).reshape(128, 8, 256)),
        "pscale": np.ascontiguousarray(pool_scale.reshape(8, 128).T),
        "wconv": np.ascontiguousarray(w_conv.reshape(3, 8, 128).transpose(2, 0, 1)),
        "bmerge": np.ascontiguousarray(b_merge.reshape(4, NCH, 128).transpose(2, 0, 1)),
    }


def icnt_table(pos0, n):
    pos = np.arange(pos0, pos0 + n)
    tab = np.stack([1.0 / np.minimum(pos + 1, w) for w in (2, 4, 8, 16)], axis=0).astype(np.float32)
    return np.ascontiguousarray(np.broadcast_to(tab[None], (128, 4, n)))


def run_phase_c(h, oo, consts):
    nc = get_prog("c")
    in_maps = []
    for c in range(NCORES):
        b = c // (NCORES // BATCH)
        off = (c % (NCORES // BATCH)) * TPC
        r0 = c * TPC
        xh = np.zeros((128, D), np.float32) if off == 0 else np.ascontiguousarray(h[r0 - 128:r0])
        oa = np.ascontiguousarray(oo[:, b, :, :, off:off + TPC].transpose(1, 0, 2, 3).reshape(2, W, TPC))
        m = dict(consts)
        m.update({"x": np.ascontiguousarray(h[r0:r0 + TPC]), "xh": xh, "oa": oa, "icnt": icnt_table(off, TPC)})
        in_maps.append(m)
    res = run_bass_kernel_spmd(nc, in_maps, core_ids=list(range(NCORES)))
    return np.concatenate([r["y"] for r in res.results], axis=0)


def build_phase_d(TPC=TPC):
    nc = bass.Bass("TRN2", target_bir_lowering=False)
    x = nc.dram_tensor("x", [TPC, D], F32, kind="ExternalInput").ap()
    g = nc.dram_tensor("g", [128, D], F32, kind="ExternalInput").ap()
    y = nc.dram_tensor("y", [TPC, D], F32, kind="ExternalOutput").ap()
    with ExitStack() as es:
        kb = KB(nc, es)
        grep = kb.sb("grep", [128, D], F32)
        NB_ = 2
        xb = [kb.sb(f"xb{i}", [128, D], F32) for i in range(NB_)]
        yb = [kb.sb(f"yb{i}", [128, D], F32) for i in range(NB_)]
        junk = kb.sb("junk", [128, D], BF16)
        ss = kb.sb("ss", [128, 1], F32)
        rs = kb.sb("rs", [128, 1], F32)
        epsT = kb.sb("epsT", [128, 1], F32)
        last = {"act": None, "dve": None}

        def chain(e, ins, *deps):
            kb.wait(e, last[e], *deps)
            t = kb.mark(ins(), e)
            last[e] = t
            return t

        csem = DmaSem(kb, "csem")
        xs = [DmaSem(kb, f"xs{i}") for i in range(NB_)]
        ys = [DmaSem(kb, f"ys{i}") for i in range(NB_)]
        xfree = [None] * NB_
        c1 = csem.inc(nc.sync.dma_start(out=grep[:], in_=g[:, :]))
        t_e = chain("dve", lambda: nc.vector.memset(epsT[:], EPS), c1)
        kb.wait("act", t_e)
        for i in range(TPC // 128):
            s_ = i % NB_
            kb.wait("sp", xfree[s_])
            ld = xs[s_].inc(nc.sync.dma_start(out=xb[s_][:], in_=x[i * 128:(i + 1) * 128, :]))
            chain("act", lambda: nc.scalar.activation(out=junk[:], in_=xb[s_][:], func=AF.Square, accum_out=ss[:]), ld, last["dve"])
            t_a = chain("act", lambda: nc.scalar.activation(out=rs[:], in_=ss[:], func=AF.Sqrt, scale=1.0 / D, bias=epsT[:]))
            chain("dve", lambda: nc.vector.reciprocal(out=rs[:], in_=rs[:]), t_a)
            t_y = chain("dve", lambda: nc.vector.scalar_tensor_tensor(out=yb[s_][:], in0=xb[s_][:], scalar=rs[:, 0:1], in1=grep[:],
                                                                     op0=ALU.mult, op1=ALU.mult), ys[s_].tok())
            xfree[s_] = t_y
            kb.wait("sp", t_y)
            ys[s_].inc(nc.sync.dma_start(out=y[i * 128:(i + 1) * 128, :], in_=yb[s_][:]))
        for s_ in ys:
            kb.wait("sp", s_.tok())
    return nc


def run_phase_d(h, final_g):
    nc = get_prog("d")
    grep = np.ascontiguousarray(np.broadcast_to(final_g[None, :], (128, D)))
    in_maps = [{"x": np.ascontiguousarray(h[c * TPC:(c + 1) * TPC]), "g": grep} for c in range(NCORES)]
    res = run_bass_kernel_spmd(nc, in_maps, core_ids=list(range(NCORES)))
    return np.concatenate([r["y"] for r in res.results], axis=0)


def kernel(x, norm_g, w_in, b_forget, w_pool, pool_scale, w_conv, w_branch, b_merge, w_out, final_g):
    f32 = lambda a: np.asarray(a, dtype=np.float32)
    x, norm_g, w_in, b_forget, w_pool, pool_scale, w_conv, w_branch, b_merge, w_out, final_g = map(
        f32, (x, norm_g, w_in, b_forget, w_pool, pool_scale, w_conv, w_branch, b_merge, w_out, final_g))
    h = np.ascontiguousarray(x.reshape(TOK, D))
    for l in range(2):
        qk, v, fo = run_phase_a(h, norm_g[l], w_in[l])
        oo = run_phase_b(qk, v, fo, b_forget[l])
        del qk, v, fo
        consts = prep_c_consts(w_in[l], w_pool[l], pool_scale[l], w_conv[l], w_branch[l], b_merge[l], w_out[l], norm_g[l])
        h = run_phase_c(h, oo, consts)
        del consts, oo
    out = run_phase_d(h, final_g)
    return out.reshape(BATCH, SEQ, D).astype(np.float32)
```

```python
import numpy as np
from contextlib import ExitStack
import concourse.bass as bass
import concourse.mybir as mybir
from concourse.bass_utils import run_bass_kernel_spmd

F32 = mybir.dt.float32
BF16 = mybir.dt.bfloat16
AF = mybir.ActivationFunctionType
ALU = mybir.AluOpType
AX = mybir.AxisListType

NCORES = 8
D = 4096
NCH = D // 128
SEQ = 8192
BATCH = 2
TOK = BATCH * SEQ
TPC = TOK // NCORES
TT = 512
W = 1024
NH = 8
EPS = 1e-6
SCALE = 128 ** -0.5
NEG = -60000.0


class KB:
    def __init__(self, nc, es):
        self.nc = nc
        self.es = es
        self.eng = {"pe": nc.tensor, "act": nc.scalar, "dve": nc.vector, "pool": nc.gpsimd, "sp": nc.sync}
        self.psem = {}
        self.pcnt = {}
        for e in ("pe", "act", "dve", "pool"):
            self.psem[e] = es.enter_context(nc.semaphore("prog_" + e))
            self.pcnt[e] = 0
        self.waited = {}
        self.nsem = 0

    def sb(self, name, shape, dt):
        return self.es.enter_context(self.nc.sbuf_tensor(name, shape, dt))

    def ps(self, name, shape, dt):
        return self.es.enter_context(self.nc.psum_tensor(name, shape, dt))

    def sem(self, name):
        self.nsem += 1
        return self.es.enter_context(self.nc.semaphore(name))

    def mark(self, instr, e):
        self.pcnt[e] += 1
        instr.then_inc(self.psem[e], 1)
        return (self.psem[e], self.pcnt[e], e)

    def wait(self, e, *toks):
        flat = []
        for tok in toks:
            if isinstance(tok, list):
                flat.extend(tok)
            else:
                flat.append(tok)
        for tok in flat:
            if tok is None:
                continue
            sem, val, src = tok
            key = (e, id(sem))
            if self.waited.get(key, 0) >= val:
                continue
            self.waited[key] = val
            self.eng[e].wait_ge(sem, val)


class DmaSem:
    def __init__(self, kb, name):
        self.sem = kb.sem(name)
        self.val = 0

    def inc(self, instr, n=1):
        self.val += 16
        instr.then_inc(self.sem, 16)
        return (self.sem, self.val, "dma")

    def tok(self):
        return (self.sem, self.val, "dma")


def emit_norm_tile(kb, x_rows, grep, xbuf, xn, ss, rs, hT, tpps, tok0, st):
    nc = kb.nc
    kb.wait("sp", st.get("xbuf_free"))
    ld = st["xsem"].inc(nc.sync.dma_start(out=xbuf[:], in_=x_rows))
    kb.wait("act", ld, st.get("xn_free"))
    t_sq = kb.mark(nc.scalar.activation(out=xn[:], in_=xbuf[:], func=AF.Square, accum_out=ss[:]), "act")
    kb.wait("act", t_sq)
    t_act = kb.mark(nc.scalar.activation(out=rs[:], in_=ss[:], func=AF.Sqrt, scale=1.0 / D, bias=st["eps"][:]), "act")
    kb.wait("dve", t_act)
    t_rc = kb.mark(nc.vector.reciprocal(out=rs[:], in_=rs[:]), "dve")
    kb.wait("dve", t_rc)
    t_xn = kb.mark(nc.vector.scalar_tensor_tensor(out=xn[:], in0=xbuf[:], scalar=rs[:, 0:1], in1=grep[:],
                                                  op0=ALU.mult, op1=ALU.mult), "dve")
    st["xbuf_free"] = t_xn
    kb.wait("pe", t_xn)
    last_pe = None
    for c8 in range(NCH // 8):
        slot = st["tp_i"] % len(tpps)
        st["tp_i"] += 1
        kb.wait("pe", st["tp_free"][slot])
        for j in range(8):
            c = c8 * 8 + j
            ins = nc.tensor.transpose(out=tpps[slot][:, j, :], in_=xn[:, c * 128:(c + 1) * 128], identity=st["ident"][:])
        t_pe = kb.mark(ins, "pe")
        last_pe = t_pe
        kb.wait("dve", t_pe, st.get("hT_free"))
        t_ev = kb.mark(nc.vector.tensor_copy(out=hT[:, c8 * 8:(c8 + 1) * 8, tok0:tok0 + 128], in_=tpps[slot][:, :, :]), "dve")
        st["tp_free"][slot] = t_ev
        st["hT_ready"] = t_ev
    st["xn_free"] = last_pe


def make_ident(kb, ident_f, ident):
    nc = kb.nc
    t = kb.mark(nc.gpsimd.memset(ident_f[:], 0.0), "pool")
    kb.wait("pool", t)
    t = kb.mark(nc.gpsimd.affine_select(out=ident_f[:], in_=ident_f[:], pattern=[[-1, 128]], compare_op=ALU.not_equal,
                                        fill=1.0, base=0, channel_multiplier=1), "pool")
    kb.wait("pool", t)
    return kb.mark(nc.gpsimd.tensor_copy(out=ident[:], in_=ident_f[:]), "pool")


def build_phase_a(TPC=TPC):
    nc = bass.Bass("TRN2", target_bir_lowering=False)
    x = nc.dram_tensor("x", [TPC, D], F32, kind="ExternalInput").ap()
    g = nc.dram_tensor("g", [128, D], F32, kind="ExternalInput").ap()
    wa = nc.dram_tensor("wa", [D, 6 * W], F32, kind="ExternalInput").ap()
    wf = nc.dram_tensor("wf", [D, NH], F32, kind="ExternalInput").ap()
    qk = nc.dram_tensor("qk", [4, NH, 128, TPC], F32, kind="ExternalOutput").ap()
    v = nc.dram_tensor("v", [2, TPC, W], F32, kind="ExternalOutput").ap()
    fo = nc.dram_tensor("fo", [NH, TPC], F32, kind="ExternalOutput").ap()
    wa_v = wa.rearrange("(c p) n -> p c n", p=128)
    wf_v = wf.rearrange("(c p) n -> p c n", p=128)
    with ExitStack() as es:
        kb = KB(nc, es)
        grep = kb.sb("grep", [128, D], F32)
        xbuf = kb.sb("xbuf", [128, D], F32)
        xn = kb.sb("xn", [128, D], BF16)
        ss = kb.sb("ss", [128, 1], F32)
        rs = kb.sb("rs", [128, 1], F32)
        epsT = kb.sb("epsT", [128, 1], F32)
        ident_f = kb.sb("ident_f", [128, 128], F32)
        ident = kb.sb("ident", [128, 128], BF16)
        hT = kb.sb("hT", [128, NCH, TT], BF16)
        NST = 3
        stage = [kb.sb(f"stage{i}", [128, 4, 512], F32) for i in range(NST)]
        wblk = [kb.sb(f"wblk{i}", [128, NCH, 512], BF16) for i in range(2)]
        wf_f = kb.sb("wf_f", [128, NCH, NH], F32)
        wf_b = kb.sb("wf_b", [128, NCH, NH], BF16)
        NOST = 4
        ost = [kb.sb(f"ost{i}", [128, 512], F32) for i in range(NOST)]
        tpps = [kb.ps(f"tpps{i}", [128, 8, 128], BF16) for i in range(2)]
        NPS = 4
        mps = [kb.ps(f"mps{i}", [128, 512], F32) for i in range(NPS)]

        st = {"xsem": DmaSem(kb, "xsem"), "tp_i": 0, "tp_free": [None, None], "ident": ident, "eps": epsT}
        csem = DmaSem(kb, "csem")
        stage_sem = [DmaSem(kb, f"stsem{i}") for i in range(NST)]
        stage_free = [None] * NST
        wblk_free = [None] * 2
        ost_sem = [DmaSem(kb, f"ostsem{i}") for i in range(NOST)]
        mps_free = [None] * NPS
        fin = []

        t_id = make_ident(kb, ident_f, ident)
        t_eps = kb.mark(nc.vector.memset(epsT[:], EPS), "dve")
        c1 = csem.inc(nc.sync.dma_start(out=grep[:], in_=g[:, :]))
        c2 = csem.inc(nc.sync.dma_start(out=wf_f[:], in_=wf_v))
        kb.wait("dve", c2)
        t_wf = kb.mark(nc.vector.tensor_copy(out=wf_b[:], in_=wf_f[:]), "dve")
        kb.wait("dve", c1)
        kb.wait("pe", t_id, t_wf)
        kb.wait("act", t_eps)

        piece_i = 0
        blk_i = 0
        mps_i = 0
        ost_i = 0
        last_mm = None
        for tt in range(TPC // TT):
            st["hT_free"] = last_mm
            for s in range(TT // 128):
                r0 = tt * TT + s * 128
                emit_norm_tile(kb, x[r0:r0 + 128, :], grep, xbuf, xn, ss, rs, hT, tpps, s * 128, st)
            kb.wait("pe", st["hT_ready"])
            ms = mps_i % NPS
            mps_i += 1
            kb.wait("pe", mps_free[ms])
            for c in range(NCH):
                ins = nc.tensor.matmul(mps[ms][0:NH, :], lhsT=wf_b[:, c, :], rhs=hT[:, c, :], start=(c == 0), stop=(c == NCH - 1))
            t_mm = kb.mark(ins, "pe")
            os_ = ost_i % NOST
            ost_i += 1
            kb.wait("act", t_mm, ost_sem[os_].tok())
            t_ev = kb.mark(nc.scalar.copy(out=ost[os_][0:NH, :], in_=mps[ms][0:NH, :]), "act")
            mps_free[ms] = t_ev
            kb.wait("act", t_ev)
            ost_sem[os_].inc(nc.scalar.dma_start(out=fo[:, tt * TT:(tt + 1) * TT], in_=ost[os_][0:NH, :]))
            for blk in range(12):
                kind = blk // 2
                half = blk % 2
                wslot = blk_i % 2
                blk_i += 1
                t_cast = None
                for p in range(8):
                    ss_ = piece_i % NST
                    piece_i += 1
                    kb.wait("sp", stage_free[ss_])
                    ld = stage_sem[ss_].inc(nc.sync.dma_start(out=stage[ss_][:], in_=wa_v[:, 4 * p:4 * p + 4, blk * 512:(blk + 1) * 512]))
                    kb.wait("pool", ld)
                    if p == 0:
                        kb.wait("pool", wblk_free[wslot])
                    t_cast = kb.mark(nc.gpsimd.tensor_copy(out=wblk[wslot][:, 4 * p:4 * p + 4, :], in_=stage[ss_][:]), "pool")
                    stage_free[ss_] = t_cast
                kb.wait("pe", t_cast)
                for j in range(4):
                    ms = mps_i % NPS
                    mps_i += 1
                    kb.wait("pe", mps_free[ms])
                    for c in range(NCH):
                        if kind in (2, 5):
                            ins = nc.tensor.matmul(mps[ms][:], lhsT=hT[:, c, j * 128:(j + 1) * 128], rhs=wblk[wslot][:, c, :],
                                                   start=(c == 0), stop=(c == NCH - 1))
                        else:
                            ins = nc.tensor.matmul(mps[ms][:], lhsT=wblk[wslot][:, c, j * 128:(j + 1) * 128], rhs=hT[:, c, :],
                                                   start=(c == 0), stop=(c == NCH - 1))
                    t_mm = kb.mark(ins, "pe")
                    last_mm = t_mm
                    os_ = ost_i % NOST
                    ost_i += 1
                    kb.wait("act", t_mm, ost_sem[os_].tok())
                    t_ev = kb.mark(nc.scalar.copy(out=ost[os_][:], in_=mps[ms][:]), "act")
                    mps_free[ms] = t_ev
                    if kind in (2, 5):
                        dst = v[kind // 3, tt * TT + j * 128: tt * TT + (j + 1) * 128, half * 512:(half + 1) * 512]
                    else:
                        which = {0: 0, 1: 1, 3: 2, 4: 3}[kind]
                        dst = qk[which, half * 4 + j, :, tt * TT:(tt + 1) * TT]
                    kb.wait("act", t_ev)
                    ost_sem[os_].inc(nc.scalar.dma_start(out=dst, in_=ost[os_][:]))
                wblk_free[wslot] = last_mm
        for s_ in ost_sem:
            kb.wait("act", s_.tok())
    return nc


_CACHE = {}


def get_prog(name):
    if name not in _CACHE:
        _CACHE[name] = {"a": build_phase_a, "b": build_phase_b, "c": build_phase_c, "d": build_phase_d}[name]()
    return _CACHE[name]


def run_phase_a(x_flat, g, w_in):
    nc = get_prog("a")
    cols = np.concatenate([np.arange(0, 3 * W), np.arange(4 * W, 7 * W)])
    wa = np.ascontiguousarray(w_in[:, cols])
    wf = np.ascontiguousarray(w_in[:, 8 * W:8 * W + NH])
    grep = np.ascontiguousarray(np.broadcast_to(g[None, :], (128, D)))
    in_maps = [{"x": np.ascontiguousarray(x_flat[c * TPC:(c + 1) * TPC]), "g": grep, "wa": wa, "wf": wf} for c in range(NCORES)]
    res = run_bass_kernel_spmd(nc, in_maps, core_ids=list(range(NCORES)))
    qk = np.concatenate([r["qk"] for r in res.results], axis=3)
    v = np.concatenate([r["v"] for r in res.results], axis=1)
    fo = np.concatenate([r["fo"] for r in res.results], axis=1)
    return qk, v, fo


def build_phase_b(SEQ=SEQ, NB=BATCH):
    nc = bass.Bass("TRN2", target_bir_lowering=False)
    NKC = SEQ // 128
    NQT = SEQ // 512
    NBLK = SEQ // 256
    NPC = SEQ // 2048
    qk = nc.dram_tensor("qk", [NB, 4, 128, SEQ], F32, kind="ExternalInput").ap()
    vv = nc.dram_tensor("vv", [NB, 2, SEQ, 128], F32, kind="ExternalInput").ap()
    ff = nc.dram_tensor("ff", [NB, 1, SEQ], F32, kind="ExternalInput").ap()
    bf = nc.dram_tensor("bf", [1, 1], F32, kind="ExternalInput").ap()
    oo = nc.dram_tensor("oo", [NB, 2, 128, SEQ], F32, kind="ExternalOutput").ap()
    with ExitStack() as es:
        kb = KB(nc, es)
        qT = kb.sb("qT", [128, SEQ], BF16)
        kT = kb.sb("kT", [128, SEQ], BF16)
        vS = kb.sb("vS", [128, NKC, 128], BF16)
        crep = kb.sb("crep", [128, SEQ], F32)
        negc = kb.sb("negc", [128, NKC], F32)
        NSTG = 2
        stg = [kb.sb(f"stg{i}", [128, 2048], F32) for i in range(NSTG)]
        biasT = kb.sb("biasT", [32, SEQ], BF16)
        gate_all = kb.sb("gate_all", [128, SEQ // 128, 32], F32)
        ksum = kb.sb("ksum", [128, 32], F32)
        gate_m = kb.sb("gate_m", [128, 2, 32], F32)
        top8 = kb.sb("top8", [128, 2, 8], F32)
        selb = kb.sb("selb", [128, 2, 32], F32)
        E_all = kb.sb("E_all", [32, 32, 128], BF16)
        ident_f = kb.sb("ident_f", [128, 128], F32)
        ident = kb.sb("ident", [128, 128], BF16)
        tri_f = kb.sb("tri_f", [128, 128], F32)
        tri = kb.sb("tri", [128, 128], BF16)
        ones_b = kb.sb("ones_b", [128, 128], BF16)
        onesrow = kb.sb("onesrow", [1, 128], F32)
        negone = kb.sb("negone", [1, 2], F32)
        one1 = kb.sb("one1", [1, 1], F32)
        negb = kb.sb("negb", [1, 1], F32)
        lrow = kb.sb("lrow", [1, 2048], F32)
        carry = kb.sb("carry", [1, 1], F32)
        NPT = 4
        PT = [kb.sb(f"PT{i}", [128, 512], BF16) for i in range(NPT)]
        tmpS = [kb.sb(f"tmpS{i}", [128, 512], F32) for i in range(NPT)]
        rden = kb.sb("rden", [128, 512], F32)
        NOS = 2
        oS = [kb.sb(f"oS{i}", [128, 512], F32) for i in range(NOS)]
        Sps = [kb.ps(f"Sps{i}", [128, 512], F32) for i in range(NPT)]
        accps = kb.ps("accps", [128, 512], F32)
        denps = kb.ps("denps", [128, 512], F32)
        mscps = [kb.ps(f"mscps{i}", [128, 512], F32) for i in range(2)]

        ldsem = [DmaSem(kb, f"ldsem{i}") for i in range(NSTG)]
        csem = DmaSem(kb, "csem")
        osem = [DmaSem(kb, f"osem{i}") for i in range(NOS)]
        stg_free = [None] * NSTG
        st = {"stg_i": 0, "pt_i": 0, "os_i": 0, "msc_i": 0}
        S_free = [None] * NPT
        PT_free = [None] * NPT
        tmp_free = [None] * NPT
        msc_free = [None, None]
        last = {"pe": None, "act": None, "dve": None, "pool": None}

        def chain(e, ins, *deps):
            kb.wait(e, last[e], *deps)
            t = kb.mark(ins(), e)
            last[e] = t
            return t

        t = chain("pool", lambda: nc.gpsimd.memset(ident_f[:], 0.0))
        t = chain("pool", lambda: nc.gpsimd.affine_select(out=ident_f[:], in_=ident_f[:], pattern=[[-1, 128]],
                                                          compare_op=ALU.not_equal, fill=1.0, base=0, channel_multiplier=1))
        t = chain("pool", lambda: nc.gpsimd.tensor_copy(out=ident[:], in_=ident_f[:]))
        t = chain("pool", lambda: nc.gpsimd.memset(tri_f[:], 0.0))
        t = chain("pool", lambda: nc.gpsimd.affine_select(out=tri_f[:], in_=tri_f[:], pattern=[[1, 128]],
                                                          compare_op=ALU.is_ge, fill=NEG, base=0, channel_multiplier=-1))
        t = chain("pool", lambda: nc.gpsimd.tensor_copy(out=tri[:], in_=tri_f[:]))
        t = chain("pool", lambda: nc.gpsimd.memset(E_all[:], 0.0))
        t = chain("pool", lambda: nc.gpsimd.affine_select(out=E_all[:], in_=E_all[:], pattern=[[-1, 32], [0, 128]],
                                                          compare_op=ALU.not_equal, fill=1.0, base=0, channel_multiplier=1))
        t = chain("pool", lambda: nc.gpsimd.memset(ones_b[:], 1.0))
        t = chain("pool", lambda: nc.gpsimd.memset(onesrow[:], 1.0))
        t = chain("pool", lambda: nc.gpsimd.memset(negone[:], -1.0))
        t = chain("pool", lambda: nc.gpsimd.memset(one1[:], 1.0))
        t_const = chain("pool", lambda: nc.gpsimd.memset(carry[:], 0.0))
        cb = csem.inc(nc.sync.dma_start(out=negb[:], in_=bf[:, :]))
        t_negb = chain("dve", lambda: nc.vector.tensor_scalar(out=negb[:], in0=negb[:], scalar1=-1.0, scalar2=None, op0=ALU.mult), cb)
        chain("dve", lambda: nc.vector.memset(ksum[:], 0.0))
        chain("dve", lambda: nc.vector.memset(gate_all[:], 0.0))
        kb.wait("pe", t_const)
        kb.wait("act", t_const)
        kb.wait("dve", t_const)

        def load_piece(src_ap, shape_view=None):
            s_ = st["stg_i"] % NSTG
            st["stg_i"] += 1
            kb.wait("sp", stg_free[s_])
            dst = stg[s_][:] if shape_view is None else shape_view(stg[s_])
            tok = ldsem[s_].inc(nc.sync.dma_start(out=dst, in_=src_ap))
            return s_, tok

        def attention(b, which, attn_done):
            moba = (which == 0)
            t_k = None
            for p in range(NPC):
                s_, tok = load_piece(qk[b, 2 * which + 1, :, p * 2048:(p + 1) * 2048])
                kb.wait("pool", tok, attn_done)
                t_k = kb.mark(nc.gpsimd.tensor_copy(out=kT[:, p * 2048:(p + 1) * 2048], in_=stg[s_][:]), "pool")
                fr = [t_k]
                if moba:
                    t_ks = chain("dve", lambda: nc.vector.tensor_reduce(
                        out=ksum[:, p * 8:(p + 1) * 8], in_=stg[s_][:].rearrange("p (n t) -> p n t", t=256), axis=AX.X, op=ALU.add),
                        tok, attn_done)
                    fr.append(t_ks)
                stg_free[s_] = fr
            t_v = None
            for p in range(NPC):
                s_, tok = load_piece(vv[b, which, p * 2048:(p + 1) * 2048, :].rearrange("(c p) d -> p c d", p=128),
                                     lambda tl: tl[:].rearrange("p (c d) -> p c d", d=128))
                kb.wait("pool", tok, attn_done)
                t_v = kb.mark(nc.gpsimd.tensor_copy(out=vS[:, p * 16:(p + 1) * 16, :],
                                                   in_=stg[s_][:].rearrange("p (c d) -> p c d", d=128)), "pool")
                stg_free[s_] = [t_v]
            t_q = None
            t_gate = None
            for p in range(NPC):
                s_, tok = load_piece(qk[b, 2 * which, :, p * 2048:(p + 1) * 2048])
                kb.wait("pool", tok, attn_done)
                t_q = kb.mark(nc.gpsimd.tensor_copy(out=qT[:, p * 2048:(p + 1) * 2048], in_=stg[s_][:]), "pool")
                fr = [t_q]
                if moba:
                    m_ = st["msc_i"] % 2
                    st["msc_i"] += 1
                    kb.wait("pe", tok, last["dve"], msc_free[m_])
                    gv = mscps[m_][:].rearrange("p (t n) -> p t n", n=32)
                    for j in range(16):
                        ins = nc.tensor.matmul(gv[:, j, :], lhsT=stg[s_][:, j * 128:(j + 1) * 128], rhs=ksum[:, :], start=True, stop=True)
                    t_g = kb.mark(ins, "pe")
                    last["pe"] = t_g
                    fr.append(t_g)
                    t_gate = chain("dve", lambda: nc.vector.tensor_copy(out=gate_all[:, p * 16:(p + 1) * 16, :], in_=gv), t_g)
                    msc_free[m_] = t_gate
                stg_free[s_] = fr
            t_bias = None
            if moba:
                for qb in range(NBLK):
                    chain("dve", lambda: nc.vector.memset(gate_m[:], -1e30))
                    if qb > 0:
                        chain("dve", lambda: nc.vector.tensor_copy(out=gate_m[:, :, 0:qb], in_=gate_all[:, 2 * qb:2 * qb + 2, 0:qb]))
                    for j in range(2):
                        chain("dve", lambda: nc.vector.max(out=top8[:, j, :], in_=gate_m[:, j, :]))
                        chain("dve", lambda: nc.vector.tensor_scalar(out=selb[:, j, :], in0=gate_m[:, j, :], scalar1=top8[:, j, 2:3],
                                                                    scalar2=None, op0=ALU.is_ge))
                    chain("dve", lambda: nc.vector.tensor_scalar(out=selb[:], in0=selb[:], scalar1=-1.0, scalar2=-NEG,
                                                                op0=ALU.add, op1=ALU.mult))
                    if qb + 1 < 32:
                        chain("dve", lambda: nc.vector.memset(selb[:, :, qb + 1:32], NEG))
                    t_sel = chain("dve", lambda: nc.vector.memset(selb[:, :, qb:qb + 1], 0.0))
                    m_ = st["msc_i"] % 2
                    st["msc_i"] += 1
                    kb.wait("pe", t_sel, msc_free[m_])
                    for j in range(2):
                        ins = nc.tensor.transpose(out=mscps[m_][0:32, j * 128:(j + 1) * 128], in_=selb[:, j, :], identity=ident_f[:])
                    t_tr = kb.mark(ins, "pe")
                    last["pe"] = t_tr
                    t_bias = chain("dve", lambda: nc.vector.tensor_copy(out=biasT[:, qb * 256:(qb + 1) * 256], in_=mscps[m_][0:32, 0:256]),
                                   t_tr, attn_done)
                    msc_free[m_] = t_bias
            else:
                for p in range(NPC):
                    s_, tok = load_piece(ff[b, :, p * 2048:(p + 1) * 2048], lambda tl: tl[0:1, :])
                    fr_ = stg[s_][0:1, :]
                    chain("act", lambda: nc.scalar.activation(out=fr_, in_=fr_, func=AF.Exp, scale=-1.0, bias=negb[:]), tok, t_negb)
                    t_l = chain("act", lambda: nc.scalar.activation(out=fr_, in_=fr_, func=AF.Ln, scale=1.0, bias=one1[:]))
                    chain("dve", lambda: nc.vector.tensor_scalar(out=fr_, in0=fr_, scalar1=-1.0 / SCALE, scalar2=None, op0=ALU.mult), t_l)
                    init = 0.0 if p == 0 else carry[:, 0:1]
                    kb.wait("dve", last["pe"])
                    t_sc = chain("dve", lambda: nc.vector.tensor_tensor_scan(out=lrow[:], data0=one1[:, 0:1].to_broadcast([1, 2048]), data1=fr_,
                                                                           initial=init, op0=ALU.mult, op1=ALU.add))
                    stg_free[s_] = [t_sc]
                    t_sc = chain("dve", lambda: nc.vector.tensor_copy(out=carry[:], in_=lrow[:, 2047:2048]))
                    for i in range(4):
                        m_ = st["msc_i"] % 2
                        st["msc_i"] += 1
                        kb.wait("pe", t_sc, msc_free[m_])
                        t_mm = kb.mark(nc.tensor.matmul(mscps[m_][:], lhsT=onesrow[:], rhs=lrow[:, i * 512:(i + 1) * 512], start=True, stop=True), "pe")
                        last["pe"] = t_mm
                        t_cr = chain("dve", lambda: nc.vector.tensor_copy(out=crep[:, p * 2048 + i * 512:p * 2048 + (i + 1) * 512], in_=mscps[m_][:]),
                                     t_mm, attn_done)
                        msc_free[m_] = t_cr
                    m_ = st["msc_i"] % 2
                    st["msc_i"] += 1
                    kb.wait("pe", t_sc, msc_free[m_])
                    for kc in range(16):
                        ins = nc.tensor.matmul(mscps[m_][:, 2 * kc:2 * kc + 2], lhsT=lrow[:, kc * 128:(kc + 1) * 128], rhs=negone[:], start=True, stop=True)
                    t_mm = kb.mark(ins, "pe")
                    last["pe"] = t_mm
                    t_bias = chain("dve", lambda: nc.vector.tensor_copy(
                        out=negc[:, p * 16:(p + 1) * 16], in_=mscps[m_][:, 0:32].rearrange("p (c two) -> p c two", two=2)[:, :, 0]), t_mm, attn_done)
                    msc_free[m_] = t_bias

            kb.wait("pe", t_k, t_v, t_q, t_bias)
            units = [(Q, kc) for Q in range(NQT) for kc in range(4 * Q + 4)]
            res = {"last_pv": None}

            def emit_qk(Q, kc):
                q0 = Q * 512
                c0 = max(0, kc * 128 - q0)
                diag = kc * 128 >= q0
                n = kc // 2
                sl = st["pt_i"] % NPT
                st["pt_i"] += 1
                kb.wait("pe", S_free[sl])
                ins = nc.tensor.matmul(Sps[sl][:, c0:512], lhsT=kT[:, kc * 128:(kc + 1) * 128], rhs=qT[:, q0 + c0:q0 + 512],
                                       start=True, stop=not (moba or diag))
                if moba:
                    ins = nc.tensor.matmul(Sps[sl][:, c0:512], lhsT=E_all[:, n, :], rhs=biasT[:, q0 + c0:q0 + 512],
                                           start=False, stop=not diag)
                if diag:
                    ins = nc.tensor.matmul(Sps[sl][:, c0:c0 + 128], lhsT=ident[:], rhs=tri[:], start=False, stop=True)
                t_s = kb.mark(ins, "pe")
                if moba:
                    kb.wait("act", t_s, PT_free[sl])
                    t_p = kb.mark(nc.scalar.activation(out=PT[sl][:, c0:512], in_=Sps[sl][:, c0:512], func=AF.Exp, scale=SCALE), "act")
                    S_free[sl] = t_p
                else:
                    kb.wait("dve", t_s, tmp_free[sl])
                    t_t = kb.mark(nc.vector.scalar_tensor_tensor(out=tmpS[sl][:, c0:512], in0=Sps[sl][:, c0:512], scalar=negc[:, kc:kc + 1],
                                                                 in1=crep[:, q0 + c0:q0 + 512], op0=ALU.add, op1=ALU.add), "dve")
                    S_free[sl] = t_t
                    kb.wait("act", t_t, PT_free[sl])
                    t_p = kb.mark(nc.scalar.activation(out=PT[sl][:, c0:512], in_=tmpS[sl][:, c0:512], func=AF.Exp, scale=SCALE), "act")
                    tmp_free[sl] = t_p
                return (Q, kc, sl, c0, t_p)

            def emit_pv(info):
                Q, kc, sl, c0, t_p = info
                q0 = Q * 512
                nkc = 4 * Q + 4
                first = (kc == 0)
                kb.wait("pe", t_p)
                if first:
                    kb.wait("pe", st.get("acc_free"))
                nc.tensor.matmul(accps[:, c0:512], lhsT=vS[:, kc, :], rhs=PT[sl][:, c0:512], start=first, stop=(kc == nkc - 1))
                ins = nc.tensor.matmul(denps[:, c0:512], lhsT=ones_b[:], rhs=PT[sl][:, c0:512], start=first, stop=(kc == nkc - 1))
                t_pv = kb.mark(ins, "pe")
                PT_free[sl] = t_pv
                res["last_pv"] = t_pv
                if kc == nkc - 1:
                    t_r = chain("dve", lambda: nc.vector.reciprocal(out=rden[:], in_=denps[:]), t_pv)
                    o_ = st["os_i"] % NOS
                    st["os_i"] += 1
                    t_o = chain("dve", lambda: nc.vector.tensor_tensor(out=oS[o_][:], in0=accps[:], in1=rden[:], op=ALU.mult), osem[o_].tok())
                    st["acc_free"] = t_o
                    kb.wait("sp", t_o)
                    osem[o_].inc(nc.sync.dma_start(out=oo[b, which, :, q0:q0 + 512], in_=oS[o_][:]))

            LAG = 2
            pend = []
            for (Q, kc) in units:
                pend.append(emit_qk(Q, kc))
                if len(pend) > LAG:
                    emit_pv(pend.pop(0))
            while pend:
                emit_pv(pend.pop(0))
            t_last_pv = res["last_pv"]
            return t_last_pv

        done = None
        for b in range(NB):
            for which in range(2):
                done = attention(b, which, done)
        for s_ in osem:
            kb.wait("sp", s_.tok())
    return nc


def run_phase_b(qk, v, fo, b_forget):
    nc = get_prog("b")
    in_maps = []
    for h in range(NCORES):
        qk_h = np.ascontiguousarray(qk[:, h].reshape(4, 128, BATCH, SEQ).transpose(2, 0, 1, 3))
        v_h = np.ascontiguousarray(v[:, :, h * 128:(h + 1) * 128].reshape(2, BATCH, SEQ, 128).transpose(1, 0, 2, 3))
        f_h = np.ascontiguousarray(fo[h].reshape(BATCH, 1, SEQ))
        in_maps.append({"qk": qk_h, "vv": v_h, "ff": f_h, "bf": np.ascontiguousarray(b_forget[h].reshape(1, 1))})
    res = run_bass_kernel_spmd(nc, in_maps, core_ids=list(range(NCORES)))
    return np.stack([r["oo"] for r in res.results], axis=0)


HAL = 16


def build_phase_c(TPC=TPC, T=512):
    nc = bass.Bass("TRN2", target_bir_lowering=False)
    NT = TPC // T
    x = nc.dram_tensor("x", [TPC, D], F32, kind="ExternalInput").ap()
    xh = nc.dram_tensor("xh", [128, D], F32, kind="ExternalInput").ap()
    gcol = nc.dram_tensor("gcol", [128, NCH], F32, kind="ExternalInput").ap()
    wc = nc.dram_tensor("wc", [D, 24576], F32, kind="ExternalInput").ap()
    wbr = nc.dram_tensor("wbr", [D, D], F32, kind="ExternalInput").ap()
    wo = nc.dram_tensor("wo", [D, D], F32, kind="ExternalInput").ap()
    oa = nc.dram_tensor("oa", [2, W, TPC], F32, kind="ExternalInput").ap()
    icnt = nc.dram_tensor("icnt", [128, 4, TPC], F32, kind="ExternalInput").ap()
    wpool = nc.dram_tensor("wpool", [128, 8, 256], F32, kind="ExternalInput").ap()
    pscale = nc.dram_tensor("pscale", [128, 8], F32, kind="ExternalInput").ap()
    wconv = nc.dram_tensor("wconv", [128, 3, 8], F32, kind="ExternalInput").ap()
    bmerge = nc.dram_tensor("bmerge", [128, 4, NCH], F32, kind="ExternalInput").ap()
    y = nc.dram_tensor("y", [TPC, D], F32, kind="ExternalOutput").ap()
    NBT = 32 + 80 + 16
    wcache = nc.dram_tensor("wcache", [NBT, 128, NCH * 256], BF16, kind="Internal").ap()
    wc_v = wc.rearrange("(c p) n -> p c n", p=128)
    wbr_v = wbr.rearrange("(c p) n -> p c n", p=128)
    wo_v = wo.rearrange("(c p) n -> p c n", p=128)
    TE = T + HAL
    with ExitStack() as es:
        kb = KB(nc, es)
        hT = kb.sb("hT", [128, NCH, TE], BF16)
        brT = kb.sb("brT", [128, NCH, T], BF16)
        assert NCH * T >= 12288
        mg_raw = kb.sb("mg_raw", [128, NCH * T], BF16)
        mgT = mg_raw[:].rearrange("p (c t) -> p c t", t=T)
        xbuf = mg_raw[:, 0:8192].bitcast(F32)
        xn = mg_raw[:, 8192:12288]
        NW = 2
        wblk = [kb.sb(f"wblk{i}", [128, NCH, 256], BF16) for i in range(NW)]
        NST = 2
        stage = [kb.sb(f"stage{i}", [128, 8, 256], F32) for i in range(NST)]
        ss = kb.sb("ss", [128, 1], F32)
        rs = kb.sb("rs", [128, 1], F32)
        epsT = kb.sb("epsT", [128, 1], F32)
        gcolS = kb.sb("gcolS", [128, NCH], F32)
        ident_f = kb.sb("ident_f", [128, 128], F32)
        ident = kb.sb("ident", [128, 128], BF16)
        icS = kb.sb("icS", [128, 4, T], F32)
        wpool_f = stage[0]
        wpool_b = kb.sb("wpool_b", [128, 8, 256], BF16)
        pscS = kb.sb("pscS", [128, 8], F32)
        wcvS = kb.sb("wcvS", [128, 3, 8], F32)
        bmS = kb.sb("bmS", [128, 4, NCH], F32)
        NTF = 2
        tmpf = [kb.sb(f"tmpf{i}", [128, T], F32) for i in range(NTF)]
        otile = [kb.sb(f"otile{i}", [128, T], F32) for i in range(2)]
        uext = kb.sb("uext", [128, 2, TE], F32)
        pa = kb.sb("pa", [128, 2, TE], F32)
        pb = kb.sb("pb", [128, 2, TE], F32)
        pooledT = kb.sb("pooledT", [128, 2, T], BF16)
        yS = kb.sb("yS", [128, 2, T], F32)
        bS = kb.sb("bS", [128, T], F32)
        cext = kb.sb("cext", [128, TE], F32)
        zext = kb.sb("zext", [128, TE], F32)
        ycv = kb.sb("ycv", [128, T], F32)
        sgT = kb.sb("sgT", [128, 4, 2, T], BF16)
        accm = kb.sb("accm", [128, T], F32)
        xres = [kb.sb(f"xres{i}", [128, 256], F32) for i in range(2)]
        orow = [kb.sb(f"orow{i}", [128, 256], F32) for i in range(2)]
        tpps = [kb.ps(f"tpps{i}", [128, 8, 128], BF16) for i in range(2)]
        NPS = 4
        mps = [kb.ps(f"mps{i}", [128, 512], F32) for i in range(NPS)]
        hps = kb.ps("hps", [128, 512], F32)

        last = {"pe": None, "act": None, "dve": None, "pool": None}

        def chain(e, ins, *deps):
            kb.wait(e, last[e], *deps)
            t = kb.mark(ins(), e)
            last[e] = t
            return t

        def pe_mark(ins):
            t = kb.mark(ins, "pe")
            last["pe"] = t
            return t

        csem = DmaSem(kb, "csem")
        xsem = DmaSem(kb, "xsem")
        stage_sem = [DmaSem(kb, f"stsem{i}") for i in range(NST)]
        stage_free = [None] * NST
        wblk_free = [None] * NW
        wl_sem = [DmaSem(kb, f"wlsem{i}") for i in range(NW)]
        wst_sem = [DmaSem(kb, f"wstsem{i}") for i in range(NW)]
        wst_tok = {}
        osem = [DmaSem(kb, f"osem{i}") for i in range(2)]
        otile_free = [None, None]
        xrsem = [DmaSem(kb, f"xrsem{i}") for i in range(2)]
        xres_free = [None, None]
        orsem = [DmaSem(kb, f"orsem{i}") for i in range(2)]
        icsem = DmaSem(kb, "icsem")
        mps_free = [None] * NPS
        hps_free = [None]
        tp_free = [None, None]
        cnt = {"piece": 0, "blk": 0, "mps": 0, "tp": 0, "tf": 0, "ot": 0, "xr": 0}

        t = chain("pool", lambda: nc.gpsimd.memset(ident_f[:], 0.0))
        t = chain("pool", lambda: nc.gpsimd.affine_select(out=ident_f[:], in_=ident_f[:], pattern=[[-1, 128]],
                                                          compare_op=ALU.not_equal, fill=1.0, base=0, channel_multiplier=1))
        t_id = chain("pool", lambda: nc.gpsimd.tensor_copy(out=ident[:], in_=ident_f[:]))
        c_all = None
        for dst, src in ((gcolS, gcol), (wpool_f, wpool), (pscS, pscale), (wcvS, wconv), (bmS, bmerge)):
            c_all = csem.inc(nc.sync.dma_start(out=dst[:], in_=src))
        chain("dve", lambda: nc.vector.memset(epsT[:], EPS))
        t_c = chain("dve", lambda: nc.vector.tensor_copy(out=wpool_b[:], in_=wpool_f[:]), c_all)
        stage_free[0] = t_c
        kb.wait("pe", t_id, t_c)
        kb.wait("act", t_c)

        def norm_tile(x_rows, dst_col0, src_c0, ncols):
            kb.wait("sp", last["dve"], last["pe"])
            ld = xsem.inc(nc.sync.dma_start(out=xbuf[:], in_=x_rows))
            chain("act", lambda: nc.scalar.activation(out=xn[:], in_=xbuf[:], func=AF.Square, accum_out=ss[:]), ld, last["pe"], last["dve"])
            t_a = chain("act", lambda: nc.scalar.activation(out=rs[:], in_=ss[:], func=AF.Sqrt, scale=1.0 / D, bias=epsT[:]))
            chain("dve", lambda: nc.vector.reciprocal(out=rs[:], in_=rs[:]), t_a)
            t_xn = chain("dve", lambda: nc.vector.tensor_scalar(out=xn[:], in0=xbuf[:], scalar1=rs[:, 0:1], scalar2=None, op0=ALU.mult))
            for c8 in range(NCH // 8):
                sl = cnt["tp"] % 2
                cnt["tp"] += 1
                kb.wait("pe", t_xn, tp_free[sl])
                for j in range(8):
                    c = c8 * 8 + j
                    ins = nc.tensor.transpose(out=tpps[sl][:, j, :], in_=xn[:, c * 128:(c + 1) * 128], identity=ident[:])
                t_pe = pe_mark(ins)
                for j in range(8):
                    c = c8 * 8 + j
                    t_ev = chain("dve", lambda: nc.vector.tensor_scalar(out=hT[:, c, dst_col0:dst_col0 + ncols],
                                                                       in0=tpps[sl][:, j, src_c0:src_c0 + ncols],
                                                                       scalar1=gcolS[:, c:c + 1], scalar2=None, op0=ALU.mult), t_pe)
                tp_free[sl] = t_ev

        def load_block(view, col0):
            slot = cnt["blk"] % NW
            bi = cnt["blk"] % NBT
            tile_i = cnt["blk"] // NBT
            cnt["blk"] += 1
            if tile_i == 0:
                t_cast = None
                for p in range(4):
                    s_ = cnt["piece"] % NST
                    cnt["piece"] += 1
                    kb.wait("sp", stage_free[s_])
                    ld = stage_sem[s_].inc(nc.sync.dma_start(out=stage[s_][:], in_=view[:, 8 * p:8 * p + 8, col0:col0 + 256]))
                    deps = [ld]
                    if p == 0:
                        deps.append(wblk_free[slot])
                    t_cast = chain("pool", lambda: nc.gpsimd.tensor_copy(out=wblk[slot][:, 8 * p:8 * p + 8, :], in_=stage[s_][:]), *deps)
                    stage_free[s_] = t_cast
                kb.wait("pe", t_cast)
                if NT > 1:
                    kb.wait("pool", t_cast)
                    wst_tok[bi] = wst_sem[slot].inc(nc.gpsimd.dma_start(out=wcache[bi, :, :], in_=wblk[slot][:].rearrange("p c n -> p (c n)")))
                    cnt["wst"] = (slot, wst_tok[bi])
            else:
                kb.wait("sp", wblk_free[slot], wst_tok[bi])
                ld = wl_sem[slot].inc(nc.sync.dma_start(out=wblk[slot][:].rearrange("p c n -> p (c n)"), in_=wcache[bi, :, :]))
                kb.wait("pe", ld)
            return slot

        def release(slot):
            toks = [last["pe"]]
            w = cnt.get("wst")
            if w is not None and w[0] == slot:
                toks.append(w[1])
            wblk_free[slot] = toks

        def mm_feat(slot, j, halo=False):
            ms = cnt["mps"] % NPS
            cnt["mps"] += 1
            kb.wait("pe", mps_free[ms])
            for c in range(NCH):
                ins = nc.tensor.matmul(mps[ms][:, 0:T], lhsT=wblk[slot][:, c, j * 128:(j + 1) * 128], rhs=hT[:, c, HAL:TE],
                                       start=(c == 0), stop=(c == NCH - 1))
            if halo:
                kb.wait("pe", hps_free[0])
                for c in range(NCH):
                    ins = nc.tensor.matmul(hps[:, 0:HAL], lhsT=wblk[slot][:, c, j * 128:(j + 1) * 128], rhs=hT[:, c, 0:HAL],
                                           start=(c == 0), stop=(c == NCH - 1))
            return ms, pe_mark(ins)

        def get_tmpf():
            i = cnt["tf"] % NTF
            cnt["tf"] += 1
            return tmpf[i]

        pending = []

        for tt in range(NT):
            t0 = tt * T
            if tt == 0:
                norm_tile(xh[:, :], 0, 128 - HAL, HAL)
            else:
                chain("dve", lambda: nc.vector.tensor_copy(out=hT[:, :, 0:HAL], in_=hT[:, :, T:TE]), last["pe"])
            for s in range(T // 128):
                norm_tile(x[t0 + s * 128:t0 + (s + 1) * 128, :], HAL + s * 128, 0, 128)
            kb.wait("pe", last["dve"])
            kb.wait("sp", last["dve"])
            t_ic = icsem.inc(nc.sync.dma_start(out=icS[:], in_=icnt[:, :, t0:t0 + T]))
            kb.wait("dve", t_ic)

            for blk in range(32):
                slot = load_block(wc_v, blk * 256)
                if blk < 8:
                    br = blk // 4
                    for j in range(2):
                        ch = 2 * (blk % 4) + j
                        ms, t_mm = mm_feat(slot, j)
                        tf = get_tmpf()
                        t_a = chain("act", lambda: nc.scalar.activation(out=tf[:], in_=mps[ms][:, 0:T], func=AF.Silu), t_mm, last["dve"])
                        mps_free[ms] = t_a
                        oi = cnt["ot"] % 2
                        cnt["ot"] += 1
                        kb.wait("sp", otile_free[oi])
                        t_o = osem[oi].inc(nc.sync.dma_start(out=otile[oi][:], in_=oa[br, ch * 128:(ch + 1) * 128, t0:t0 + T]))
                        t_d = chain("dve", lambda: nc.vector.tensor_tensor(out=brT[:, br * 8 + ch, :], in0=tf[:], in1=otile[oi][:], op=ALU.mult), t_a, t_o)
                        otile_free[oi] = t_d
                elif blk < 16:
                    g = (blk - 8) // 2
                    if (blk - 8) % 2 == 0:
                        for j in range(2):
                            ms, t_mm = mm_feat(slot, j, halo=True)
                            chain("act", lambda: nc.scalar.copy(out=uext[:, j, HAL:TE], in_=mps[ms][:, 0:T]), t_mm, last["dve"])
                            t_a = chain("act", lambda: nc.scalar.copy(out=uext[:, j, 0:HAL], in_=hps[:, 0:HAL]))
                            mps_free[ms] = t_a
                            hps_free[0] = t_a
                        src = uext
                        bufs = [pa, pb]
                        for k in range(g + 1):
                            sh = 2 ** k
                            dst = bufs[k % 2]
                            lo = 2 * sh - 1
                            chain("dve", lambda: nc.vector.tensor_tensor(out=dst[:, :, lo:TE], in0=src[:, :, lo:TE], in1=src[:, :, lo - sh:TE - sh], op=ALU.add), last["act"])
                            src = dst
                        for j in range(2):
                            tf = get_tmpf()
                            chain("dve", lambda: nc.vector.tensor_tensor(out=tf[:], in0=src[:, j, HAL:TE], in1=icS[:, g, :], op=ALU.mult), last["act"])
                            t_p = chain("dve", lambda: nc.vector.tensor_tensor(out=pooledT[:, j, :], in0=tf[:], in1=uext[:, j, HAL:TE], op=ALU.subtract), last["pe"])
                        for oc in range(2):
                            ms = cnt["mps"] % NPS
                            cnt["mps"] += 1
                            kb.wait("pe", mps_free[ms], t_p)
                            for j in range(2):
                                ins = nc.tensor.matmul(mps[ms][:, 0:T], lhsT=wpool_b[:, 2 * g + j, oc * 128:(oc + 1) * 128], rhs=pooledT[:, j, :],
                                                       start=(j == 0), stop=(j == 1))
                            t_mm = pe_mark(ins)
                            t_y = chain("dve", lambda: nc.vector.tensor_scalar(out=yS[:, oc, :], in0=mps[ms][:, 0:T], scalar1=pscS[:, 2 * g + oc:2 * g + oc + 1],
                                                                              scalar2=None, op0=ALU.mult), t_mm)
                            mps_free[ms] = t_y
                    else:
                        for j in range(2):
                            ms, t_mm = mm_feat(slot, j)
                            tf = get_tmpf()
                            t_a = chain("act", lambda: nc.scalar.activation(out=tf[:], in_=mps[ms][:, 0:T], func=AF.Silu), t_mm, last["dve"])
                            mps_free[ms] = t_a
                            chain("dve", lambda: nc.vector.tensor_tensor(out=brT[:, 16 + 2 * g + j, :], in0=tf[:], in1=yS[:, j, :], op=ALU.mult), t_a)
                else:
                    i = (blk - 16) // 2
                    if (blk - 16) % 2 == 0:
                        ms, t_mm = mm_feat(slot, 0)
                        t_a = chain("act", lambda: nc.scalar.copy(out=bS[:], in_=mps[ms][:, 0:T]), t_mm, last["dve"])
                        mps_free[ms] = t_a
                        ms, t_mm = mm_feat(slot, 1, halo=True)
                        chain("act", lambda: nc.scalar.copy(out=cext[:, HAL:TE], in_=mps[ms][:, 0:T]), t_mm, last["dve"])
                        t_a = chain("act", lambda: nc.scalar.copy(out=cext[:, 0:HAL], in_=hps[:, 0:HAL]))
                        mps_free[ms] = t_a
                        hps_free[0] = t_a
                    else:
                        ms, t_mm = mm_feat(slot, 0, halo=True)
                        chain("dve", lambda: nc.vector.tensor_tensor(out=zext[:, HAL:TE], in0=mps[ms][:, 0:T], in1=cext[:, HAL:TE], op=ALU.mult), t_mm, last["act"])
                        t_d = chain("dve", lambda: nc.vector.tensor_tensor(out=zext[:, 0:HAL], in0=hps[:, 0:HAL], in1=cext[:, 0:HAL], op=ALU.mult))
                        mps_free[ms] = t_d
                        hps_free[0] = t_d
                        chain("dve", lambda: nc.vector.tensor_scalar(out=ycv[:], in0=zext[:, HAL - 2:TE - 2], scalar1=wcvS[:, 0, i:i + 1], scalar2=None, op0=ALU.mult))
                        chain("dve", lambda: nc.vector.scalar_tensor_tensor(out=ycv[:], in0=zext[:, HAL - 1:TE - 1], scalar=wcvS[:, 1, i:i + 1], in1=ycv[:],
                                                                           op0=ALU.mult, op1=ALU.add))
                        chain("dve", lambda: nc.vector.scalar_tensor_tensor(out=ycv[:], in0=zext[:, HAL:TE], scalar=wcvS[:, 2, i:i + 1], in1=ycv[:],
                                                                           op0=ALU.mult, op1=ALU.add))
                        chain("dve", lambda: nc.vector.tensor_tensor(out=ycv[:], in0=ycv[:], in1=bS[:], op=ALU.mult))
                        ms, t_mm = mm_feat(slot, 1)
                        tf = get_tmpf()
                        t_a = chain("act", lambda: nc.scalar.activation(out=tf[:], in_=mps[ms][:, 0:T], func=AF.Silu), t_mm, last["dve"])
                        mps_free[ms] = t_a
                        chain("dve", lambda: nc.vector.tensor_tensor(out=brT[:, 24 + i, :], in0=tf[:], in1=ycv[:], op=ALU.mult), t_a)
                release(slot)

            for dp in range(NCH // 2):
                for mb in range(4):
                    slot = load_block(wc_v, 8192 + (dp * 4 + mb) * 256)
                    dcl = mb // 2
                    dc = 2 * dp + dcl
                    for j in range(2):
                        i = 2 * (mb % 2) + j
                        ms, t_mm = mm_feat(slot, j)
                        t_a = chain("act", lambda: nc.scalar.activation(out=sgT[:, i, dcl, :], in_=mps[ms][:, 0:T], func=AF.Sigmoid,
                                                                       bias=bmS[:, i, dc:dc + 1]), t_mm, last["dve"])
                        mps_free[ms] = t_a
                    release(slot)
                slot = load_block(wbr_v, dp * 256)
                kb.wait("pe", last["dve"])
                for dcl in range(2):
                    dc = 2 * dp + dcl
                    for i in range(4):
                        ms = cnt["mps"] % NPS
                        cnt["mps"] += 1
                        kb.wait("pe", mps_free[ms])
                        for wcn in range(8):
                            ins = nc.tensor.matmul(mps[ms][:, 0:T], lhsT=wblk[slot][:, 8 * i + wcn, dcl * 128:(dcl + 1) * 128], rhs=brT[:, 8 * i + wcn, :],
                                                   start=(wcn == 0), stop=(wcn == 7))
                        t_mm = pe_mark(ins)
                        if i == 0:
                            t_d = chain("dve", lambda: nc.vector.tensor_tensor(out=accm[:], in0=mps[ms][:, 0:T], in1=sgT[:, i, dcl, :], op=ALU.mult), t_mm, last["act"])
                        else:
                            tf = get_tmpf()
                            t_d = chain("dve", lambda: nc.vector.tensor_tensor(out=tf[:], in0=mps[ms][:, 0:T], in1=sgT[:, i, dcl, :], op=ALU.mult), t_mm, last["act"])
                            if i < 3:
                                chain("dve", lambda: nc.vector.tensor_tensor(out=accm[:], in0=accm[:], in1=tf[:], op=ALU.add))
                            else:
                                chain("dve", lambda: nc.vector.tensor_tensor(out=mgT[:, dc, :], in0=accm[:], in1=tf[:], op=ALU.add), last["pe"])
                        mps_free[ms] = t_d
                release(slot)

            kb.wait("pe", last["dve"])
            for ob in range(16):
                slot = load_block(wo_v, ob * 256)
                for st_ in pending:
                    st_()
                pending = []
                for s in range(T // 128):
                    ms = cnt["mps"] % NPS
                    cnt["mps"] += 1
                    kb.wait("pe", mps_free[ms])
                    for c in range(NCH):
                        ins = nc.tensor.matmul(mps[ms][:, 0:256], lhsT=mgT[:, c, s * 128:(s + 1) * 128], rhs=wblk[slot][:, c, :],
                                               start=(c == 0), stop=(c == NCH - 1))
                    t_mm = pe_mark(ins)
                    xi = cnt["xr"] % 2
                    cnt["xr"] += 1
                    kb.wait("sp", xres_free[xi])
                    r0 = t0 + s * 128
                    t_x = xrsem[xi].inc(nc.sync.dma_start(out=xres[xi][:], in_=x[r0:r0 + 128, ob * 256:(ob + 1) * 256]))
                    t_d = chain("dve", lambda: nc.vector.tensor_tensor(out=orow[xi][:], in0=mps[ms][:, 0:256], in1=xres[xi][:], op=ALU.add),
                                t_mm, t_x, orsem[xi].tok())
                    mps_free[ms] = t_d
                    xres_free[xi] = t_d

                    def mk_store(xi=xi, r0=r0, ob=ob, t_d=t_d):
                        kb.wait("sp", t_d)
                        orsem[xi].inc(nc.sync.dma_start(out=y[r0:r0 + 128, ob * 256:(ob + 1) * 256], in_=orow[xi][:]))
                    mk_store()
                release(slot)
        for s_ in orsem:
            kb.wait("sp", s_.tok())
    return nc


def prep_c_consts(w_in, w_pool, pool_scale, w_conv, w_branch, b_merge, w_out, g):
    cols = []
    for blk in range(4):
        cols.append(np.arange(3072 + blk * 256, 3072 + (blk + 1) * 256))
    for blk in range(4):
        cols.append(np.arange(7168 + blk * 256, 7168 + (blk + 1) * 256))
    for gg in range(4):
        cols.append(np.arange(8200 + gg * 256, 8200 + (gg + 1) * 256))
        cols.append(np.arange(9224 + gg * 256, 9224 + (gg + 1) * 256))
    for i in range(8):
        cols.append(np.arange(10248 + i * 128, 10248 + (i + 1) * 128))
        cols.append(np.arange(11272 + i * 128, 11272 + (i + 1) * 128))
        cols.append(np.arange(12296 + i * 128, 12296 + (i + 1) * 128))
        cols.append(np.arange(13320 + i * 128, 13320 + (i + 1) * 128))
    for dp in range(16):
        for mb in range(4):
            dc = 2 * dp + mb // 2
            for j in range(2):
                i = 2 * (mb % 2) + j
                cols.append(np.arange(14344 + i * 4096 + dc * 128, 14344 + i * 4096 + (dc + 1) * 128))
    cols = np.concatenate(cols)
    assert cols.shape[0] == 24576
    return {
        "wc": np.ascontiguousarray(w_in[:, cols]),
        "wbr": np.ascontiguousarray(w_branch.reshape(D, D)),
        "wo": np.ascontiguousarray(w_out),
        "gcol": np.ascontiguousarray(g.reshape(NCH, 128).T),
        "wpool": np.ascontiguousarray(w_pool.reshape(4, 2, 128, 256).transpose(2, 0, 1, 3).reshape(128, 8, 256)),
        "pscale": np.ascontiguousarray(pool_scale.reshape(8, 128).T),
        "wconv": np.ascontiguousarray(w_conv.reshape(3, 8, 128).transpose(2, 0, 1)),
        "bmerge": np.ascontiguousarray(b_merge.reshape(4, NCH, 128).transpose(2, 0, 1)),
    }


def icnt_table(pos0, n):
    pos = np.arange(pos0, pos0 + n)
    tab = np.stack([1.0 / np.minimum(pos + 1, w) for w in (2, 4, 8, 16)], axis=0).astype(np.float32)
    return np.ascontiguousarray(np.broadcast_to(tab[None], (128, 4, n)))


def run_phase_c(h, oo, consts):
    nc = get_prog("c")
    in_maps = []
    for c in range(NCORES):
        b = c // (NCORES // BATCH)
        off = (c % (NCORES // BATCH)) * TPC
        r0 = c * TPC
        xh = np.zeros((128, D), np.float32) if off == 0 else np.ascontiguousarray(h[r0 - 128:r0])
        oa = np.ascontiguousarray(oo[:, b, :, :, off:off + TPC].transpose(1, 0, 2, 3).reshape(2, W, TPC))
        m = dict(consts)
        m.update({"x": np.ascontiguousarray(h[r0:r0 + TPC]), "xh": xh, "oa": oa, "icnt": icnt_table(off, TPC)})
        in_maps.append(m)
    res = run_bass_kernel_spmd(nc, in_maps, core_ids=list(range(NCORES)))
    return np.concatenate([r["y"] for r in res.results], axis=0)


def build_phase_d(TPC=TPC):
    nc = bass.Bass("TRN2", target_bir_lowering=False)
    x = nc.dram_tensor("x", [TPC, D], F32, kind="ExternalInput").ap()
    g = nc.dram_tensor("g", [128, D], F32, kind="ExternalInput").ap()
    y = nc.dram_tensor("y", [TPC, D], F32, kind="ExternalOutput").ap()
    with ExitStack() as es:
        kb = KB(nc, es)
        grep = kb.sb("grep", [128, D], F32)
        NB_ = 2
        xb = [kb.sb(f"xb{i}", [128, D], F32) for i in range(NB_)]
        yb = [kb.sb(f"yb{i}", [128, D], F32) for i in range(NB_)]
        junk = kb.sb("junk", [128, D], BF16)
        ss = kb.sb("ss", [128, 1], F32)
        rs = kb.sb("rs", [128, 1], F32)
        epsT = kb.sb("epsT", [128, 1], F32)
        last = {"act": None, "dve": None}

        def chain(e, ins, *deps):
            kb.wait(e, last[e], *deps)
            t = kb.mark(ins(), e)
            last[e] = t
            return t

        csem = DmaSem(kb, "csem")
        xs = [DmaSem(kb, f"xs{i}") for i in range(NB_)]
        ys = [DmaSem(kb, f"ys{i}") for i in range(NB_)]
        xfree = [None] * NB_
        c1 = csem.inc(nc.sync.dma_start(out=grep[:], in_=g[:, :]))
        t_e = chain("dve", lambda: nc.vector.memset(epsT[:], EPS), c1)
        kb.wait("act", t_e)
        for i in range(TPC // 128):
            s_ = i % NB_
            kb.wait("sp", xfree[s_])
            ld = xs[s_].inc(nc.sync.dma_start(out=xb[s_][:], in_=x[i * 128:(i + 1) * 128, :]))
            chain("act", lambda: nc.scalar.activation(out=junk[:], in_=xb[s_][:], func=AF.Square, accum_out=ss[:]), ld, last["dve"])
            t_a = chain("act", lambda: nc.scalar.activation(out=rs[:], in_=ss[:], func=AF.Sqrt, scale=1.0 / D, bias=epsT[:]))
            chain("dve", lambda: nc.vector.reciprocal(out=rs[:], in_=rs[:]), t_a)
            t_y = chain("dve", lambda: nc.vector.scalar_tensor_tensor(out=yb[s_][:], in0=xb[s_][:], scalar=rs[:, 0:1], in1=grep[:],
                                                                     op0=ALU.mult, op1=ALU.mult), ys[s_].tok())
            xfree[s_] = t_y
            kb.wait("sp", t_y)
            ys[s_].inc(nc.sync.dma_start(out=y[i * 128:(i + 1) * 128, :], in_=yb[s_][:]))
        for s_ in ys:
            kb.wait("sp", s_.tok())
    return nc


def run_phase_d(h, final_g):
    nc = get_prog("d")
    grep = np.ascontiguousarray(np.broadcast_to(final_g[None, :], (128, D)))
    in_maps = [{"x": np.ascontiguousarray(h[c * TPC:(c + 1) * TPC]), "g": grep} for c in range(NCORES)]
    res = run_bass_kernel_spmd(nc, in_maps, core_ids=list(range(NCORES)))
    return np.concatenate([r["y"] for r in res.results], axis=0)


def kernel(x, norm_g, w_in, b_forget, w_pool, pool_scale, w_conv, w_branch, b_merge, w_out, final_g):
    f32 = lambda a: np.asarray(a, dtype=np.float32)
    x, norm_g, w_in, b_forget, w_pool, pool_scale, w_conv, w_branch, b_merge, w_out, final_g = map(
        f32, (x, norm_g, w_in, b_forget, w_pool, pool_scale, w_conv, w_branch, b_merge, w_out, final_g))
    h = np.ascontiguousarray(x.reshape(TOK, D))
    for l in range(2):
        qk, v, fo = run_phase_a(h, norm_g[l], w_in[l])
        oo = run_phase_b(qk, v, fo, b_forget[l])
        del qk, v, fo
        consts = prep_c_consts(w_in[l], w_pool[l], pool_scale[l], w_conv[l], w_branch[l], b_merge[l], w_out[l], norm_g[l])
        h = run_phase_c(h, oo, consts)
        del consts, oo
    out = run_phase_d(h, final_g)
    return out.reshape(BATCH, SEQ, D).astype(np.float32)
```

```python
import numpy as np
from contextlib import ExitStack
import concourse.bass as bass
import concourse.mybir as mybir
from concourse.bass_utils import run_bass_kernel_spmd

F32 = mybir.dt.float32
BF16 = mybir.dt.bfloat16
AF = mybir.ActivationFunctionType
ALU = mybir.AluOpType
AX = mybir.AxisListType

NCORES = 8
D = 4096
NCH = D // 128
SEQ = 8192
BATCH = 2
TOK = BATCH * SEQ
TPC = TOK // NCORES
TT = 512
W = 1024
NH = 8
EPS = 1e-6
SCALE = 128 ** -0.5
NEG = -60000.0


class KB:
    def __init__(self, nc, es):
        self.nc = nc
        self.es = es
        self.eng = {"pe": nc.tensor, "act": nc.scalar, "dve": nc.vector, "pool": nc.gpsimd, "sp": nc.sync}
        self.psem = {}
        self.pcnt = {}
        for e in ("pe", "act", "dve", "pool"):
            self.psem[e] = es.enter_context(nc.semaphore("prog_" + e))
            self.pcnt[e] = 0
        self.waited = {}
        self.nsem = 0

    def sb(self, name, shape, dt):
        return self.es.enter_context(self.nc.sbuf_tensor(name, shape, dt))

    def ps(self, name, shape, dt):
        return self.es.enter_context(self.nc.psum_tensor(name, shape, dt))

    def sem(self, name):
        self.nsem += 1
        return self.es.enter_context(self.nc.semaphore(name))

    def mark(self, instr, e):
        self.pcnt[e] += 1
        instr.then_inc(self.psem[e], 1)
        return (self.psem[e], self.pcnt[e], e)

    def wait(self, e, *toks):
        flat = []
        for tok in toks:
            if isinstance(tok, list):
                flat.extend(tok)
            else:
                flat.append(tok)
        for tok in flat:
            if tok is None:
                continue
            sem, val, src = tok
            key = (e, id(sem))
            if self.waited.get(key, 0) >= val:
                continue
            self.waited[key] = val
            self.eng[e].wait_ge(sem, val)


class DmaSem:
    def __init__(self, kb, name):
        self.sem = kb.sem(name)
        self.val = 0

    def inc(self, instr, n=1):
        self.val += 16
        instr.then_inc(self.sem, 16)
        return (self.sem, self.val, "dma")

    def tok(self):
        return (self.sem, self.val, "dma")


def emit_norm_tile(kb, x_rows, grep, xbuf, xn, ss, rs, hT, tpps, tok0, st):
    nc = kb.nc
    kb.wait("sp", st.get("xbuf_free"))
    ld = st["xsem"].inc(nc.sync.dma_start(out=xbuf[:], in_=x_rows))
    kb.wait("act", ld, st.get("xn_free"))
    t_sq = kb.mark(nc.scalar.activation(out=xn[:], in_=xbuf[:], func=AF.Square, accum_out=ss[:]), "act")
    kb.wait("act", t_sq)
    t_act = kb.mark(nc.scalar.activation(out=rs[:], in_=ss[:], func=AF.Sqrt, scale=1.0 / D, bias=st["eps"][:]), "act")
    kb.wait("dve", t_act)
    t_rc = kb.mark(nc.vector.reciprocal(out=rs[:], in_=rs[:]), "dve")
    kb.wait("dve", t_rc)
    t_xn = kb.mark(nc.vector.scalar_tensor_tensor(out=xn[:], in0=xbuf[:], scalar=rs[:, 0:1], in1=grep[:],
                                                  op0=ALU.mult, op1=ALU.mult), "dve")
    st["xbuf_free"] = t_xn
    kb.wait("pe", t_xn)
    last_pe = None
    for c8 in range(NCH // 8):
        slot = st["tp_i"] % len(tpps)
        st["tp_i"] += 1
        kb.wait("pe", st["tp_free"][slot])
        for j in range(8):
            c = c8 * 8 + j
            ins = nc.tensor.transpose(out=tpps[slot][:, j, :], in_=xn[:, c * 128:(c + 1) * 128], identity=st["ident"][:])
        t_pe = kb.mark(ins, "pe")
        last_pe = t_pe
        kb.wait("dve", t_pe, st.get("hT_free"))
        t_ev = kb.mark(nc.vector.tensor_copy(out=hT[:, c8 * 8:(c8 + 1) * 8, tok0:tok0 + 128], in_=tpps[slot][:, :, :]), "dve")
        st["tp_free"][slot] = t_ev
        st["hT_ready"] = t_ev
    st["xn_free"] = last_pe


def make_ident(kb, ident_f, ident):
    nc = kb.nc
    t = kb.mark(nc.gpsimd.memset(ident_f[:], 0.0), "pool")
    kb.wait("pool", t)
    t = kb.mark(nc.gpsimd.affine_select(out=ident_f[:], in_=ident_f[:], pattern=[[-1, 128]], compare_op=ALU.not_equal,
                                        fill=1.0, base=0, channel_multiplier=1), "pool")
    kb.wait("pool", t)
    return kb.mark(nc.gpsimd.tensor_copy(out=ident[:], in_=ident_f[:]), "pool")


def build_phase_a(TPC=TPC):
    nc = bass.Bass("TRN2", target_bir_lowering=False)
    x = nc.dram_tensor("x", [TPC, D], F32, kind="ExternalInput").ap()
    g = nc.dram_tensor("g", [128, D], F32, kind="ExternalInput").ap()
    wa = nc.dram_tensor("wa", [12, 128, NCH * 512], BF16, kind="ExternalInput").ap()
    wf = nc.dram_tensor("wf", [128, NCH * NH], BF16, kind="ExternalInput").ap()
    qk = nc.dram_tensor("qk", [4, NH, 128, TPC], F32, kind="ExternalOutput").ap()
    v = nc.dram_tensor("v", [2, TPC, W], F32, kind="ExternalOutput").ap()
    fo = nc.dram_tensor("fo", [NH, TPC], F32, kind="ExternalOutput").ap()
    with ExitStack() as es:
        kb = KB(nc, es)
        grep = kb.sb("grep", [128, D], F32)
        xbuf = kb.sb("xbuf", [128, D], F32)
        xn = kb.sb("xn", [128, D], BF16)
        ss = kb.sb("ss", [128, 1], F32)
        rs = kb.sb("rs", [128, 1], F32)
        epsT = kb.sb("epsT", [128, 1], F32)
        ident_f = kb.sb("ident_f", [128, 128], F32)
        ident = kb.sb("ident", [128, 128], BF16)
        hT = kb.sb("hT", [128, NCH, TT], BF16)
        NWA = 3
        wblk = [kb.sb(f"wblk{i}", [128, NCH, 512], BF16) for i in range(NWA)]
        wf_b = kb.sb("wf_b", [128, NCH, NH], BF16)
        NOST = 4
        ost = [kb.sb(f"ost{i}", [128, 512], F32) for i in range(NOST)]
        tpps = [kb.ps(f"tpps{i}", [128, 8, 128], BF16) for i in range(2)]
        NPS = 4
        mps = [kb.ps(f"mps{i}", [128, 512], F32) for i in range(NPS)]

        st = {"xsem": DmaSem(kb, "xsem"), "tp_i": 0, "tp_free": [None, None], "ident": ident, "eps": epsT}
        csem = DmaSem(kb, "csem")
        wl_sem = [DmaSem(kb, f"wlsem{i}") for i in range(NWA)]
        wblk_free = [None] * NWA
        ost_sem = [DmaSem(kb, f"ostsem{i}") for i in range(NOST)]
        mps_free = [None] * NPS
        fin = []

        t_id = make_ident(kb, ident_f, ident)
        t_eps = kb.mark(nc.vector.memset(epsT[:], EPS), "dve")
        c1 = csem.inc(nc.sync.dma_start(out=grep[:], in_=g[:, :]))
        c2 = csem.inc(nc.sync.dma_start(out=wf_b[:].rearrange("p c n -> p (c n)"), in_=wf[:, :]))
        kb.wait("dve", c2)
        kb.wait("pe", t_id, c2)
        kb.wait("act", t_eps)

        piece_i = 0
        blk_i = 0
        mps_i = 0
        ost_i = 0
        last_mm = None
        for tt in range(TPC // TT):
            st["hT_free"] = last_mm
            for s in range(TT // 128):
                r0 = tt * TT + s * 128
                emit_norm_tile(kb, x[r0:r0 + 128, :], grep, xbuf, xn, ss, rs, hT, tpps, s * 128, st)
            kb.wait("pe", st["hT_ready"])
            ms = mps_i % NPS
            mps_i += 1
            kb.wait("pe", mps_free[ms])
            for c in range(NCH):
                ins = nc.tensor.matmul(mps[ms][0:NH, :], lhsT=wf_b[:, c, :], rhs=hT[:, c, :], start=(c == 0), stop=(c == NCH - 1))
            t_mm = kb.mark(ins, "pe")
            os_ = ost_i % NOST
            ost_i += 1
            kb.wait("act", t_mm, ost_sem[os_].tok())
            t_ev = kb.mark(nc.scalar.copy(out=ost[os_][0:NH, :], in_=mps[ms][0:NH, :]), "act")
            mps_free[ms] = t_ev
            kb.wait("act", t_ev)
            ost_sem[os_].inc(nc.scalar.dma_start(out=fo[:, tt * TT:(tt + 1) * TT], in_=ost[os_][0:NH, :]))
            for blk in range(12):
                kind = blk // 2
                half = blk % 2
                wslot = blk_i % NWA
                blk_i += 1
                kb.wait("sp", wblk_free[wslot])
                t_cast = wl_sem[wslot].inc(nc.sync.dma_start(out=wblk[wslot][:].rearrange("p c n -> p (c n)"), in_=wa[blk, :, :]))
                kb.wait("pe", t_cast)
                for j in range(4):
                    ms = mps_i % NPS
                    mps_i += 1
                    kb.wait("pe", mps_free[ms])
                    for c in range(NCH):
                        if kind in (2, 5):
                            ins = nc.tensor.matmul(mps[ms][:], lhsT=hT[:, c, j * 128:(j + 1) * 128], rhs=wblk[wslot][:, c, :],
                                                   start=(c == 0), stop=(c == NCH - 1))
                        else:
                            ins = nc.tensor.matmul(mps[ms][:], lhsT=wblk[wslot][:, c, j * 128:(j + 1) * 128], rhs=hT[:, c, :],
                                                   start=(c == 0), stop=(c == NCH - 1))
                    t_mm = kb.mark(ins, "pe")
                    last_mm = t_mm
                    os_ = ost_i % NOST
                    ost_i += 1
                    kb.wait("act", t_mm, ost_sem[os_].tok())
                    t_ev = kb.mark(nc.scalar.copy(out=ost[os_][:], in_=mps[ms][:]), "act")
                    mps_free[ms] = t_ev
                    if kind in (2, 5):
                        dst = v[kind // 3, tt * TT + j * 128: tt * TT + (j + 1) * 128, half * 512:(half + 1) * 512]
                    else:
                        which = {0: 0, 1: 1, 3: 2, 4: 3}[kind]
                        dst = qk[which, half * 4 + j, :, tt * TT:(tt + 1) * TT]
                    kb.wait("act", t_ev)
                    ost_sem[os_].inc(nc.scalar.dma_start(out=dst, in_=ost[os_][:]))
                wblk_free[wslot] = last_mm
        for s_ in ost_sem:
            kb.wait("act", s_.tok())
    return nc


_CACHE = {}


def get_prog(name):
    if name not in _CACHE:
        _CACHE[name] = {"a": build_phase_a, "b": build_phase_b, "c": build_phase_c, "d": build_phase_d}[name]()
    return _CACHE[name]


def prep_a_f32(w_in):
    cols = np.concatenate([np.arange(0, 3 * W), np.arange(4 * W, 7 * W)])
    wa = w_in[:, cols].reshape(NCH, 128, 12, 512).transpose(2, 1, 0, 3).reshape(12, 128, NCH * 512)
    wf = w_in[:, 8 * W:8 * W + NH].reshape(NCH, 128, NH).transpose(1, 0, 2).reshape(128, NCH * NH)
    return {"wa": np.ascontiguousarray(wa), "wf": np.ascontiguousarray(wf)}


def run_phase_a(x_flat, g, wq):
    nc = get_prog("a")
    grep = np.ascontiguousarray(np.broadcast_to(g[None, :], (128, D)))
    in_maps = [{"x": np.ascontiguousarray(x_flat[c * TPC:(c + 1) * TPC]), "g": grep, "wa": wq["wa"], "wf": wq["wf"]} for c in range(NCORES)]
    res = run_bass_kernel_spmd(nc, in_maps, core_ids=list(range(NCORES)))
    qk = np.concatenate([r["qk"] for r in res.results], axis=3)
    v = np.concatenate([r["v"] for r in res.results], axis=1)
    fo = np.concatenate([r["fo"] for r in res.results], axis=1)
    return qk, v, fo


def build_phase_b(SEQ=SEQ, NB=BATCH):
    nc = bass.Bass("TRN2", target_bir_lowering=False)
    NKC = SEQ // 128
    NQT = SEQ // 512
    NBLK = SEQ // 256
    NPC = SEQ // 2048
    qk = nc.dram_tensor("qk", [NB, 4, 128, SEQ], F32, kind="ExternalInput").ap()
    vv = nc.dram_tensor("vv", [NB, 2, SEQ, 128], F32, kind="ExternalInput").ap()
    ff = nc.dram_tensor("ff", [NB, 1, SEQ], F32, kind="ExternalInput").ap()
    bf = nc.dram_tensor("bf", [1, 1], F32, kind="ExternalInput").ap()
    oo = nc.dram_tensor("oo", [NB, 2, 128, SEQ], F32, kind="ExternalOutput").ap()
    with ExitStack() as es:
        kb = KB(nc, es)
        qT = kb.sb("qT", [128, SEQ], BF16)
        kT = kb.sb("kT", [128, SEQ], BF16)
        vS = kb.sb("vS", [128, NKC, 128], BF16)
        crep = kb.sb("crep", [128, SEQ], F32)
        negc = kb.sb("negc", [128, NKC], F32)
        NSTG = 2
        stg = [kb.sb(f"stg{i}", [128, 2048], F32) for i in range(NSTG)]
        biasT = kb.sb("biasT", [32, SEQ], BF16)
        gate_all = kb.sb("gate_all", [128, SEQ // 128, 32], F32)
        ksum = kb.sb("ksum", [128, 32], F32)
        gate_m = kb.sb("gate_m", [128, 2, 32], F32)
        top8 = kb.sb("top8", [128, 2, 8], F32)
        selb = kb.sb("selb", [128, 2, 32], F32)
        E_all = kb.sb("E_all", [32, 32, 128], BF16)
        ident_f = kb.sb("ident_f", [128, 128], F32)
        ident = kb.sb("ident", [128, 128], BF16)
        tri_f = kb.sb("tri_f", [128, 128], F32)
        tri = kb.sb("tri", [128, 128], BF16)
        ones_b = kb.sb("ones_b", [128, 128], BF16)
        onesrow = kb.sb("onesrow", [1, 128], F32)
        negone = kb.sb("negone", [1, 2], F32)
        one1 = kb.sb("one1", [1, 1], F32)
        negb = kb.sb("negb", [1, 1], F32)
        lrow = kb.sb("lrow", [1, 2048], F32)
        carry = kb.sb("carry", [1, 1], F32)
        NPT = 4
        PT = [kb.sb(f"PT{i}", [128, 512], BF16) for i in range(NPT)]
        tmpS = [kb.sb(f"tmpS{i}", [128, 512], F32) for i in range(NPT)]
        rden = kb.sb("rden", [128, 512], F32)
        NOS = 2
        oS = [kb.sb(f"oS{i}", [128, 512], F32) for i in range(NOS)]
        Sps = [kb.ps(f"Sps{i}", [128, 512], F32) for i in range(NPT)]
        accps = kb.ps("accps", [128, 512], F32)
        denps = kb.ps("denps", [128, 512], F32)
        mscps = [kb.ps(f"mscps{i}", [128, 512], F32) for i in range(2)]

        ldsem = [DmaSem(kb, f"ldsem{i}") for i in range(NSTG)]
        csem = DmaSem(kb, "csem")
        osem = [DmaSem(kb, f"osem{i}") for i in range(NOS)]
        stg_free = [None] * NSTG
        st = {"stg_i": 0, "pt_i": 0, "os_i": 0, "msc_i": 0}
        S_free = [None] * NPT
        PT_free = [None] * NPT
        tmp_free = [None] * NPT
        msc_free = [None, None]
        last = {"pe": None, "act": None, "dve": None, "pool": None}

        def chain(e, ins, *deps):
            kb.wait(e, last[e], *deps)
            t = kb.mark(ins(), e)
            last[e] = t
            return t

        t = chain("pool", lambda: nc.gpsimd.memset(ident_f[:], 0.0))
        t = chain("pool", lambda: nc.gpsimd.affine_select(out=ident_f[:], in_=ident_f[:], pattern=[[-1, 128]],
                                                          compare_op=ALU.not_equal, fill=1.0, base=0, channel_multiplier=1))
        t = chain("pool", lambda: nc.gpsimd.tensor_copy(out=ident[:], in_=ident_f[:]))
        t = chain("pool", lambda: nc.gpsimd.memset(tri_f[:], 0.0))
        t = chain("pool", lambda: nc.gpsimd.affine_select(out=tri_f[:], in_=tri_f[:], pattern=[[1, 128]],
                                                          compare_op=ALU.is_ge, fill=NEG, base=0, channel_multiplier=-1))
        t = chain("pool", lambda: nc.gpsimd.tensor_copy(out=tri[:], in_=tri_f[:]))
        t = chain("pool", lambda: nc.gpsimd.memset(E_all[:], 0.0))
        t = chain("pool", lambda: nc.gpsimd.affine_select(out=E_all[:], in_=E_all[:], pattern=[[-1, 32], [0, 128]],
                                                          compare_op=ALU.not_equal, fill=1.0, base=0, channel_multiplier=1))
        t = chain("pool", lambda: nc.gpsimd.memset(ones_b[:], 1.0))
        t = chain("pool", lambda: nc.gpsimd.memset(onesrow[:], 1.0))
        t = chain("pool", lambda: nc.gpsimd.memset(negone[:], -1.0))
        t = chain("pool", lambda: nc.gpsimd.memset(one1[:], 1.0))
        t_const = chain("pool", lambda: nc.gpsimd.memset(carry[:], 0.0))
        cb = csem.inc(nc.sync.dma_start(out=negb[:], in_=bf[:, :]))
        t_negb = chain("dve", lambda: nc.vector.tensor_scalar(out=negb[:], in0=negb[:], scalar1=-1.0, scalar2=None, op0=ALU.mult), cb)
        chain("dve", lambda: nc.vector.memset(ksum[:], 0.0))
        chain("dve", lambda: nc.vector.memset(gate_all[:], 0.0))
        kb.wait("pe", t_const)
        kb.wait("act", t_const)
        kb.wait("dve", t_const)

        def load_piece(src_ap, shape_view=None):
            s_ = st["stg_i"] % NSTG
            st["stg_i"] += 1
            kb.wait("sp", stg_free[s_])
            dst = stg[s_][:] if shape_view is None else shape_view(stg[s_])
            tok = ldsem[s_].inc(nc.sync.dma_start(out=dst, in_=src_ap))
            return s_, tok

        def attention(b, which, attn_done):
            moba = (which == 0)
            t_k = None
            for p in range(NPC):
                s_, tok = load_piece(qk[b, 2 * which + 1, :, p * 2048:(p + 1) * 2048])
                kb.wait("pool", tok, attn_done)
                t_k = kb.mark(nc.gpsimd.tensor_copy(out=kT[:, p * 2048:(p + 1) * 2048], in_=stg[s_][:]), "pool")
                fr = [t_k]
                if moba:
                    t_ks = chain("dve", lambda: nc.vector.tensor_reduce(
                        out=ksum[:, p * 8:(p + 1) * 8], in_=stg[s_][:].rearrange("p (n t) -> p n t", t=256), axis=AX.X, op=ALU.add),
                        tok, attn_done)
                    fr.append(t_ks)
                stg_free[s_] = fr
            t_v = None
            for p in range(NPC):
                s_, tok = load_piece(vv[b, which, p * 2048:(p + 1) * 2048, :].rearrange("(c p) d -> p c d", p=128),
                                     lambda tl: tl[:].rearrange("p (c d) -> p c d", d=128))
                kb.wait("pool", tok, attn_done)
                t_v = kb.mark(nc.gpsimd.tensor_copy(out=vS[:, p * 16:(p + 1) * 16, :],
                                                   in_=stg[s_][:].rearrange("p (c d) -> p c d", d=128)), "pool")
                stg_free[s_] = [t_v]
            t_q = None
            t_gate = None
            for p in range(NPC):
                s_, tok = load_piece(qk[b, 2 * which, :, p * 2048:(p + 1) * 2048])
                kb.wait("pool", tok, attn_done)
                t_q = kb.mark(nc.gpsimd.tensor_copy(out=qT[:, p * 2048:(p + 1) * 2048], in_=stg[s_][:]), "pool")
                fr = [t_q]
                if moba:
                    m_ = st["msc_i"] % 2
                    st["msc_i"] += 1
                    kb.wait("pe", tok, last["dve"], msc_free[m_])
                    gv = mscps[m_][:].rearrange("p (t n) -> p t n", n=32)
                    for j in range(16):
                        ins = nc.tensor.matmul(gv[:, j, :], lhsT=stg[s_][:, j * 128:(j + 1) * 128], rhs=ksum[:, :], start=True, stop=True)
                    t_g = kb.mark(ins, "pe")
                    last["pe"] = t_g
                    fr.append(t_g)
                    t_gate = chain("dve", lambda: nc.vector.tensor_copy(out=gate_all[:, p * 16:(p + 1) * 16, :], in_=gv), t_g)
                    msc_free[m_] = t_gate
                stg_free[s_] = fr
            t_bias = None
            if moba:
                for qb in range(NBLK):
                    chain("dve", lambda: nc.vector.memset(gate_m[:], -1e30))
                    if qb > 0:
                        chain("dve", lambda: nc.vector.tensor_copy(out=gate_m[:, :, 0:qb], in_=gate_all[:, 2 * qb:2 * qb + 2, 0:qb]))
                    for j in range(2):
                        chain("dve", lambda: nc.vector.max(out=top8[:, j, :], in_=gate_m[:, j, :]))
                        chain("dve", lambda: nc.vector.tensor_scalar(out=selb[:, j, :], in0=gate_m[:, j, :], scalar1=top8[:, j, 2:3],
                                                                    scalar2=None, op0=ALU.is_ge))
                    chain("dve", lambda: nc.vector.tensor_scalar(out=selb[:], in0=selb[:], scalar1=-1.0, scalar2=-NEG,
                                                                op0=ALU.add, op1=ALU.mult))
                    if qb + 1 < 32:
                        chain("dve", lambda: nc.vector.memset(selb[:, :, qb + 1:32], NEG))
                    t_sel = chain("dve", lambda: nc.vector.memset(selb[:, :, qb:qb + 1], 0.0))
                    m_ = st["msc_i"] % 2
                    st["msc_i"] += 1
                    kb.wait("pe", t_sel, msc_free[m_])
                    for j in range(2):
                        ins = nc.tensor.transpose(out=mscps[m_][0:32, j * 128:(j + 1) * 128], in_=selb[:, j, :], identity=ident_f[:])
                    t_tr = kb.mark(ins, "pe")
                    last["pe"] = t_tr
                    t_bias = chain("dve", lambda: nc.vector.tensor_copy(out=biasT[:, qb * 256:(qb + 1) * 256], in_=mscps[m_][0:32, 0:256]),
                                   t_tr, attn_done)
                    msc_free[m_] = t_bias
            else:
                for p in range(NPC):
                    s_, tok = load_piece(ff[b, :, p * 2048:(p + 1) * 2048], lambda tl: tl[0:1, :])
                    fr_ = stg[s_][0:1, :]
                    chain("act", lambda: nc.scalar.activation(out=fr_, in_=fr_, func=AF.Exp, scale=-1.0, bias=negb[:]), tok, t_negb)
                    t_l = chain("act", lambda: nc.scalar.activation(out=fr_, in_=fr_, func=AF.Ln, scale=1.0, bias=one1[:]))
                    chain("dve", lambda: nc.vector.tensor_scalar(out=fr_, in0=fr_, scalar1=-1.0 / SCALE, scalar2=None, op0=ALU.mult), t_l)
                    init = 0.0 if p == 0 else carry[:, 0:1]
                    kb.wait("dve", last["pe"])
                    t_sc = chain("dve", lambda: nc.vector.tensor_tensor_scan(out=lrow[:], data0=one1[:, 0:1].to_broadcast([1, 2048]), data1=fr_,
                                                                           initial=init, op0=ALU.mult, op1=ALU.add))
                    stg_free[s_] = [t_sc]
                    t_sc = chain("dve", lambda: nc.vector.tensor_copy(out=carry[:], in_=lrow[:, 2047:2048]))
                    for i in range(4):
                        m_ = st["msc_i"] % 2
                        st["msc_i"] += 1
                        kb.wait("pe", t_sc, msc_free[m_])
                        t_mm = kb.mark(nc.tensor.matmul(mscps[m_][:], lhsT=onesrow[:], rhs=lrow[:, i * 512:(i + 1) * 512], start=True, stop=True), "pe")
                        last["pe"] = t_mm
                        t_cr = chain("dve", lambda: nc.vector.tensor_copy(out=crep[:, p * 2048 + i * 512:p * 2048 + (i + 1) * 512], in_=mscps[m_][:]),
                                     t_mm, attn_done)
                        msc_free[m_] = t_cr
                    m_ = st["msc_i"] % 2
                    st["msc_i"] += 1
                    kb.wait("pe", t_sc, msc_free[m_])
                    for kc in range(16):
                        ins = nc.tensor.matmul(mscps[m_][:, 2 * kc:2 * kc + 2], lhsT=lrow[:, kc * 128:(kc + 1) * 128], rhs=negone[:], start=True, stop=True)
                    t_mm = kb.mark(ins, "pe")
                    last["pe"] = t_mm
                    t_bias = chain("dve", lambda: nc.vector.tensor_copy(
                        out=negc[:, p * 16:(p + 1) * 16], in_=mscps[m_][:, 0:32].rearrange("p (c two) -> p c two", two=2)[:, :, 0]), t_mm, attn_done)
                    msc_free[m_] = t_bias

            kb.wait("pe", t_k, t_v, t_q, t_bias)
            units = [(Q, kc) for Q in range(NQT) for kc in range(4 * Q + 4)]
            res = {"last_pv": None}

            def emit_qk(Q, kc):
                q0 = Q * 512
                c0 = max(0, kc * 128 - q0)
                diag = kc * 128 >= q0
                n = kc // 2
                sl = st["pt_i"] % NPT
                st["pt_i"] += 1
                kb.wait("pe", S_free[sl])
                ins = nc.tensor.matmul(Sps[sl][:, c0:512], lhsT=kT[:, kc * 128:(kc + 1) * 128], rhs=qT[:, q0 + c0:q0 + 512],
                                       start=True, stop=not (moba or diag))
                if moba:
                    ins = nc.tensor.matmul(Sps[sl][:, c0:512], lhsT=E_all[:, n, :], rhs=biasT[:, q0 + c0:q0 + 512],
                                           start=False, stop=not diag)
                if diag:
                    ins = nc.tensor.matmul(Sps[sl][:, c0:c0 + 128], lhsT=ident[:], rhs=tri[:], start=False, stop=True)
                t_s = kb.mark(ins, "pe")
                if moba:
                    kb.wait("act", t_s, PT_free[sl])
                    t_p = kb.mark(nc.scalar.activation(out=PT[sl][:, c0:512], in_=Sps[sl][:, c0:512], func=AF.Exp, scale=SCALE), "act")
                    S_free[sl] = t_p
                else:
                    kb.wait("dve", t_s, tmp_free[sl])
                    t_t = kb.mark(nc.vector.scalar_tensor_tensor(out=tmpS[sl][:, c0:512], in0=Sps[sl][:, c0:512], scalar=negc[:, kc:kc + 1],
                                                                 in1=crep[:, q0 + c0:q0 + 512], op0=ALU.add, op1=ALU.add), "dve")
                    S_free[sl] = t_t
                    kb.wait("act", t_t, PT_free[sl])
                    t_p = kb.mark(nc.scalar.activation(out=PT[sl][:, c0:512], in_=tmpS[sl][:, c0:512], func=AF.Exp, scale=SCALE), "act")
                    tmp_free[sl] = t_p
                return (Q, kc, sl, c0, t_p)

            def emit_pv(info):
                Q, kc, sl, c0, t_p = info
                q0 = Q * 512
                nkc = 4 * Q + 4
                first = (kc == 0)
                kb.wait("pe", t_p)
                if first:
                    kb.wait("pe", st.get("acc_free"))
                nc.tensor.matmul(accps[:, c0:512], lhsT=vS[:, kc, :], rhs=PT[sl][:, c0:512], start=first, stop=(kc == nkc - 1))
                ins = nc.tensor.matmul(denps[:, c0:512], lhsT=ones_b[:], rhs=PT[sl][:, c0:512], start=first, stop=(kc == nkc - 1))
                t_pv = kb.mark(ins, "pe")
                PT_free[sl] = t_pv
                res["last_pv"] = t_pv
                if kc == nkc - 1:
                    t_r = chain("dve", lambda: nc.vector.reciprocal(out=rden[:], in_=denps[:]), t_pv)
                    o_ = st["os_i"] % NOS
                    st["os_i"] += 1
                    t_o = chain("dve", lambda: nc.vector.tensor_tensor(out=oS[o_][:], in0=accps[:], in1=rden[:], op=ALU.mult), osem[o_].tok())
                    st["acc_free"] = t_o
                    kb.wait("sp", t_o)
                    osem[o_].inc(nc.sync.dma_start(out=oo[b, which, :, q0:q0 + 512], in_=oS[o_][:]))

            LAG = 2
            pend = []
            for (Q, kc) in units:
                pend.append(emit_qk(Q, kc))
                if len(pend) > LAG:
                    emit_pv(pend.pop(0))
            while pend:
                emit_pv(pend.pop(0))
            t_last_pv = res["last_pv"]
            return t_last_pv

        done = None
        for b in range(NB):
            for which in range(2):
                done = attention(b, which, done)
        for s_ in osem:
            kb.wait("sp", s_.tok())
    return nc


def run_phase_b(qk, v, fo, b_forget):
    nc = get_prog("b")
    in_maps = []
    for h in range(NCORES):
        qk_h = np.ascontiguousarray(qk[:, h].reshape(4, 128, BATCH, SEQ).transpose(2, 0, 1, 3))
        v_h = np.ascontiguousarray(v[:, :, h * 128:(h + 1) * 128].reshape(2, BATCH, SEQ, 128).transpose(1, 0, 2, 3))
        f_h = np.ascontiguousarray(fo[h].reshape(BATCH, 1, SEQ))
        in_maps.append({"qk": qk_h, "vv": v_h, "ff": f_h, "bf": np.ascontiguousarray(b_forget[h].reshape(1, 1))})
    res = run_bass_kernel_spmd(nc, in_maps, core_ids=list(range(NCORES)))
    return np.stack([r["oo"] for r in res.results], axis=0)


HAL = 16


def build_phase_c(TPC=TPC, T=512):
    nc = bass.Bass("TRN2", target_bir_lowering=False)
    NT = TPC // T
    x = nc.dram_tensor("x", [TPC, D], F32, kind="ExternalInput").ap()
    xh = nc.dram_tensor("xh", [128, D], F32, kind="ExternalInput").ap()
    gcol = nc.dram_tensor("gcol", [128, NCH], F32, kind="ExternalInput").ap()
    NBT = 32 + 80 + 16
    wq = nc.dram_tensor("wq", [NBT, 128, NCH * 256], BF16, kind="ExternalInput").ap()
    oa = nc.dram_tensor("oa", [2, W, TPC], F32, kind="ExternalInput").ap()
    icnt = nc.dram_tensor("icnt", [128, 4, TPC], F32, kind="ExternalInput").ap()
    wpool = nc.dram_tensor("wpool", [128, 8, 256], F32, kind="ExternalInput").ap()
    pscale = nc.dram_tensor("pscale", [128, 8], F32, kind="ExternalInput").ap()
    wconv = nc.dram_tensor("wconv", [128, 3, 8], F32, kind="ExternalInput").ap()
    bmerge = nc.dram_tensor("bmerge", [128, 4, NCH], F32, kind="ExternalInput").ap()
    y = nc.dram_tensor("y", [TPC, D], F32, kind="ExternalOutput").ap()
    TE = T + HAL
    with ExitStack() as es:
        kb = KB(nc, es)
        hT = kb.sb("hT", [128, NCH, TE], BF16)
        brT = kb.sb("brT", [128, NCH, T], BF16)
        assert NCH * T >= 12288
        mg_raw = kb.sb("mg_raw", [128, NCH * T], BF16)
        mgT = mg_raw[:].rearrange("p (c t) -> p c t", t=T)
        xbuf = mg_raw[:, 0:8192].bitcast(F32)
        xn = mg_raw[:, 8192:12288]
        NW = 2
        wblk = [kb.sb(f"wblk{i}", [128, NCH, 256], BF16) for i in range(NW)]
        wpool_f = kb.sb("wpool_f", [128, 8, 256], F32)
        ss = kb.sb("ss", [128, 1], F32)
        rs = kb.sb("rs", [128, 1], F32)
        epsT = kb.sb("epsT", [128, 1], F32)
        gcolS = kb.sb("gcolS", [128, NCH], F32)
        ident_f = kb.sb("ident_f", [128, 128], F32)
        ident = kb.sb("ident", [128, 128], BF16)
        icS = kb.sb("icS", [128, 4, T], F32)
        wpool_b = kb.sb("wpool_b", [128, 8, 256], BF16)
        pscS = kb.sb("pscS", [128, 8], F32)
        wcvS = kb.sb("wcvS", [128, 3, 8], F32)
        bmS = kb.sb("bmS", [128, 4, NCH], F32)
        NTF = 2
        tmpf = [kb.sb(f"tmpf{i}", [128, T], F32) for i in range(NTF)]
        otile = [kb.sb(f"otile{i}", [128, T], F32) for i in range(2)]
        uext = kb.sb("uext", [128, 2, TE], F32)
        pa = kb.sb("pa", [128, 2, TE], F32)
        pb = kb.sb("pb", [128, 2, TE], F32)
        pooledT = kb.sb("pooledT", [128, 2, T], BF16)
        yS = kb.sb("yS", [128, 2, T], F32)
        bS = kb.sb("bS", [128, T], F32)
        cext = kb.sb("cext", [128, TE], F32)
        zext = kb.sb("zext", [128, TE], F32)
        ycv = kb.sb("ycv", [128, T], F32)
        sgT = kb.sb("sgT", [128, 4, 2, T], BF16)
        accm = kb.sb("accm", [128, T], F32)
        xres = [kb.sb(f"xres{i}", [128, 256], F32) for i in range(2)]
        orow = [kb.sb(f"orow{i}", [128, 256], F32) for i in range(2)]
        tpps = [kb.ps(f"tpps{i}", [128, 8, 128], BF16) for i in range(2)]
        NPS = 4
        mps = [kb.ps(f"mps{i}", [128, 512], F32) for i in range(NPS)]
        hps = kb.ps("hps", [128, 512], F32)

        last = {"pe": None, "act": None, "dve": None, "pool": None}

        def chain(e, ins, *deps):
            kb.wait(e, last[e], *deps)
            t = kb.mark(ins(), e)
            last[e] = t
            return t

        def pe_mark(ins):
            t = kb.mark(ins, "pe")
            last["pe"] = t
            return t

        csem = DmaSem(kb, "csem")
        xsem = DmaSem(kb, "xsem")
        wblk_free = [None] * NW
        wl_sem = [DmaSem(kb, f"wlsem{i}") for i in range(NW)]
        osem = [DmaSem(kb, f"osem{i}") for i in range(2)]
        otile_free = [None, None]
        xrsem = [DmaSem(kb, f"xrsem{i}") for i in range(2)]
        xres_free = [None, None]
        orsem = [DmaSem(kb, f"orsem{i}") for i in range(2)]
        icsem = DmaSem(kb, "icsem")
        mps_free = [None] * NPS
        hps_free = [None]
        tp_free = [None, None]
        cnt = {"piece": 0, "blk": 0, "mps": 0, "tp": 0, "tf": 0, "ot": 0, "xr": 0}

        t = chain("pool", lambda: nc.gpsimd.memset(ident_f[:], 0.0))
        t = chain("pool", lambda: nc.gpsimd.affine_select(out=ident_f[:], in_=ident_f[:], pattern=[[-1, 128]],
                                                          compare_op=ALU.not_equal, fill=1.0, base=0, channel_multiplier=1))
        t_id = chain("pool", lambda: nc.gpsimd.tensor_copy(out=ident[:], in_=ident_f[:]))
        c_all = None
        for dst, src in ((gcolS, gcol), (wpool_f, wpool), (pscS, pscale), (wcvS, wconv), (bmS, bmerge)):
            c_all = csem.inc(nc.sync.dma_start(out=dst[:], in_=src))
        chain("dve", lambda: nc.vector.memset(epsT[:], EPS))
        t_c = chain("dve", lambda: nc.vector.tensor_copy(out=wpool_b[:], in_=wpool_f[:]), c_all)
        kb.wait("pe", t_id, t_c)
        kb.wait("act", t_c)

        def norm_tile(x_rows, dst_col0, src_c0, ncols):
            kb.wait("sp", last["dve"], last["pe"])
            ld = xsem.inc(nc.sync.dma_start(out=xbuf[:], in_=x_rows))
            chain("act", lambda: nc.scalar.activation(out=xn[:], in_=xbuf[:], func=AF.Square, accum_out=ss[:]), ld, last["pe"], last["dve"])
            t_a = chain("act", lambda: nc.scalar.activation(out=rs[:], in_=ss[:], func=AF.Sqrt, scale=1.0 / D, bias=epsT[:]))
            chain("dve", lambda: nc.vector.reciprocal(out=rs[:], in_=rs[:]), t_a)
            t_xn = chain("dve", lambda: nc.vector.tensor_scalar(out=xn[:], in0=xbuf[:], scalar1=rs[:, 0:1], scalar2=None, op0=ALU.mult))
            for c8 in range(NCH // 8):
                sl = cnt["tp"] % 2
                cnt["tp"] += 1
                kb.wait("pe", t_xn, tp_free[sl])
                for j in range(8):
                    c = c8 * 8 + j
                    ins = nc.tensor.transpose(out=tpps[sl][:, j, :], in_=xn[:, c * 128:(c + 1) * 128], identity=ident[:])
                t_pe = pe_mark(ins)
                for j in range(8):
                    c = c8 * 8 + j
                    t_ev = chain("dve", lambda: nc.vector.tensor_scalar(out=hT[:, c, dst_col0:dst_col0 + ncols],
                                                                       in0=tpps[sl][:, j, src_c0:src_c0 + ncols],
                                                                       scalar1=gcolS[:, c:c + 1], scalar2=None, op0=ALU.mult), t_pe)
                tp_free[sl] = t_ev

        def load_block(view=None, col0=None):
            slot = cnt["blk"] % NW
            bi = cnt["blk"] % NBT
            cnt["blk"] += 1
            kb.wait("sp", wblk_free[slot])
            ld = wl_sem[slot].inc(nc.sync.dma_start(out=wblk[slot][:].rearrange("p c n -> p (c n)"), in_=wq[bi, :, :]))
            kb.wait("pe", ld)
            return slot

        def release(slot):
            wblk_free[slot] = last["pe"]

        def mm_feat(slot, j, halo=False):
            ms = cnt["mps"] % NPS
            cnt["mps"] += 1
            kb.wait("pe", mps_free[ms])
            for c in range(NCH):
                ins = nc.tensor.matmul(mps[ms][:, 0:T], lhsT=wblk[slot][:, c, j * 128:(j + 1) * 128], rhs=hT[:, c, HAL:TE],
                                       start=(c == 0), stop=(c == NCH - 1))
            if halo:
                kb.wait("pe", hps_free[0])
                for c in range(NCH):
                    ins = nc.tensor.matmul(hps[:, 0:HAL], lhsT=wblk[slot][:, c, j * 128:(j + 1) * 128], rhs=hT[:, c, 0:HAL],
                                           start=(c == 0), stop=(c == NCH - 1))
            return ms, pe_mark(ins)

        def get_tmpf():
            i = cnt["tf"] % NTF
            cnt["tf"] += 1
            return tmpf[i]

        pending = []

        for tt in range(NT):
            t0 = tt * T
            if tt == 0:
                norm_tile(xh[:, :], 0, 128 - HAL, HAL)
            else:
                chain("dve", lambda: nc.vector.tensor_copy(out=hT[:, :, 0:HAL], in_=hT[:, :, T:TE]), last["pe"])
            for s in range(T // 128):
                norm_tile(x[t0 + s * 128:t0 + (s + 1) * 128, :], HAL + s * 128, 0, 128)
            kb.wait("pe", last["dve"])
            kb.wait("sp", last["dve"])
            t_ic = icsem.inc(nc.sync.dma_start(out=icS[:], in_=icnt[:, :, t0:t0 + T]))
            kb.wait("dve", t_ic)

            for blk in range(32):
                slot = load_block()
                if blk < 8:
                    br = blk // 4
                    for j in range(2):
                        ch = 2 * (blk % 4) + j
                        ms, t_mm = mm_feat(slot, j)
                        tf = get_tmpf()
                        t_a = chain("act", lambda: nc.scalar.activation(out=tf[:], in_=mps[ms][:, 0:T], func=AF.Silu), t_mm, last["dve"])
                        mps_free[ms] = t_a
                        oi = cnt["ot"] % 2
                        cnt["ot"] += 1
                        kb.wait("sp", otile_free[oi])
                        t_o = osem[oi].inc(nc.sync.dma_start(out=otile[oi][:], in_=oa[br, ch * 128:(ch + 1) * 128, t0:t0 + T]))
                        t_d = chain("dve", lambda: nc.vector.tensor_tensor(out=brT[:, br * 8 + ch, :], in0=tf[:], in1=otile[oi][:], op=ALU.mult), t_a, t_o)
                        otile_free[oi] = t_d
                elif blk < 16:
                    g = (blk - 8) // 2
                    if (blk - 8) % 2 == 0:
                        for j in range(2):
                            ms, t_mm = mm_feat(slot, j, halo=True)
                            chain("act", lambda: nc.scalar.copy(out=uext[:, j, HAL:TE], in_=mps[ms][:, 0:T]), t_mm, last["dve"])
                            t_a = chain("act", lambda: nc.scalar.copy(out=uext[:, j, 0:HAL], in_=hps[:, 0:HAL]))
                            mps_free[ms] = t_a
                            hps_free[0] = t_a
                        src = uext
                        bufs = [pa, pb]
                        for k in range(g + 1):
                            sh = 2 ** k
                            dst = bufs[k % 2]
                            lo = 2 * sh - 1
                            chain("dve", lambda: nc.vector.tensor_tensor(out=dst[:, :, lo:TE], in0=src[:, :, lo:TE], in1=src[:, :, lo - sh:TE - sh], op=ALU.add), last["act"])
                            src = dst
                        for j in range(2):
                            tf = get_tmpf()
                            chain("dve", lambda: nc.vector.tensor_tensor(out=tf[:], in0=src[:, j, HAL:TE], in1=icS[:, g, :], op=ALU.mult), last["act"])
                            t_p = chain("dve", lambda: nc.vector.tensor_tensor(out=pooledT[:, j, :], in0=tf[:], in1=uext[:, j, HAL:TE], op=ALU.subtract), last["pe"])
                        for oc in range(2):
                            ms = cnt["mps"] % NPS
                            cnt["mps"] += 1
                            kb.wait("pe", mps_free[ms], t_p)
                            for j in range(2):
                                ins = nc.tensor.matmul(mps[ms][:, 0:T], lhsT=wpool_b[:, 2 * g + j, oc * 128:(oc + 1) * 128], rhs=pooledT[:, j, :],
                                                       start=(j == 0), stop=(j == 1))
                            t_mm = pe_mark(ins)
                            t_y = chain("dve", lambda: nc.vector.tensor_scalar(out=yS[:, oc, :], in0=mps[ms][:, 0:T], scalar1=pscS[:, 2 * g + oc:2 * g + oc + 1],
                                                                              scalar2=None, op0=ALU.mult), t_mm)
                            mps_free[ms] = t_y
                    else:
                        for j in range(2):
                            ms, t_mm = mm_feat(slot, j)
                            tf = get_tmpf()
                            t_a = chain("act", lambda: nc.scalar.activation(out=tf[:], in_=mps[ms][:, 0:T], func=AF.Silu), t_mm, last["dve"])
                            mps_free[ms] = t_a
                            chain("dve", lambda: nc.vector.tensor_tensor(out=brT[:, 16 + 2 * g + j, :], in0=tf[:], in1=yS[:, j, :], op=ALU.mult), t_a)
                else:
                    i = (blk - 16) // 2
                    if (blk - 16) % 2 == 0:
                        ms, t_mm = mm_feat(slot, 0)
                        t_a = chain("act", lambda: nc.scalar.copy(out=bS[:], in_=mps[ms][:, 0:T]), t_mm, last["dve"])
                        mps_free[ms] = t_a
                        ms, t_mm = mm_feat(slot, 1, halo=True)
                        chain("act", lambda: nc.scalar.copy(out=cext[:, HAL:TE], in_=mps[ms][:, 0:T]), t_mm, last["dve"])
                        t_a = chain("act", lambda: nc.scalar.copy(out=cext[:, 0:HAL], in_=hps[:, 0:HAL]))
                        mps_free[ms] = t_a
                        hps_free[0] = t_a
                    else:
                        ms, t_mm = mm_feat(slot, 0, halo=True)
                        chain("dve", lambda: nc.vector.tensor_tensor(out=zext[:, HAL:TE], in0=mps[ms][:, 0:T], in1=cext[:, HAL:TE], op=ALU.mult), t_mm, last["act"])
                        t_d = chain("dve", lambda: nc.vector.tensor_tensor(out=zext[:, 0:HAL], in0=hps[:, 0:HAL], in1=cext[:, 0:HAL], op=ALU.mult))
                        mps_free[ms] = t_d
                        hps_free[0] = t_d
                        chain("dve", lambda: nc.vector.tensor_scalar(out=ycv[:], in0=zext[:, HAL - 2:TE - 2], scalar1=wcvS[:, 0, i:i + 1], scalar2=None, op0=ALU.mult))
                        chain("dve", lambda: nc.vector.scalar_tensor_tensor(out=ycv[:], in0=zext[:, HAL - 1:TE - 1], scalar=wcvS[:, 1, i:i + 1], in1=ycv[:],
                                                                           op0=ALU.mult, op1=ALU.add))
                        chain("dve", lambda: nc.vector.scalar_tensor_tensor(out=ycv[:], in0=zext[:, HAL:TE], scalar=wcvS[:, 2, i:i + 1], in1=ycv[:],
                                                                           op0=ALU.mult, op1=ALU.add))
                        chain("dve", lambda: nc.vector.tensor_tensor(out=ycv[:], in0=ycv[:], in1=bS[:], op=ALU.mult))
                        ms, t_mm = mm_feat(slot, 1)
                        tf = get_tmpf()
                        t_a = chain("act", lambda: nc.scalar.activation(out=tf[:], in_=mps[ms][:, 0:T], func=AF.Silu), t_mm, last["dve"])
                        mps_free[ms] = t_a
                        chain("dve", lambda: nc.vector.tensor_tensor(out=brT[:, 24 + i, :], in0=tf[:], in1=ycv[:], op=ALU.mult), t_a)
                release(slot)

            for dp in range(NCH // 2):
                for mb in range(4):
                    slot = load_block()
                    dcl = mb // 2
                    dc = 2 * dp + dcl
                    for j in range(2):
                        i = 2 * (mb % 2) + j
                        ms, t_mm = mm_feat(slot, j)
                        t_a = chain("act", lambda: nc.scalar.activation(out=sgT[:, i, dcl, :], in_=mps[ms][:, 0:T], func=AF.Sigmoid,
                                                                       bias=bmS[:, i, dc:dc + 1]), t_mm, last["dve"])
                        mps_free[ms] = t_a
                    release(slot)
                slot = load_block()
                kb.wait("pe", last["dve"])
                for dcl in range(2):
                    dc = 2 * dp + dcl
                    for i in range(4):
                        ms = cnt["mps"] % NPS
                        cnt["mps"] += 1
                        kb.wait("pe", mps_free[ms])
                        for wcn in range(8):
                            ins = nc.tensor.matmul(mps[ms][:, 0:T], lhsT=wblk[slot][:, 8 * i + wcn, dcl * 128:(dcl + 1) * 128], rhs=brT[:, 8 * i + wcn, :],
                                                   start=(wcn == 0), stop=(wcn == 7))
                        t_mm = pe_mark(ins)
                        if i == 0:
                            t_d = chain("dve", lambda: nc.vector.tensor_tensor(out=accm[:], in0=mps[ms][:, 0:T], in1=sgT[:, i, dcl, :], op=ALU.mult), t_mm, last["act"])
                        else:
                            tf = get_tmpf()
                            t_d = chain("dve", lambda: nc.vector.tensor_tensor(out=tf[:], in0=mps[ms][:, 0:T], in1=sgT[:, i, dcl, :], op=ALU.mult), t_mm, last["act"])
                            if i < 3:
                                chain("dve", lambda: nc.vector.tensor_tensor(out=accm[:], in0=accm[:], in1=tf[:], op=ALU.add))
                            else:
                                chain("dve", lambda: nc.vector.tensor_tensor(out=mgT[:, dc, :], in0=accm[:], in1=tf[:], op=ALU.add), last["pe"])
                        mps_free[ms] = t_d
                release(slot)

            kb.wait("pe", last["dve"])
            for ob in range(16):
                slot = load_block()
                for st_ in pending:
                    st_()
                pending = []
                for s in range(T // 128):
                    ms = cnt["mps"] % NPS
                    cnt["mps"] += 1
                    kb.wait("pe", mps_free[ms])
                    for c in range(NCH):
                        ins = nc.tensor.matmul(mps[ms][:, 0:256], lhsT=mgT[:, c, s * 128:(s + 1) * 128], rhs=wblk[slot][:, c, :],
                                               start=(c == 0), stop=(c == NCH - 1))
                    t_mm = pe_mark(ins)
                    xi = cnt["xr"] % 2
                    cnt["xr"] += 1
                    kb.wait("sp", xres_free[xi])
                    r0 = t0 + s * 128
                    t_x = xrsem[xi].inc(nc.sync.dma_start(out=xres[xi][:], in_=x[r0:r0 + 128, ob * 256:(ob + 1) * 256]))
                    t_d = chain("dve", lambda: nc.vector.tensor_tensor(out=orow[xi][:], in0=mps[ms][:, 0:256], in1=xres[xi][:], op=ALU.add),
                                t_mm, t_x, orsem[xi].tok())
                    mps_free[ms] = t_d
                    xres_free[xi] = t_d

                    def mk_store(xi=xi, r0=r0, ob=ob, t_d=t_d):
                        kb.wait("sp", t_d)
                        orsem[xi].inc(nc.sync.dma_start(out=y[r0:r0 + 128, ob * 256:(ob + 1) * 256], in_=orow[xi][:]))
                    mk_store()
                release(slot)
        for s_ in orsem:
            kb.wait("sp", s_.tok())
    return nc


def prep_c_consts(w_in, w_pool, pool_scale, w_conv, w_branch, b_merge, w_out, g):
    def blk(mat, cols):
        return mat[:, cols].reshape(NCH, 128, 256).transpose(1, 0, 2).reshape(128, NCH * 256)
    blocks = []
    for b in range(4):
        blocks.append(blk(w_in, np.arange(3072 + b * 256, 3072 + (b + 1) * 256)))
    for b in range(4):
        blocks.append(blk(w_in, np.arange(7168 + b * 256, 7168 + (b + 1) * 256)))
    for gg in range(4):
        blocks.append(blk(w_in, np.arange(8200 + gg * 256, 8200 + (gg + 1) * 256)))
        blocks.append(blk(w_in, np.arange(9224 + gg * 256, 9224 + (gg + 1) * 256)))
    for i in range(8):
        blocks.append(blk(w_in, np.concatenate([np.arange(10248 + i * 128, 10248 + (i + 1) * 128), np.arange(11272 + i * 128, 11272 + (i + 1) * 128)])))
        blocks.append(blk(w_in, np.concatenate([np.arange(12296 + i * 128, 12296 + (i + 1) * 128), np.arange(13320 + i * 128, 13320 + (i + 1) * 128)])))
    wbr = w_branch.reshape(D, D)
    for dp in range(16):
        for mb in range(4):
            dc = 2 * dp + mb // 2
            cc = []
            for j in range(2):
                i = 2 * (mb % 2) + j
                cc.append(np.arange(14344 + i * 4096 + dc * 128, 14344 + i * 4096 + (dc + 1) * 128))
            blocks.append(blk(w_in, np.concatenate(cc)))
        blocks.append(blk(wbr, np.arange(dp * 256, (dp + 1) * 256)))
    for ob in range(16):
        blocks.append(blk(w_out, np.arange(ob * 256, (ob + 1) * 256)))
    assert len(blocks) == 128
    consts = {
        "gcol": np.ascontiguousarray(g.reshape(NCH, 128).T),
        "wpool": np.ascontiguousarray(w_pool.reshape(4, 2, 128, 256).transpose(2, 0, 1, 3).reshape(128, 8, 256)),
        "pscale": np.ascontiguousarray(pool_scale.reshape(8, 128).T),
        "wconv": np.ascontiguousarray(w_conv.reshape(3, 8, 128).transpose(2, 0, 1)),
        "bmerge": np.ascontiguousarray(b_merge.reshape(4, NCH, 128).transpose(2, 0, 1)),
    }
    return consts, np.stack(blocks, axis=0)


def icnt_table(pos0, n):
    pos = np.arange(pos0, pos0 + n)
    tab = np.stack([1.0 / np.minimum(pos + 1, w) for w in (2, 4, 8, 16)], axis=0).astype(np.float32)
    return np.ascontiguousarray(np.broadcast_to(tab[None], (128, 4, n)))


def run_phase_c(h, oo, consts, wq):
    nc = get_prog("c")
    in_maps = []
    for c in range(NCORES):
        b = c // (NCORES // BATCH)
        off = (c % (NCORES // BATCH)) * TPC
        r0 = c * TPC
        xh = np.zeros((128, D), np.float32) if off == 0 else np.ascontiguousarray(h[r0 - 128:r0])
        oa = np.ascontiguousarray(oo[:, b, :, :, off:off + TPC].transpose(1, 0, 2, 3).reshape(2, W, TPC))
        m = dict(consts)
        m.update({"x": np.ascontiguousarray(h[r0:r0 + TPC]), "xh": xh, "oa": oa, "icnt": icnt_table(off, TPC), "wq": wq})
        in_maps.append(m)
    res = run_bass_kernel_spmd(nc, in_maps, core_ids=list(range(NCORES)))
    return np.concatenate([r["y"] for r in res.results], axis=0)


def build_phase_d(TPC=TPC):
    nc = bass.Bass("TRN2", target_bir_lowering=False)
    x = nc.dram_tensor("x", [TPC, D], F32, kind="ExternalInput").ap()
    g = nc.dram_tensor("g", [128, D], F32, kind="ExternalInput").ap()
    y = nc.dram_tensor("y", [TPC, D], F32, kind="ExternalOutput").ap()
    with ExitStack() as es:
        kb = KB(nc, es)
        grep = kb.sb("grep", [128, D], F32)
        NB_ = 2
        xb = [kb.sb(f"xb{i}", [128, D], F32) for i in range(NB_)]
        yb = [kb.sb(f"yb{i}", [128, D], F32) for i in range(NB_)]
        junk = kb.sb("junk", [128, D], BF16)
        ss = kb.sb("ss", [128, 1], F32)
        rs = kb.sb("rs", [128, 1], F32)
        epsT = kb.sb("epsT", [128, 1], F32)
        last = {"act": None, "dve": None}

        def chain(e, ins, *deps):
            kb.wait(e, last[e], *deps)
            t = kb.mark(ins(), e)
            last[e] = t
            return t

        csem = DmaSem(kb, "csem")
        xs = [DmaSem(kb, f"xs{i}") for i in range(NB_)]
        ys = [DmaSem(kb, f"ys{i}") for i in range(NB_)]
        xfree = [None] * NB_
        c1 = csem.inc(nc.sync.dma_start(out=grep[:], in_=g[:, :]))
        t_e = chain("dve", lambda: nc.vector.memset(epsT[:], EPS), c1)
        kb.wait("act", t_e)
        for i in range(TPC // 128):
            s_ = i % NB_
            kb.wait("sp", xfree[s_])
            ld = xs[s_].inc(nc.sync.dma_start(out=xb[s_][:], in_=x[i * 128:(i + 1) * 128, :]))
            chain("act", lambda: nc.scalar.activation(out=junk[:], in_=xb[s_][:], func=AF.Square, accum_out=ss[:]), ld, last["dve"])
            t_a = chain("act", lambda: nc.scalar.activation(out=rs[:], in_=ss[:], func=AF.Sqrt, scale=1.0 / D, bias=epsT[:]))
            chain("dve", lambda: nc.vector.reciprocal(out=rs[:], in_=rs[:]), t_a)
            t_y = chain("dve", lambda: nc.vector.scalar_tensor_tensor(out=yb[s_][:], in0=xb[s_][:], scalar=rs[:, 0:1], in1=grep[:],
                                                                     op0=ALU.mult, op1=ALU.mult), ys[s_].tok())
            xfree[s_] = t_y
            kb.wait("sp", t_y)
            ys[s_].inc(nc.sync.dma_start(out=y[i * 128:(i + 1) * 128, :], in_=yb[s_][:]))
        for s_ in ys:
            kb.wait("sp", s_.tok())
    return nc


def run_phase_d(h, final_g):
    nc = get_prog("d")
    grep = np.ascontiguousarray(np.broadcast_to(final_g[None, :], (128, D)))
    in_maps = [{"x": np.ascontiguousarray(h[c * TPC:(c + 1) * TPC]), "g": grep} for c in range(NCORES)]
    res = run_bass_kernel_spmd(nc, in_maps, core_ids=list(range(NCORES)))
    return np.concatenate([r["y"] for r in res.results], axis=0)


WCH = 4096


def build_cast(NP):
    nc = bass.Bass("TRN2", target_bir_lowering=False)
    x = nc.dram_tensor("x", [NP, 128, WCH], F32, kind="ExternalInput").ap()
    y = nc.dram_tensor("y", [NP, 128, WCH], BF16, kind="ExternalOutput").ap()
    with ExitStack() as es:
        kb = KB(nc, es)
        NB_ = 4
        xb = [kb.sb(f"xb{i}", [128, WCH], F32) for i in range(NB_)]
        yb = [kb.sb(f"yb{i}", [128, WCH], BF16) for i in range(NB_)]
        xs = [DmaSem(kb, f"xs{i}") for i in range(NB_)]
        ys = [DmaSem(kb, f"ys{i}") for i in range(NB_)]
        xfree = [None] * NB_
        engs = ["dve", "pool", "act"]
        for i in range(NP):
            s_ = i % NB_
            e = engs[i % 3]
            kb.wait("sp", xfree[s_])
            ld = xs[s_].inc(nc.sync.dma_start(out=xb[s_][:], in_=x[i, :, :]))
            kb.wait(e, ld, ys[s_].tok())
            if e == "dve":
                ins = nc.vector.tensor_copy(out=yb[s_][:], in_=xb[s_][:])
            elif e == "pool":
                ins = nc.gpsimd.tensor_copy(out=yb[s_][:], in_=xb[s_][:])
            else:
                ins = nc.scalar.copy(out=yb[s_][:], in_=xb[s_][:])
            t = kb.mark(ins, e)
            xfree[s_] = t
            kb.wait("sp", t)
            ys[s_].inc(nc.sync.dma_start(out=y[i, :, :], in_=yb[s_][:]))
        for s_ in ys:
            kb.wait("sp", s_.tok())
    return nc


def run_cast(arrays):
    sizes = [a.size for a in arrays]
    total = sum(sizes)
    unit = NCORES * 128 * WCH
    npc = -(-total // unit)
    flat = np.zeros(npc * unit, np.float32)
    off = 0
    for a in arrays:
        flat[off:off + a.size] = a.reshape(-1)
        off += a.size
    flat = flat.reshape(NCORES, npc, 128, WCH)
    key = ("w", npc)
    if key not in _CACHE:
        _CACHE[key] = build_cast(npc)
    res = run_bass_kernel_spmd(_CACHE[key], [{"x": flat[c]} for c in range(NCORES)], core_ids=list(range(NCORES)))
    out = np.concatenate([np.asarray(r["y"]).reshape(-1) for r in res.results])
    outs = []
    off = 0
    for a in arrays:
        outs.append(out[off:off + a.size].reshape(a.shape))
        off += a.size
    return outs


def kernel(x, norm_g, w_in, b_forget, w_pool, pool_scale, w_conv, w_branch, b_merge, w_out, final_g):
    f32 = lambda a: np.asarray(a, dtype=np.float32)
    x, norm_g, w_in, b_forget, w_pool, pool_scale, w_conv, w_branch, b_merge, w_out, final_g = map(
        f32, (x, norm_g, w_in, b_forget, w_pool, pool_scale, w_conv, w_branch, b_merge, w_out, final_g))
    cc, fl = [], []
    for l in range(2):
        pa = prep_a_f32(w_in[l])
        consts, wblocks = prep_c_consts(w_in[l], w_pool[l], pool_scale[l], w_conv[l], w_branch[l], b_merge[l], w_out[l], norm_g[l])
        cc.append(consts)
        fl += [pa["wa"], pa["wf"], wblocks]
    q = run_cast(fl)
    del fl
    h = np.ascontiguousarray(x.reshape(TOK, D))
    for l in range(2):
        wq_a = {"wa": q[3 * l], "wf": q[3 * l + 1]}
        qk, v, fo = run_phase_a(h, norm_g[l], wq_a)
        oo = run_phase_b(qk, v, fo, b_forget[l])
        del qk, v, fo
        h = run_phase_c(h, oo, cc[l], q[3 * l + 2])
        del oo
    out = run_phase_d(h, final_g)
    return out.reshape(BATCH, SEQ, D).astype(np.float32)
```

```python
import numpy as np
from contextlib import ExitStack
import concourse.bass as bass
import concourse.mybir as mybir
from concourse.bass_utils import run_bass_kernel_spmd

F32 = mybir.dt.float32
BF16 = mybir.dt.bfloat16
AF = mybir.ActivationFunctionType
ALU = mybir.AluOpType
AX = mybir.AxisListType

NCORES = 8
D = 4096
NCH = D // 128
SEQ = 8192
BATCH = 2
TOK = BATCH * SEQ
TPC = TOK // NCORES
TT = 512
W = 1024
NH = 8
EPS = 1e-6
SCALE = 128 ** -0.5
NEG = -60000.0


class KB:
    def __init__(self, nc, es):
        self.nc = nc
        self.es = es
        self.eng = {"pe": nc.tensor, "act": nc.scalar, "dve": nc.vector, "pool": nc.gpsimd, "sp": nc.sync}
        self.psem = {}
        self.pcnt = {}
        for e in ("pe", "act", "dve", "pool"):
            self.psem[e] = es.enter_context(nc.semaphore("prog_" + e))
            self.pcnt[e] = 0
        self.waited = {}
        self.nsem = 0

    def sb(self, name, shape, dt):
        return self.es.enter_context(self.nc.sbuf_tensor(name, shape, dt))

    def ps(self, name, shape, dt):
        return self.es.enter_context(self.nc.psum_tensor(name, shape, dt))

    def sem(self, name):
        self.nsem += 1
        return self.es.enter_context(self.nc.semaphore(name))

    def mark(self, instr, e):
        self.pcnt[e] += 1
        instr.then_inc(self.psem[e], 1)
        return (self.psem[e], self.pcnt[e], e)

    def wait(self, e, *toks):
        flat = []
        for tok in toks:
            if isinstance(tok, list):
                flat.extend(tok)
            else:
                flat.append(tok)
        for tok in flat:
            if tok is None:
                continue
            sem, val, src = tok
            key = (e, id(sem))
            if self.waited.get(key, 0) >= val:
                continue
            self.waited[key] = val
            self.eng[e].wait_ge(sem, val)


class DmaSem:
    def __init__(self, kb, name):
        self.sem = kb.sem(name)
        self.val = 0

    def inc(self, instr, n=1):
        self.val += 16
        instr.then_inc(self.sem, 16)
        return (self.sem, self.val, "dma")

    def tok(self):
        return (self.sem, self.val, "dma")


def emit_norm_tile(kb, x_rows, grep, xbuf, xn, ss, rs, hT, tpps, tok0, st):
    nc = kb.nc
    kb.wait("sp", st.get("xbuf_free"))
    ld = st["xsem"].inc(nc.sync.dma_start(out=xbuf[:], in_=x_rows))
    kb.wait("act", ld, st.get("xn_free"))
    t_sq = kb.mark(nc.scalar.activation(out=xn[:], in_=xbuf[:], func=AF.Square, accum_out=ss[:]), "act")
    kb.wait("act", t_sq)
    t_act = kb.mark(nc.scalar.activation(out=rs[:], in_=ss[:], func=AF.Sqrt, scale=1.0 / D, bias=st["eps"][:]), "act")
    kb.wait("dve", t_act)
    t_rc = kb.mark(nc.vector.reciprocal(out=rs[:], in_=rs[:]), "dve")
    kb.wait("dve", t_rc)
    t_xn = kb.mark(nc.vector.scalar_tensor_tensor(out=xn[:], in0=xbuf[:], scalar=rs[:, 0:1], in1=grep[:],
                                                  op0=ALU.mult, op1=ALU.mult), "dve")
    st["xbuf_free"] = t_xn
    kb.wait("pe", t_xn)
    last_pe = None
    for c8 in range(NCH // 8):
        slot = st["tp_i"] % len(tpps)
        st["tp_i"] += 1
        kb.wait("pe", st["tp_free"][slot])
        for j in range(8):
            c = c8 * 8 + j
            ins = nc.tensor.transpose(out=tpps[slot][:, j, :], in_=xn[:, c * 128:(c + 1) * 128], identity=st["ident"][:])
        t_pe = kb.mark(ins, "pe")
        last_pe = t_pe
        kb.wait("dve", t_pe, st.get("hT_free"))
        t_ev = kb.mark(nc.vector.tensor_copy(out=hT[:, c8 * 8:(c8 + 1) * 8, tok0:tok0 + 128], in_=tpps[slot][:, :, :]), "dve")
        st["tp_free"][slot] = t_ev
        st["hT_ready"] = t_ev
    st["xn_free"] = last_pe


def make_ident(kb, ident_f, ident):
    nc = kb.nc
    t = kb.mark(nc.gpsimd.memset(ident_f[:], 0.0), "pool")
    kb.wait("pool", t)
    t = kb.mark(nc.gpsimd.affine_select(out=ident_f[:], in_=ident_f[:], pattern=[[-1, 128]], compare_op=ALU.not_equal,
                                        fill=1.0, base=0, channel_multiplier=1), "pool")
    kb.wait("pool", t)
    return kb.mark(nc.gpsimd.tensor_copy(out=ident[:], in_=ident_f[:]), "pool")


def build_phase_a(TPC=TPC):
    nc = bass.Bass("TRN2", target_bir_lowering=False)
    x = nc.dram_tensor("x", [TPC, D], F32, kind="ExternalInput").ap()
    g = nc.dram_tensor("g", [128, D], F32, kind="ExternalInput").ap()
    wa = nc.dram_tensor("wa", [12, 128, NCH * 512], BF16, kind="ExternalInput").ap()
    wf = nc.dram_tensor("wf", [128, NCH * NH], BF16, kind="ExternalInput").ap()
    qk = nc.dram_tensor("qk", [4, NH, 128, TPC], F32, kind="ExternalOutput").ap()
    v = nc.dram_tensor("v", [2, TPC, W], F32, kind="ExternalOutput").ap()
    fo = nc.dram_tensor("fo", [NH, TPC], F32, kind="ExternalOutput").ap()
    with ExitStack() as es:
        kb = KB(nc, es)
        grep = kb.sb("grep", [128, D], F32)
        xbuf = kb.sb("xbuf", [128, D], F32)
        xn = kb.sb("xn", [128, D], BF16)
        ss = kb.sb("ss", [128, 1], F32)
        rs = kb.sb("rs", [128, 1], F32)
        epsT = kb.sb("epsT", [128, 1], F32)
        ident_f = kb.sb("ident_f", [128, 128], F32)
        ident = kb.sb("ident", [128, 128], BF16)
        hT = kb.sb("hT", [128, NCH, TT], BF16)
        NWA = 3
        wblk = [kb.sb(f"wblk{i}", [128, NCH, 512], BF16) for i in range(NWA)]
        wf_b = kb.sb("wf_b", [128, NCH, NH], BF16)
        NOST = 4
        ost = [kb.sb(f"ost{i}", [128, 512], F32) for i in range(NOST)]
        tpps = [kb.ps(f"tpps{i}", [128, 8, 128], BF16) for i in range(2)]
        NPS = 4
        mps = [kb.ps(f"mps{i}", [128, 512], F32) for i in range(NPS)]

        st = {"xsem": DmaSem(kb, "xsem"), "tp_i": 0, "tp_free": [None, None], "ident": ident, "eps": epsT}
        csem = DmaSem(kb, "csem")
        wl_sem = [DmaSem(kb, f"wlsem{i}") for i in range(NWA)]
        wblk_free = [None] * NWA
        ost_sem = [DmaSem(kb, f"ostsem{i}") for i in range(NOST)]
        mps_free = [None] * NPS
        fin = []

        t_id = make_ident(kb, ident_f, ident)
        t_eps = kb.mark(nc.vector.memset(epsT[:], EPS), "dve")
        c1 = csem.inc(nc.sync.dma_start(out=grep[:], in_=g[:, :]))
        c2 = csem.inc(nc.sync.dma_start(out=wf_b[:].rearrange("p c n -> p (c n)"), in_=wf[:, :]))
        kb.wait("dve", c2)
        kb.wait("pe", t_id, c2)
        kb.wait("act", t_eps)

        piece_i = 0
        blk_i = 0
        mps_i = 0
        ost_i = 0
        last_mm = None
        for tt in range(TPC // TT):
            st["hT_free"] = last_mm
            for s in range(TT // 128):
                r0 = tt * TT + s * 128
                emit_norm_tile(kb, x[r0:r0 + 128, :], grep, xbuf, xn, ss, rs, hT, tpps, s * 128, st)
            kb.wait("pe", st["hT_ready"])
            ms = mps_i % NPS
            mps_i += 1
            kb.wait("pe", mps_free[ms])
            for c in range(NCH):
                ins = nc.tensor.matmul(mps[ms][0:NH, :], lhsT=wf_b[:, c, :], rhs=hT[:, c, :], start=(c == 0), stop=(c == NCH - 1))
            t_mm = kb.mark(ins, "pe")
            os_ = ost_i % NOST
            ost_i += 1
            kb.wait("act", t_mm, ost_sem[os_].tok())
            t_ev = kb.mark(nc.scalar.copy(out=ost[os_][0:NH, :], in_=mps[ms][0:NH, :]), "act")
            mps_free[ms] = t_ev
            kb.wait("act", t_ev)
            ost_sem[os_].inc(nc.scalar.dma_start(out=fo[:, tt * TT:(tt + 1) * TT], in_=ost[os_][0:NH, :]))
            for blk in range(12):
                kind = blk // 2
                half = blk % 2
                wslot = blk_i % NWA
                blk_i += 1
                kb.wait("sp", wblk_free[wslot])
                t_cast = wl_sem[wslot].inc(nc.sync.dma_start(out=wblk[wslot][:].rearrange("p c n -> p (c n)"), in_=wa[blk, :, :]))
                kb.wait("pe", t_cast)
                for j in range(4):
                    ms = mps_i % NPS
                    mps_i += 1
                    kb.wait("pe", mps_free[ms])
                    for c in range(NCH):
                        if kind in (2, 5):
                            ins = nc.tensor.matmul(mps[ms][:], lhsT=hT[:, c, j * 128:(j + 1) * 128], rhs=wblk[wslot][:, c, :],
                                                   start=(c == 0), stop=(c == NCH - 1))
                        else:
                            ins = nc.tensor.matmul(mps[ms][:], lhsT=wblk[wslot][:, c, j * 128:(j + 1) * 128], rhs=hT[:, c, :],
                                                   start=(c == 0), stop=(c == NCH - 1))
                    t_mm = kb.mark(ins, "pe")
                    last_mm = t_mm
                    os_ = ost_i % NOST
                    ost_i += 1
                    kb.wait("act", t_mm, ost_sem[os_].tok())
                    t_ev = kb.mark(nc.scalar.copy(out=ost[os_][:], in_=mps[ms][:]), "act")
                    mps_free[ms] = t_ev
                    if kind in (2, 5):
                        dst = v[kind // 3, tt * TT + j * 128: tt * TT + (j + 1) * 128, half * 512:(half + 1) * 512]
                    else:
                        which = {0: 0, 1: 1, 3: 2, 4: 3}[kind]
                        dst = qk[which, half * 4 + j, :, tt * TT:(tt + 1) * TT]
                    kb.wait("act", t_ev)
                    ost_sem[os_].inc(nc.scalar.dma_start(out=dst, in_=ost[os_][:]))
                wblk_free[wslot] = last_mm
        for s_ in ost_sem:
            kb.wait("act", s_.tok())
    return nc


_CACHE = {}


def get_prog(name):
    if name not in _CACHE:
        _CACHE[name] = {"a": build_phase_a, "b": build_phase_b, "c": build_phase_c, "d": build_phase_d}[name]()
    return _CACHE[name]


def prep_a_f32(w_in):
    cols = np.concatenate([np.arange(0, 3 * W), np.arange(4 * W, 7 * W)])
    wa = w_in[:, cols].reshape(NCH, 128, 12, 512).transpose(2, 1, 0, 3).reshape(12, 128, NCH * 512)
    wf = w_in[:, 8 * W:8 * W + NH].reshape(NCH, 128, NH).transpose(1, 0, 2).reshape(128, NCH * NH)
    return {"wa": np.ascontiguousarray(wa), "wf": np.ascontiguousarray(wf)}


def run_phase_a(x_flat, g, wq):
    nc = get_prog("a")
    grep = np.ascontiguousarray(np.broadcast_to(g[None, :], (128, D)))
    in_maps = [{"x": np.ascontiguousarray(x_flat[c * TPC:(c + 1) * TPC]), "g": grep, "wa": wq["wa"], "wf": wq["wf"]} for c in range(NCORES)]
    res = run_bass_kernel_spmd(nc, in_maps, core_ids=list(range(NCORES)))
    qk = np.concatenate([r["qk"] for r in res.results], axis=3)
    v = np.concatenate([r["v"] for r in res.results], axis=1)
    fo = np.concatenate([r["fo"] for r in res.results], axis=1)
    return qk, v, fo


def build_phase_b(SEQ=SEQ, NB=BATCH):
    nc = bass.Bass("TRN2", target_bir_lowering=False)
    NKC = SEQ // 128
    NQT = SEQ // 512
    NBLK = SEQ // 256
    NPC = SEQ // 2048
    qk = nc.dram_tensor("qk", [NB, 4, 128, SEQ], F32, kind="ExternalInput").ap()
    vv = nc.dram_tensor("vv", [NB, 2, SEQ, 128], F32, kind="ExternalInput").ap()
    ff = nc.dram_tensor("ff", [NB, 1, SEQ], F32, kind="ExternalInput").ap()
    bf = nc.dram_tensor("bf", [1, 1], F32, kind="ExternalInput").ap()
    oo = nc.dram_tensor("oo", [NB, 2, 128, SEQ], F32, kind="ExternalOutput").ap()
    with ExitStack() as es:
        kb = KB(nc, es)
        qTs = [kb.sb(f"qT{i}", [128, SEQ], BF16) for i in range(2)]
        kTs = [kb.sb(f"kT{i}", [128, SEQ], BF16) for i in range(2)]
        vSs = [kb.sb(f"vS{i}", [128, NKC, 128], BF16) for i in range(2)]
        crep = kb.sb("crep", [128, SEQ], F32)
        negc = kb.sb("negc", [128, NKC], F32)
        NSTG = 2
        stg = [kb.sb(f"stg{i}", [128, 2048], F32) for i in range(NSTG)]
        biasT = kb.sb("biasT", [32, SEQ], BF16)
        gate_all = kb.sb("gate_all", [128, SEQ // 128, 32], F32)
        ksum = kb.sb("ksum", [128, 32], F32)
        gate_m = kb.sb("gate_m", [128, 2, 32], F32)
        top8 = kb.sb("top8", [128, 2, 8], F32)
        selb = kb.sb("selb", [128, 2, 32], F32)
        E_all = kb.sb("E_all", [32, 32, 128], BF16)
        ident_f = kb.sb("ident_f", [128, 128], F32)
        ident = kb.sb("ident", [128, 128], BF16)
        tri_f = kb.sb("tri_f", [128, 128], F32)
        tri = kb.sb("tri", [128, 128], BF16)
        ones_b = kb.sb("ones_b", [128, 128], BF16)
        onesrow = kb.sb("onesrow", [1, 128], F32)
        negone = kb.sb("negone", [1, 2], F32)
        one1 = kb.sb("one1", [1, 1], F32)
        negb = kb.sb("negb", [1, 1], F32)
        lrow = kb.sb("lrow", [1, 2048], F32)
        carry = kb.sb("carry", [1, 1], F32)
        NPT = 4
        PT = [kb.sb(f"PT{i}", [128, 512], BF16) for i in range(NPT)]
        tmpS = [kb.sb(f"tmpS{i}", [128, 512], F32) for i in range(NPT)]
        rden = kb.sb("rden", [128, 512], F32)
        NOS = 2
        oS = [kb.sb(f"oS{i}", [128, 512], F32) for i in range(NOS)]
        Sps = [kb.ps(f"Sps{i}", [128, 512], F32) for i in range(NPT)]
        accps = kb.ps("accps", [128, 512], F32)
        denps = kb.ps("denps", [128, 512], F32)
        mscps = [kb.ps(f"mscps{i}", [128, 512], F32) for i in range(2)]

        ldsem = [DmaSem(kb, f"ldsem{i}") for i in range(NSTG)]
        csem = DmaSem(kb, "csem")
        osem = [DmaSem(kb, f"osem{i}") for i in range(NOS)]
        stg_free = [None] * NSTG
        st = {"stg_i": 0, "pt_i": 0, "os_i": 0, "msc_i": 0}
        S_free = [None] * NPT
        PT_free = [None] * NPT
        tmp_free = [None] * NPT
        msc_free = [None, None]
        last = {"pe": None, "act": None, "dve": None, "pool": None}

        def chain(e, ins, *deps):
            kb.wait(e, last[e], *deps)
            t = kb.mark(ins(), e)
            last[e] = t
            return t

        t = chain("pool", lambda: nc.gpsimd.memset(ident_f[:], 0.0))
        t = chain("pool", lambda: nc.gpsimd.affine_select(out=ident_f[:], in_=ident_f[:], pattern=[[-1, 128]],
                                                          compare_op=ALU.not_equal, fill=1.0, base=0, channel_multiplier=1))
        t = chain("pool", lambda: nc.gpsimd.tensor_copy(out=ident[:], in_=ident_f[:]))
        t = chain("pool", lambda: nc.gpsimd.memset(tri_f[:], 0.0))
        t = chain("pool", lambda: nc.gpsimd.affine_select(out=tri_f[:], in_=tri_f[:], pattern=[[1, 128]],
                                                          compare_op=ALU.is_ge, fill=NEG, base=0, channel_multiplier=-1))
        t = chain("pool", lambda: nc.gpsimd.tensor_copy(out=tri[:], in_=tri_f[:]))
        t = chain("pool", lambda: nc.gpsimd.memset(E_all[:], 0.0))
        t = chain("pool", lambda: nc.gpsimd.affine_select(out=E_all[:], in_=E_all[:], pattern=[[-1, 32], [0, 128]],
                                                          compare_op=ALU.not_equal, fill=1.0, base=0, channel_multiplier=1))
        t = chain("pool", lambda: nc.gpsimd.memset(ones_b[:], 1.0))
        t = chain("pool", lambda: nc.gpsimd.memset(onesrow[:], 1.0))
        t = chain("pool", lambda: nc.gpsimd.memset(negone[:], -1.0))
        t = chain("pool", lambda: nc.gpsimd.memset(one1[:], 1.0))
        t_const = chain("pool", lambda: nc.gpsimd.memset(carry[:], 0.0))
        cb = csem.inc(nc.sync.dma_start(out=negb[:], in_=bf[:, :]))
        t_negb = chain("dve", lambda: nc.vector.tensor_scalar(out=negb[:], in0=negb[:], scalar1=-1.0, scalar2=None, op0=ALU.mult), cb)
        chain("dve", lambda: nc.vector.memset(ksum[:], 0.0))
        chain("dve", lambda: nc.vector.memset(gate_all[:], 0.0))
        kb.wait("pe", t_const)
        kb.wait("act", t_const)
        kb.wait("dve", t_const)

        def load_piece(src_ap, shape_view=None):
            s_ = st["stg_i"] % NSTG
            st["stg_i"] += 1
            kb.wait("sp", stg_free[s_])
            dst = stg[s_][:] if shape_view is None else shape_view(stg[s_])
            tok = ldsem[s_].inc(nc.sync.dma_start(out=dst, in_=src_ap))
            return s_, tok

        def setup(b, which, attn_done, out):
            moba = (which == 0)
            qT, kT, vS = qTs[which], kTs[which], vSs[which]
            t_k = None
            for p in range(NPC):
                s_, tok = load_piece(qk[b, 2 * which + 1, :, p * 2048:(p + 1) * 2048])
                kb.wait("pool", tok, attn_done)
                t_k = kb.mark(nc.gpsimd.tensor_copy(out=kT[:, p * 2048:(p + 1) * 2048], in_=stg[s_][:]), "pool")
                fr = [t_k]
                if moba:
                    t_ks = chain("dve", lambda: nc.vector.tensor_reduce(
                        out=ksum[:, p * 8:(p + 1) * 8], in_=stg[s_][:].rearrange("p (n t) -> p n t", t=256), axis=AX.X, op=ALU.add),
                        tok, attn_done)
                    fr.append(t_ks)
                stg_free[s_] = fr
                yield
            t_v = None
            for p in range(NPC):
                s_, tok = load_piece(vv[b, which, p * 2048:(p + 1) * 2048, :].rearrange("(c p) d -> p c d", p=128),
                                     lambda tl: tl[:].rearrange("p (c d) -> p c d", d=128))
                kb.wait("pool", tok, attn_done)
                t_v = kb.mark(nc.gpsimd.tensor_copy(out=vS[:, p * 16:(p + 1) * 16, :],
                                                   in_=stg[s_][:].rearrange("p (c d) -> p c d", d=128)), "pool")
                stg_free[s_] = [t_v]
                yield
            t_q = None
            t_gate = None
            for p in range(NPC):
                s_, tok = load_piece(qk[b, 2 * which, :, p * 2048:(p + 1) * 2048])
                kb.wait("pool", tok, attn_done)
                t_q = kb.mark(nc.gpsimd.tensor_copy(out=qT[:, p * 2048:(p + 1) * 2048], in_=stg[s_][:]), "pool")
                fr = [t_q]
                if moba:
                    m_ = st["msc_i"] % 2
                    st["msc_i"] += 1
                    kb.wait("pe", tok, last["dve"], msc_free[m_])
                    gv = mscps[m_][:].rearrange("p (t n) -> p t n", n=32)
                    for j in range(16):
                        ins = nc.tensor.matmul(gv[:, j, :], lhsT=stg[s_][:, j * 128:(j + 1) * 128], rhs=ksum[:, :], start=True, stop=True)
                    t_g = kb.mark(ins, "pe")
                    last["pe"] = t_g
                    fr.append(t_g)
                    t_gate = chain("dve", lambda: nc.vector.tensor_copy(out=gate_all[:, p * 16:(p + 1) * 16, :], in_=gv), t_g)
                    msc_free[m_] = t_gate
                stg_free[s_] = fr
                yield
            t_bias = None
            if moba:
                for qb in range(NBLK):
                    chain("dve", lambda: nc.vector.memset(gate_m[:], -1e30))
                    if qb > 0:
                        chain("dve", lambda: nc.vector.tensor_copy(out=gate_m[:, :, 0:qb], in_=gate_all[:, 2 * qb:2 * qb + 2, 0:qb]))
                    for j in range(2):
                        chain("dve", lambda: nc.vector.max(out=top8[:, j, :], in_=gate_m[:, j, :]))
                        chain("dve", lambda: nc.vector.tensor_scalar(out=selb[:, j, :], in0=gate_m[:, j, :], scalar1=top8[:, j, 2:3],
                                                                    scalar2=None, op0=ALU.is_ge))
                    chain("dve", lambda: nc.vector.tensor_scalar(out=selb[:], in0=selb[:], scalar1=-1.0, scalar2=-NEG,
                                                                op0=ALU.add, op1=ALU.mult))
                    if qb + 1 < 32:
                        chain("dve", lambda: nc.vector.memset(selb[:, :, qb + 1:32], NEG))
                    t_sel = chain("dve", lambda: nc.vector.memset(selb[:, :, qb:qb + 1], 0.0))
                    m_ = st["msc_i"] % 2
                    st["msc_i"] += 1
                    kb.wait("pe", t_sel, msc_free[m_])
                    for j in range(2):
                        ins = nc.tensor.transpose(out=mscps[m_][0:32, j * 128:(j + 1) * 128], in_=selb[:, j, :], identity=ident_f[:])
                    t_tr = kb.mark(ins, "pe")
                    last["pe"] = t_tr
                    t_bias = chain("dve", lambda: nc.vector.tensor_copy(out=biasT[:, qb * 256:(qb + 1) * 256], in_=mscps[m_][0:32, 0:256]),
                                   t_tr, attn_done)
                    msc_free[m_] = t_bias
                    yield
            else:
                for p in range(NPC):
                    s_, tok = load_piece(ff[b, :, p * 2048:(p + 1) * 2048], lambda tl: tl[0:1, :])
                    fr_ = stg[s_][0:1, :]
                    chain("act", lambda: nc.scalar.activation(out=fr_, in_=fr_, func=AF.Exp, scale=-1.0, bias=negb[:]), tok, t_negb)
                    t_l = chain("act", lambda: nc.scalar.activation(out=fr_, in_=fr_, func=AF.Ln, scale=1.0, bias=one1[:]))
                    chain("dve", lambda: nc.vector.tensor_scalar(out=fr_, in0=fr_, scalar1=-1.0 / SCALE, scalar2=None, op0=ALU.mult), t_l)
                    init = 0.0 if p == 0 else carry[:, 0:1]
                    kb.wait("dve", last["pe"])
                    t_sc = chain("dve", lambda: nc.vector.tensor_tensor_scan(out=lrow[:], data0=one1[:, 0:1].to_broadcast([1, 2048]), data1=fr_,
                                                                           initial=init, op0=ALU.mult, op1=ALU.add))
                    stg_free[s_] = [t_sc]
                    t_sc = chain("dve", lambda: nc.vector.tensor_copy(out=carry[:], in_=lrow[:, 2047:2048]))
                    for i in range(4):
                        m_ = st["msc_i"] % 2
                        st["msc_i"] += 1
                        kb.wait("pe", t_sc, msc_free[m_])
                        t_mm = kb.mark(nc.tensor.matmul(mscps[m_][:], lhsT=onesrow[:], rhs=lrow[:, i * 512:(i + 1) * 512], start=True, stop=True), "pe")
                        last["pe"] = t_mm
                        t_cr = chain("dve", lambda: nc.vector.tensor_copy(out=crep[:, p * 2048 + i * 512:p * 2048 + (i + 1) * 512], in_=mscps[m_][:]),
                                     t_mm, attn_done)
                        msc_free[m_] = t_cr
                    m_ = st["msc_i"] % 2
                    st["msc_i"] += 1
                    kb.wait("pe", t_sc, msc_free[m_])
                    for kc in range(16):
                        ins = nc.tensor.matmul(mscps[m_][:, 2 * kc:2 * kc + 2], lhsT=lrow[:, kc * 128:(kc + 1) * 128], rhs=negone[:], start=True, stop=True)
                    t_mm = kb.mark(ins, "pe")
                    last["pe"] = t_mm
                    t_bias = chain("dve", lambda: nc.vector.tensor_copy(
                        out=negc[:, p * 16:(p + 1) * 16], in_=mscps[m_][:, 0:32].rearrange("p (c two) -> p c two", two=2)[:, :, 0]), t_mm, attn_done)
                    msc_free[m_] = t_bias
                    yield
            out["toks"] = [t_k, t_v, t_q, t_bias]
            yield

        def main(b, which, toks, bg):
            moba = (which == 0)
            qT, kT, vS = qTs[which], kTs[which], vSs[which]
            t_k, t_v, t_q, t_bias = toks
            kb.wait("pe", t_k, t_v, t_q, t_bias)
            units = [(Q, kc) for Q in range(NQT) for kc in range(4 * Q + 4)]
            res = {"last_pv": None}

            def emit_qk(Q, kc):
                q0 = Q * 512
                c0 = max(0, kc * 128 - q0)
                diag = kc * 128 >= q0
                n = kc // 2
                sl = st["pt_i"] % NPT
                st["pt_i"] += 1
                kb.wait("pe", S_free[sl])
                ins = nc.tensor.matmul(Sps[sl][:, c0:512], lhsT=kT[:, kc * 128:(kc + 1) * 128], rhs=qT[:, q0 + c0:q0 + 512],
                                       start=True, stop=not (moba or diag))
                if moba:
                    ins = nc.tensor.matmul(Sps[sl][:, c0:512], lhsT=E_all[:, n, :], rhs=biasT[:, q0 + c0:q0 + 512],
                                           start=False, stop=not diag)
                if diag:
                    ins = nc.tensor.matmul(Sps[sl][:, c0:c0 + 128], lhsT=ident[:], rhs=tri[:], start=False, stop=True)
                t_s = kb.mark(ins, "pe")
                if moba:
                    kb.wait("act", t_s, PT_free[sl])
                    t_p = kb.mark(nc.scalar.activation(out=PT[sl][:, c0:512], in_=Sps[sl][:, c0:512], func=AF.Exp, scale=SCALE), "act")
                    S_free[sl] = t_p
                else:
                    kb.wait("dve", t_s, tmp_free[sl])
                    t_t = kb.mark(nc.vector.scalar_tensor_tensor(out=tmpS[sl][:, c0:512], in0=Sps[sl][:, c0:512], scalar=negc[:, kc:kc + 1],
                                                                 in1=crep[:, q0 + c0:q0 + 512], op0=ALU.add, op1=ALU.add), "dve")
                    S_free[sl] = t_t
                    kb.wait("act", t_t, PT_free[sl])
                    t_p = kb.mark(nc.scalar.activation(out=PT[sl][:, c0:512], in_=tmpS[sl][:, c0:512], func=AF.Exp, scale=SCALE), "act")
                    tmp_free[sl] = t_p
                return (Q, kc, sl, c0, t_p)

            def emit_pv(info):
                Q, kc, sl, c0, t_p = info
                q0 = Q * 512
                nkc = 4 * Q + 4
                first = (kc == 0)
                kb.wait("pe", t_p)
                if first:
                    kb.wait("pe", st.get("acc_free"))
                nc.tensor.matmul(accps[:, c0:512], lhsT=vS[:, kc, :], rhs=PT[sl][:, c0:512], start=first, stop=(kc == nkc - 1))
                ins = nc.tensor.matmul(denps[:, c0:512], lhsT=ones_b[:], rhs=PT[sl][:, c0:512], start=first, stop=(kc == nkc - 1))
                t_pv = kb.mark(ins, "pe")
                PT_free[sl] = t_pv
                res["last_pv"] = t_pv
                if kc == nkc - 1:
                    t_r = chain("dve", lambda: nc.vector.reciprocal(out=rden[:], in_=denps[:]), t_pv)
                    o_ = st["os_i"] % NOS
                    st["os_i"] += 1
                    t_o = chain("dve", lambda: nc.vector.tensor_tensor(out=oS[o_][:], in0=accps[:], in1=rden[:], op=ALU.mult), osem[o_].tok())
                    st["acc_free"] = t_o
                    kb.wait("sp", t_o)
                    osem[o_].inc(nc.sync.dma_start(out=oo[b, which, :, q0:q0 + 512], in_=oS[o_][:]))

            LAG = 2
            pend = []
            for ui, (Q, kc) in enumerate(units):
                pend.append(emit_qk(Q, kc))
                if len(pend) > LAG:
                    emit_pv(pend.pop(0))
                if bg is not None and ui % 6 == 5:
                    next(bg, None)
            while pend:
                emit_pv(pend.pop(0))
            t_last_pv = res["last_pv"]
            return t_last_pv

        order = [(b, which) for b in range(NB) for which in range(2)]
        outs = [dict() for _ in order]
        dones = [None] * len(order)
        gens = []
        for a, (b, which) in enumerate(order):
            gens.append(None)
        g0 = setup(order[0][0], order[0][1], None, outs[0])
        for _ in g0:
            pass
        for a, (b, which) in enumerate(order):
            bg = None
            if a + 1 < len(order):
                nb_, nw_ = order[a + 1]
                prev_same = dones[a - 1] if a - 1 >= 0 else None
                bg = setup(nb_, nw_, prev_same, outs[a + 1])
            dones[a] = main(b, which, outs[a]["toks"], bg)
            if bg is not None:
                for _ in bg:
                    pass
        for s_ in osem:
            kb.wait("sp", s_.tok())
    return nc


def run_phase_b(qk, v, fo, b_forget):
    nc = get_prog("b")
    in_maps = []
    for h in range(NCORES):
        qk_h = np.ascontiguousarray(qk[:, h].reshape(4, 128, BATCH, SEQ).transpose(2, 0, 1, 3))
        v_h = np.ascontiguousarray(v[:, :, h * 128:(h + 1) * 128].reshape(2, BATCH, SEQ, 128).transpose(1, 0, 2, 3))
        f_h = np.ascontiguousarray(fo[h].reshape(BATCH, 1, SEQ))
        in_maps.append({"qk": qk_h, "vv": v_h, "ff": f_h, "bf": np.ascontiguousarray(b_forget[h].reshape(1, 1))})
    res = run_bass_kernel_spmd(nc, in_maps, core_ids=list(range(NCORES)))
    return np.stack([r["oo"] for r in res.results], axis=0)


HAL = 16


def build_phase_c(TPC=TPC, T=512):
    nc = bass.Bass("TRN2", target_bir_lowering=False)
    NT = TPC // T
    x = nc.dram_tensor("x", [TPC, D], F32, kind="ExternalInput").ap()
    xh = nc.dram_tensor("xh", [128, D], F32, kind="ExternalInput").ap()
    gcol = nc.dram_tensor("gcol", [128, NCH], F32, kind="ExternalInput").ap()
    NBT = 32 + 80 + 16
    wq = nc.dram_tensor("wq", [NBT, 128, NCH * 256], BF16, kind="ExternalInput").ap()
    oa = nc.dram_tensor("oa", [2, W, TPC], F32, kind="ExternalInput").ap()
    icnt = nc.dram_tensor("icnt", [128, 4, TPC], F32, kind="ExternalInput").ap()
    wpool = nc.dram_tensor("wpool", [128, 8, 256], F32, kind="ExternalInput").ap()
    pscale = nc.dram_tensor("pscale", [128, 8], F32, kind="ExternalInput").ap()
    wconv = nc.dram_tensor("wconv", [128, 3, 8], F32, kind="ExternalInput").ap()
    bmerge = nc.dram_tensor("bmerge", [128, 4, NCH], F32, kind="ExternalInput").ap()
    y = nc.dram_tensor("y", [TPC, D], F32, kind="ExternalOutput").ap()
    TE = T + HAL
    with ExitStack() as es:
        kb = KB(nc, es)
        hT = kb.sb("hT", [128, NCH, TE], BF16)
        brT = kb.sb("brT", [128, NCH, T], BF16)
        assert NCH * T >= 12288
        mg_raw = kb.sb("mg_raw", [128, NCH * T], BF16)
        mgT = mg_raw[:].rearrange("p (c t) -> p c t", t=T)
        xbuf = mg_raw[:, 0:8192].bitcast(F32)
        xn = mg_raw[:, 8192:12288]
        NW = 2
        wblk = [kb.sb(f"wblk{i}", [128, NCH, 256], BF16) for i in range(NW)]
        wpool_f = kb.sb("wpool_f", [128, 8, 256], F32)
        ss = kb.sb("ss", [128, 1], F32)
        rs = kb.sb("rs", [128, 1], F32)
        epsT = kb.sb("epsT", [128, 1], F32)
        gcolS = kb.sb("gcolS", [128, NCH], F32)
        ident_f = kb.sb("ident_f", [128, 128], F32)
        ident = kb.sb("ident", [128, 128], BF16)
        icS = kb.sb("icS", [128, 4, T], F32)
        wpool_b = kb.sb("wpool_b", [128, 8, 256], BF16)
        pscS = kb.sb("pscS", [128, 8], F32)
        wcvS = kb.sb("wcvS", [128, 3, 8], F32)
        bmS = kb.sb("bmS", [128, 4, NCH], F32)
        NTF = 2
        tmpf = [kb.sb(f"tmpf{i}", [128, T], F32) for i in range(NTF)]
        otile = [kb.sb(f"otile{i}", [128, T], F32) for i in range(2)]
        uext = kb.sb("uext", [128, 2, TE], F32)
        pa = kb.sb("pa", [128, 2, TE], F32)
        pb = kb.sb("pb", [128, 2, TE], F32)
        pooledT = kb.sb("pooledT", [128, 2, T], BF16)
        yS = kb.sb("yS", [128, 2, T], F32)
        bS = kb.sb("bS", [128, T], F32)
        cext = kb.sb("cext", [128, TE], F32)
        zext = kb.sb("zext", [128, TE], F32)
        ycv = kb.sb("ycv", [128, T], F32)
        sgT = kb.sb("sgT", [128, 4, 2, T], BF16)
        accm = kb.sb("accm", [128, T], F32)
        xres = [kb.sb(f"xres{i}", [128, 256], F32) for i in range(2)]
        orow = [kb.sb(f"orow{i}", [128, 256], F32) for i in range(2)]
        tpps = [kb.ps(f"tpps{i}", [128, 8, 128], BF16) for i in range(2)]
        NPS = 4
        mps = [kb.ps(f"mps{i}", [128, 512], F32) for i in range(NPS)]
        hps = kb.ps("hps", [128, 512], F32)

        last = {"pe": None, "act": None, "dve": None, "pool": None}

        def chain(e, ins, *deps):
            kb.wait(e, last[e], *deps)
            t = kb.mark(ins(), e)
            last[e] = t
            return t

        def pe_mark(ins):
            t = kb.mark(ins, "pe")
            last["pe"] = t
            return t

        csem = DmaSem(kb, "csem")
        xsem = DmaSem(kb, "xsem")
        wblk_free = [None] * NW
        wl_sem = [DmaSem(kb, f"wlsem{i}") for i in range(NW)]
        osem = [DmaSem(kb, f"osem{i}") for i in range(2)]
        otile_free = [None, None]
        xrsem = [DmaSem(kb, f"xrsem{i}") for i in range(2)]
        xres_free = [None, None]
        orsem = [DmaSem(kb, f"orsem{i}") for i in range(2)]
        icsem = DmaSem(kb, "icsem")
        mps_free = [None] * NPS
        hps_free = [None]
        tp_free = [None, None]
        cnt = {"piece": 0, "blk": 0, "mps": 0, "tp": 0, "tf": 0, "ot": 0, "xr": 0}

        t = chain("pool", lambda: nc.gpsimd.memset(ident_f[:], 0.0))
        t = chain("pool", lambda: nc.gpsimd.affine_select(out=ident_f[:], in_=ident_f[:], pattern=[[-1, 128]],
                                                          compare_op=ALU.not_equal, fill=1.0, base=0, channel_multiplier=1))
        t_id = chain("pool", lambda: nc.gpsimd.tensor_copy(out=ident[:], in_=ident_f[:]))
        c_all = None
        for dst, src in ((gcolS, gcol), (wpool_f, wpool), (pscS, pscale), (wcvS, wconv), (bmS, bmerge)):
            c_all = csem.inc(nc.sync.dma_start(out=dst[:], in_=src))
        chain("dve", lambda: nc.vector.memset(epsT[:], EPS))
        t_c = chain("dve", lambda: nc.vector.tensor_copy(out=wpool_b[:], in_=wpool_f[:]), c_all)
        kb.wait("pe", t_id, t_c)
        kb.wait("act", t_c)

        def norm_tile(x_rows, dst_col0, src_c0, ncols):
            kb.wait("sp", last["dve"], last["pe"])
            ld = xsem.inc(nc.sync.dma_start(out=xbuf[:], in_=x_rows))
            chain("act", lambda: nc.scalar.activation(out=xn[:], in_=xbuf[:], func=AF.Square, accum_out=ss[:]), ld, last["pe"], last["dve"])
            t_a = chain("act", lambda: nc.scalar.activation(out=rs[:], in_=ss[:], func=AF.Sqrt, scale=1.0 / D, bias=epsT[:]))
            chain("dve", lambda: nc.vector.reciprocal(out=rs[:], in_=rs[:]), t_a)
            t_xn = chain("dve", lambda: nc.vector.tensor_scalar(out=xn[:], in0=xbuf[:], scalar1=rs[:, 0:1], scalar2=None, op0=ALU.mult))
            for c8 in range(NCH // 8):
                sl = cnt["tp"] % 2
                cnt["tp"] += 1
                kb.wait("pe", t_xn, tp_free[sl])
                for j in range(8):
                    c = c8 * 8 + j
                    ins = nc.tensor.transpose(out=tpps[sl][:, j, :], in_=xn[:, c * 128:(c + 1) * 128], identity=ident[:])
                t_pe = pe_mark(ins)
                for j in range(8):
                    c = c8 * 8 + j
                    t_ev = chain("dve", lambda: nc.vector.tensor_scalar(out=hT[:, c, dst_col0:dst_col0 + ncols],
                                                                       in0=tpps[sl][:, j, src_c0:src_c0 + ncols],
                                                                       scalar1=gcolS[:, c:c + 1], scalar2=None, op0=ALU.mult), t_pe)
                tp_free[sl] = t_ev

        def load_block(view=None, col0=None):
            slot = cnt["blk"] % NW
            bi = cnt["blk"] % NBT
            cnt["blk"] += 1
            kb.wait("sp", wblk_free[slot])
            ld = wl_sem[slot].inc(nc.sync.dma_start(out=wblk[slot][:].rearrange("p c n -> p (c n)"), in_=wq[bi, :, :]))
            kb.wait("pe", ld)
            return slot

        def release(slot):
            wblk_free[slot] = last["pe"]

        def mm_feat(slot, j, halo=False):
            ms = cnt["mps"] % NPS
            cnt["mps"] += 1
            kb.wait("pe", mps_free[ms])
            for c in range(NCH):
                ins = nc.tensor.matmul(mps[ms][:, 0:T], lhsT=wblk[slot][:, c, j * 128:(j + 1) * 128], rhs=hT[:, c, HAL:TE],
                                       start=(c == 0), stop=(c == NCH - 1))
            if halo:
                kb.wait("pe", hps_free[0])
                for c in range(NCH):
                    ins = nc.tensor.matmul(hps[:, 0:HAL], lhsT=wblk[slot][:, c, j * 128:(j + 1) * 128], rhs=hT[:, c, 0:HAL],
                                           start=(c == 0), stop=(c == NCH - 1))
            return ms, pe_mark(ins)

        def get_tmpf():
            i = cnt["tf"] % NTF
            cnt["tf"] += 1
            return tmpf[i]

        pending = []

        for tt in range(NT):
            t0 = tt * T
            if tt == 0:
                norm_tile(xh[:, :], 0, 128 - HAL, HAL)
            else:
                chain("dve", lambda: nc.vector.tensor_copy(out=hT[:, :, 0:HAL], in_=hT[:, :, T:TE]), last["pe"])
            for s in range(T // 128):
                norm_tile(x[t0 + s * 128:t0 + (s + 1) * 128, :], HAL + s * 128, 0, 128)
            kb.wait("pe", last["dve"])
            kb.wait("sp", last["dve"])
            t_ic = icsem.inc(nc.sync.dma_start(out=icS[:], in_=icnt[:, :, t0:t0 + T]))
            kb.wait("dve", t_ic)

            for blk in range(32):
                slot = load_block()
                if blk < 8:
                    br = blk // 4
                    for j in range(2):
                        ch = 2 * (blk % 4) + j
                        ms, t_mm = mm_feat(slot, j)
                        tf = get_tmpf()
                        t_a = chain("act", lambda: nc.scalar.activation(out=tf[:], in_=mps[ms][:, 0:T], func=AF.Silu), t_mm, last["dve"])
                        mps_free[ms] = t_a
                        oi = cnt["ot"] % 2
                        cnt["ot"] += 1
                        kb.wait("sp", otile_free[oi])
                        t_o = osem[oi].inc(nc.sync.dma_start(out=otile[oi][:], in_=oa[br, ch * 128:(ch + 1) * 128, t0:t0 + T]))
                        t_d = chain("dve", lambda: nc.vector.tensor_tensor(out=brT[:, br * 8 + ch, :], in0=tf[:], in1=otile[oi][:], op=ALU.mult), t_a, t_o)
                        otile_free[oi] = t_d
                elif blk < 16:
                    g = (blk - 8) // 2
                    if (blk - 8) % 2 == 0:
                        for j in range(2):
                            ms, t_mm = mm_feat(slot, j, halo=True)
                            chain("act", lambda: nc.scalar.copy(out=uext[:, j, HAL:TE], in_=mps[ms][:, 0:T]), t_mm, last["dve"])
                            t_a = chain("act", lambda: nc.scalar.copy(out=uext[:, j, 0:HAL], in_=hps[:, 0:HAL]))
                            mps_free[ms] = t_a
                            hps_free[0] = t_a
                        src = uext
                        bufs = [pa, pb]
                        for k in range(g + 1):
                            sh = 2 ** k
                            dst = bufs[k % 2]
                            lo = 2 * sh - 1
                            chain("dve", lambda: nc.vector.tensor_tensor(out=dst[:, :, lo:TE], in0=src[:, :, lo:TE], in1=src[:, :, lo - sh:TE - sh], op=ALU.add), last["act"])
                            src = dst
                        for j in range(2):
                            tf = get_tmpf()
                            chain("dve", lambda: nc.vector.tensor_tensor(out=tf[:], in0=src[:, j, HAL:TE], in1=icS[:, g, :], op=ALU.mult), last["act"])
                            t_p = chain("dve", lambda: nc.vector.tensor_tensor(out=pooledT[:, j, :], in0=tf[:], in1=uext[:, j, HAL:TE], op=ALU.subtract), last["pe"])
                        for oc in range(2):
                            ms = cnt["mps"] % NPS
                            cnt["mps"] += 1
                            kb.wait("pe", mps_free[ms], t_p)
                            for j in range(2):
                                ins = nc.tensor.matmul(mps[ms][:, 0:T], lhsT=wpool_b[:, 2 * g + j, oc * 128:(oc + 1) * 128], rhs=pooledT[:, j, :],
                                                       start=(j == 0), stop=(j == 1))
                            t_mm = pe_mark(ins)
                            t_y = chain("dve", lambda: nc.vector.tensor_scalar(out=yS[:, oc, :], in0=mps[ms][:, 0:T], scalar1=pscS[:, 2 * g + oc:2 * g + oc + 1],
                                                                              scalar2=None, op0=ALU.mult), t_mm)
                            mps_free[ms] = t_y
                    else:
                        for j in range(2):
                            ms, t_mm = mm_feat(slot, j)
                            tf = get_tmpf()
                            t_a = chain("act", lambda: nc.scalar.activation(out=tf[:], in_=mps[ms][:, 0:T], func=AF.Silu), t_mm, last["dve"])
                            mps_free[ms] = t_a
                            chain("dve", lambda: nc.vector.tensor_tensor(out=brT[:, 16 + 2 * g + j, :], in0=tf[:], in1=yS[:, j, :], op=ALU.mult), t_a)
                else:
                    i = (blk - 16) // 2
                    if (blk - 16) % 2 == 0:
                        ms, t_mm = mm_feat(slot, 0)
                        t_a = chain("act", lambda: nc.scalar.copy(out=bS[:], in_=mps[ms][:, 0:T]), t_mm, last["dve"])
                        mps_free[ms] = t_a
                        ms, t_mm = mm_feat(slot, 1, halo=True)
                        chain("act", lambda: nc.scalar.copy(out=cext[:, HAL:TE], in_=mps[ms][:, 0:T]), t_mm, last["dve"])
                        t_a = chain("act", lambda: nc.scalar.copy(out=cext[:, 0:HAL], in_=hps[:, 0:HAL]))
                        mps_free[ms] = t_a
                        hps_free[0] = t_a
                    else:
                        ms, t_mm = mm_feat(slot, 0, halo=True)
                        chain("dve", lambda: nc.vector.tensor_tensor(out=zext[:, HAL:TE], in0=mps[ms][:, 0:T], in1=cext[:, HAL:TE], op=ALU.mult), t_mm, last["act"])
                        t_d = chain("dve", lambda: nc.vector.tensor_tensor(out=zext[:, 0:HAL], in0=hps[:, 0:HAL], in1=cext[:, 0:HAL], op=ALU.mult))
                        mps_free[ms] = t_d
                        hps_free[0] = t_d
                        chain("dve", lambda: nc.vector.tensor_scalar(out=ycv[:], in0=zext[:, HAL - 2:TE - 2], scalar1=wcvS[:, 0, i:i + 1], scalar2=None, op0=ALU.mult))
                        chain("dve", lambda: nc.vector.scalar_tensor_tensor(out=ycv[:], in0=zext[:, HAL - 1:TE - 1], scalar=wcvS[:, 1, i:i + 1], in1=ycv[:],
                                                                           op0=ALU.mult, op1=ALU.add))
                        chain("dve", lambda: nc.vector.scalar_tensor_tensor(out=ycv[:], in0=zext[:, HAL:TE], scalar=wcvS[:, 2, i:i + 1], in1=ycv[:],
                                                                           op0=ALU.mult, op1=ALU.add))
                        chain("dve", lambda: nc.vector.tensor_tensor(out=ycv[:], in0=ycv[:], in1=bS[:], op=ALU.mult))
                        ms, t_mm = mm_feat(slot, 1)
                        tf = get_tmpf()
                        t_a = chain("act", lambda: nc.scalar.activation(out=tf[:], in_=mps[ms][:, 0:T], func=AF.Silu), t_mm, last["dve"])
                        mps_free[ms] = t_a
                        chain("dve", lambda: nc.vector.tensor_tensor(out=brT[:, 24 + i, :], in0=tf[:], in1=ycv[:], op=ALU.mult), t_a)
                release(slot)

            for dp in range(NCH // 2):
                for mb in range(4):
                    slot = load_block()
                    dcl = mb // 2
                    dc = 2 * dp + dcl
                    for j in range(2):
                        i = 2 * (mb % 2) + j
                        ms, t_mm = mm_feat(slot, j)
                        t_a = chain("act", lambda: nc.scalar.activation(out=sgT[:, i, dcl, :], in_=mps[ms][:, 0:T], func=AF.Sigmoid,
                                                                       bias=bmS[:, i, dc:dc + 1]), t_mm, last["dve"])
                        mps_free[ms] = t_a
                    release(slot)
                slot = load_block()
                kb.wait("pe", last["dve"])
                for dcl in range(2):
                    dc = 2 * dp + dcl
                    for i in range(4):
                        ms = cnt["mps"] % NPS
                        cnt["mps"] += 1
                        kb.wait("pe", mps_free[ms])
                        for wcn in range(8):
                            ins = nc.tensor.matmul(mps[ms][:, 0:T], lhsT=wblk[slot][:, 8 * i + wcn, dcl * 128:(dcl + 1) * 128], rhs=brT[:, 8 * i + wcn, :],
                                                   start=(wcn == 0), stop=(wcn == 7))
                        t_mm = pe_mark(ins)
                        if i == 0:
                            t_d = chain("dve", lambda: nc.vector.tensor_tensor(out=accm[:], in0=mps[ms][:, 0:T], in1=sgT[:, i, dcl, :], op=ALU.mult), t_mm, last["act"])
                        else:
                            tf = get_tmpf()
                            t_d = chain("dve", lambda: nc.vector.tensor_tensor(out=tf[:], in0=mps[ms][:, 0:T], in1=sgT[:, i, dcl, :], op=ALU.mult), t_mm, last["act"])
                            if i < 3:
                                chain("dve", lambda: nc.vector.tensor_tensor(out=accm[:], in0=accm[:], in1=tf[:], op=ALU.add))
                            else:
                                chain("dve", lambda: nc.vector.tensor_tensor(out=mgT[:, dc, :], in0=accm[:], in1=tf[:], op=ALU.add), last["pe"])
                        mps_free[ms] = t_d
                release(slot)

            kb.wait("pe", last["dve"])
            for ob in range(16):
                slot = load_block()
                for st_ in pending:
                    st_()
                pending = []
                for s in range(T // 128):
                    ms = cnt["mps"] % NPS
                    cnt["mps"] += 1
                    kb.wait("pe", mps_free[ms])
                    for c in range(NCH):
                        ins = nc.tensor.matmul(mps[ms][:, 0:256], lhsT=mgT[:, c, s * 128:(s + 1) * 128], rhs=wblk[slot][:, c, :],
                                               start=(c == 0), stop=(c == NCH - 1))
                    t_mm = pe_mark(ins)
                    xi = cnt["xr"] % 2
                    cnt["xr"] += 1
                    kb.wait("sp", xres_free[xi])
                    r0 = t0 + s * 128
                    t_x = xrsem[xi].inc(nc.sync.dma_start(out=xres[xi][:], in_=x[r0:r0 + 128, ob * 256:(ob + 1) * 256]))
                    t_d = chain("dve", lambda: nc.vector.tensor_tensor(out=orow[xi][:], in0=mps[ms][:, 0:256], in1=xres[xi][:], op=ALU.add),
                                t_mm, t_x, orsem[xi].tok())
                    mps_free[ms] = t_d
                    xres_free[xi] = t_d

                    def mk_store(xi=xi, r0=r0, ob=ob, t_d=t_d):
                        kb.wait("sp", t_d)
                        orsem[xi].inc(nc.sync.dma_start(out=y[r0:r0 + 128, ob * 256:(ob + 1) * 256], in_=orow[xi][:]))
                    mk_store()
                release(slot)
        for s_ in orsem:
            kb.wait("sp", s_.tok())
    return nc


def prep_c_consts(w_in, w_pool, pool_scale, w_conv, w_branch, b_merge, w_out, g):
    def blk(mat, cols):
        return mat[:, cols].reshape(NCH, 128, 256).transpose(1, 0, 2).reshape(128, NCH * 256)
    blocks = []
    for b in range(4):
        blocks.append(blk(w_in, np.arange(3072 + b * 256, 3072 + (b + 1) * 256)))
    for b in range(4):
        blocks.append(blk(w_in, np.arange(7168 + b * 256, 7168 + (b + 1) * 256)))
    for gg in range(4):
        blocks.append(blk(w_in, np.arange(8200 + gg * 256, 8200 + (gg + 1) * 256)))
        blocks.append(blk(w_in, np.arange(9224 + gg * 256, 9224 + (gg + 1) * 256)))
    for i in range(8):
        blocks.append(blk(w_in, np.concatenate([np.arange(10248 + i * 128, 10248 + (i + 1) * 128), np.arange(11272 + i * 128, 11272 + (i + 1) * 128)])))
        blocks.append(blk(w_in, np.concatenate([np.arange(12296 + i * 128, 12296 + (i + 1) * 128), np.arange(13320 + i * 128, 13320 + (i + 1) * 128)])))
    wbr = w_branch.reshape(D, D)
    for dp in range(16):
        for mb in range(4):
            dc = 2 * dp + mb // 2
            cc = []
            for j in range(2):
                i = 2 * (mb % 2) + j
                cc.append(np.arange(14344 + i * 4096 + dc * 128, 14344 + i * 4096 + (dc + 1) * 128))
            blocks.append(blk(w_in, np.concatenate(cc)))
        blocks.append(blk(wbr, np.arange(dp * 256, (dp + 1) * 256)))
    for ob in range(16):
        blocks.append(blk(w_out, np.arange(ob * 256, (ob + 1) * 256)))
    assert len(blocks) == 128
    consts = {
        "gcol": np.ascontiguousarray(g.reshape(NCH, 128).T),
        "wpool": np.ascontiguousarray(w_pool.reshape(4, 2, 128, 256).transpose(2, 0, 1, 3).reshape(128, 8, 256)),
        "pscale": np.ascontiguousarray(pool_scale.reshape(8, 128).T),
        "wconv": np.ascontiguousarray(w_conv.reshape(3, 8, 128).transpose(2, 0, 1)),
        "bmerge": np.ascontiguousarray(b_merge.reshape(4, NCH, 128).transpose(2, 0, 1)),
    }
    return consts, np.stack(blocks, axis=0)


def icnt_table(pos0, n):
    pos = np.arange(pos0, pos0 + n)
    tab = np.stack([1.0 / np.minimum(pos + 1, w) for w in (2, 4, 8, 16)], axis=0).astype(np.float32)
    return np.ascontiguousarray(np.broadcast_to(tab[None], (128, 4, n)))


def run_phase_c(h, oo, consts, wq):
    nc = get_prog("c")
    in_maps = []
    for c in range(NCORES):
        b = c // (NCORES // BATCH)
        off = (c % (NCORES // BATCH)) * TPC
        r0 = c * TPC
        xh = np.zeros((128, D), np.float32) if off == 0 else np.ascontiguousarray(h[r0 - 128:r0])
        oa = np.ascontiguousarray(oo[:, b, :, :, off:off + TPC].transpose(1, 0, 2, 3).reshape(2, W, TPC))
        m = dict(consts)
        m.update({"x": np.ascontiguousarray(h[r0:r0 + TPC]), "xh": xh, "oa": oa, "icnt": icnt_table(off, TPC), "wq": wq})
        in_maps.append(m)
    res = run_bass_kernel_spmd(nc, in_maps, core_ids=list(range(NCORES)))
    return np.concatenate([r["y"] for r in res.results], axis=0)


def build_phase_d(TPC=TPC):
    nc = bass.Bass("TRN2", target_bir_lowering=False)
    x = nc.dram_tensor("x", [TPC, D], F32, kind="ExternalInput").ap()
    g = nc.dram_tensor("g", [128, D], F32, kind="ExternalInput").ap()
    y = nc.dram_tensor("y", [TPC, D], F32, kind="ExternalOutput").ap()
    with ExitStack() as es:
        kb = KB(nc, es)
        grep = kb.sb("grep", [128, D], F32)
        NB_ = 2
        xb = [kb.sb(f"xb{i}", [128, D], F32) for i in range(NB_)]
        yb = [kb.sb(f"yb{i}", [128, D], F32) for i in range(NB_)]
        junk = kb.sb("junk", [128, D], BF16)
        ss = kb.sb("ss", [128, 1], F32)
        rs = kb.sb("rs", [128, 1], F32)
        epsT = kb.sb("epsT", [128, 1], F32)
        last = {"act": None, "dve": None}

        def chain(e, ins, *deps):
            kb.wait(e, last[e], *deps)
            t = kb.mark(ins(), e)
            last[e] = t
            return t

        csem = DmaSem(kb, "csem")
        xs = [DmaSem(kb, f"xs{i}") for i in range(NB_)]
        ys = [DmaSem(kb, f"ys{i}") for i in range(NB_)]
        xfree = [None] * NB_
        c1 = csem.inc(nc.sync.dma_start(out=grep[:], in_=g[:, :]))
        t_e = chain("dve", lambda: nc.vector.memset(epsT[:], EPS), c1)
        kb.wait("act", t_e)
        for i in range(TPC // 128):
            s_ = i % NB_
            kb.wait("sp", xfree[s_])
            ld = xs[s_].inc(nc.sync.dma_start(out=xb[s_][:], in_=x[i * 128:(i + 1) * 128, :]))
            chain("act", lambda: nc.scalar.activation(out=junk[:], in_=xb[s_][:], func=AF.Square, accum_out=ss[:]), ld, last["dve"])
            t_a = chain("act", lambda: nc.scalar.activation(out=rs[:], in_=ss[:], func=AF.Sqrt, scale=1.0 / D, bias=epsT[:]))
            chain("dve", lambda: nc.vector.reciprocal(out=rs[:], in_=rs[:]), t_a)
            t_y = chain("dve", lambda: nc.vector.scalar_tensor_tensor(out=yb[s_][:], in0=xb[s_][:], scalar=rs[:, 0:1], in1=grep[:],
                                                                     op0=ALU.mult, op1=ALU.mult), ys[s_].tok())
            xfree[s_] = t_y
            kb.wait("sp", t_y)
            ys[s_].inc(nc.sync.dma_start(out=y[i * 128:(i + 1) * 128, :], in_=yb[s_][:]))
        for s_ in ys:
            kb.wait("sp", s_.tok())
    return nc


def run_phase_d(h, final_g):
    nc = get_prog("d")
    grep = np.ascontiguousarray(np.broadcast_to(final_g[None, :], (128, D)))
    in_maps = [{"x": np.ascontiguousarray(h[c * TPC:(c + 1) * TPC]), "g": grep} for c in range(NCORES)]
    res = run_bass_kernel_spmd(nc, in_maps, core_ids=list(range(NCORES)))
    return np.concatenate([r["y"] for r in res.results], axis=0)


WCH = 4096


def build_cast(NP):
    nc = bass.Bass("TRN2", target_bir_lowering=False)
    x = nc.dram_tensor("x", [NP, 128, WCH], F32, kind="ExternalInput").ap()
    y = nc.dram_tensor("y", [NP, 128, WCH], BF16, kind="ExternalOutput").ap()
    with ExitStack() as es:
        kb = KB(nc, es)
        NB_ = 4
        xb = [kb.sb(f"xb{i}", [128, WCH], F32) for i in range(NB_)]
        yb = [kb.sb(f"yb{i}", [128, WCH], BF16) for i in range(NB_)]
        xs = [DmaSem(kb, f"xs{i}") for i in range(NB_)]
        ys = [DmaSem(kb, f"ys{i}") for i in range(NB_)]
        xfree = [None] * NB_
        engs = ["dve", "pool", "act"]
        for i in range(NP):
            s_ = i % NB_
            e = engs[i % 3]
            kb.wait("sp", xfree[s_])
            ld = xs[s_].inc(nc.sync.dma_start(out=xb[s_][:], in_=x[i, :, :]))
            kb.wait(e, ld, ys[s_].tok())
            if e == "dve":
                ins = nc.vector.tensor_copy(out=yb[s_][:], in_=xb[s_][:])
            elif e == "pool":
                ins = nc.gpsimd.tensor_copy(out=yb[s_][:], in_=xb[s_][:])
            else:
                ins = nc.scalar.copy(out=yb[s_][:], in_=xb[s_][:])
            t = kb.mark(ins, e)
            xfree[s_] = t
            kb.wait("sp", t)
            ys[s_].inc(nc.sync.dma_start(out=y[i, :, :], in_=yb[s_][:]))
        for s_ in ys:
            kb.wait("sp", s_.tok())
    return nc


def run_cast(arrays):
    sizes = [a.size for a in arrays]
    total = sum(sizes)
    unit = NCORES * 128 * WCH
    npc = -(-total // unit)
    flat = np.zeros(npc * unit, np.float32)
    off = 0
    for a in arrays:
        flat[off:off + a.size] = a.reshape(-1)
        off += a.size
    flat = flat.reshape(NCORES, npc, 128, WCH)
    key = ("w", npc)
    if key not in _CACHE:
        _CACHE[key] = build_cast(npc)
    res = run_bass_kernel_spmd(_CACHE[key], [{"x": flat[c]} for c in range(NCORES)], core_ids=list(range(NCORES)))
    out = np.concatenate([np.asarray(r["y"]).reshape(-1) for r in res.results])
    outs = []
    off = 0
    for a in arrays:
        outs.append(out[off:off + a.size].reshape(a.shape))
        off += a.size
    return outs


def kernel(x, norm_g, w_in, b_forget, w_pool, pool_scale, w_conv, w_branch, b_merge, w_out, final_g):
    f32 = lambda a: np.asarray(a, dtype=np.float32)
    x, norm_g, w_in, b_forget, w_pool, pool_scale, w_conv, w_branch, b_merge, w_out, final_g = map(
        f32, (x, norm_g, w_in, b_forget, w_pool, pool_scale, w_conv, w_branch, b_merge, w_out, final_g))
    cc, fl = [], []
    for l in range(2):
        pa = prep_a_f32(w_in[l])
        consts, wblocks = prep_c_consts(w_in[l], w_pool[l], pool_scale[l], w_conv[l], w_branch[l], b_merge[l], w_out[l], norm_g[l])
        cc.append(consts)
        fl += [pa["wa"], pa["wf"], wblocks]
    q = run_cast(fl)
    del fl
    h = np.ascontiguousarray(x.reshape(TOK, D))
    for l in range(2):
        wq_a = {"wa": q[3 * l], "wf": q[3 * l + 1]}
        qk, v, fo = run_phase_a(h, norm_g[l], wq_a)
        oo = run_phase_b(qk, v, fo, b_forget[l])
        del qk, v, fo
        h = run_phase_c(h, oo, cc[l], q[3 * l + 2])
        del oo
    out = run_phase_d(h, final_g)
    return out.reshape(BATCH, SEQ, D).astype(np.float32)
```

```python
import numpy as np
from contextlib import ExitStack
import concourse.bass as bass
import concourse.mybir as mybir
from concourse.bass_utils import run_bass_kernel_spmd

F32 = mybir.dt.float32
BF16 = mybir.dt.bfloat16
AF = mybir.ActivationFunctionType
ALU = mybir.AluOpType
AX = mybir.AxisListType

NCORES = 8
D = 4096
NCH = D // 128
SEQ = 8192
BATCH = 2
TOK = BATCH * SEQ
TPC = TOK // NCORES
TT = 512
W = 1024
NH = 8
EPS = 1e-6
SCALE = 128 ** -0.5
NEG = -60000.0


class KB:
    def __init__(self, nc, es):
        self.nc = nc
        self.es = es
        self.eng = {"pe": nc.tensor, "act": nc.scalar, "dve": nc.vector, "pool": nc.gpsimd, "sp": nc.sync}
        self.psem = {}
        self.pcnt = {}
        for e in ("pe", "act", "dve", "pool"):
            self.psem[e] = es.enter_context(nc.semaphore("prog_" + e))
            self.pcnt[e] = 0
        self.waited = {}
        self.nsem = 0

    def sb(self, name, shape, dt):
        return self.es.enter_context(self.nc.sbuf_tensor(name, shape, dt))

    def ps(self, name, shape, dt):
        return self.es.enter_context(self.nc.psum_tensor(name, shape, dt))

    def sem(self, name):
        self.nsem += 1
        return self.es.enter_context(self.nc.semaphore(name))

    def mark(self, instr, e):
        self.pcnt[e] += 1
        instr.then_inc(self.psem[e], 1)
        return (self.psem[e], self.pcnt[e], e)

    def wait(self, e, *toks):
        flat = []
        for tok in toks:
            if isinstance(tok, list):
                flat.extend(tok)
            else:
                flat.append(tok)
        for tok in flat:
            if tok is None:
                continue
            sem, val, src = tok
            key = (e, id(sem))
            if self.waited.get(key, 0) >= val:
                continue
            self.waited[key] = val
            self.eng[e].wait_ge(sem, val)


class DmaSem:
    def __init__(self, kb, name):
        self.sem = kb.sem(name)
        self.val = 0

    def inc(self, instr, n=1):
        self.val += 16
        instr.then_inc(self.sem, 16)
        return (self.sem, self.val, "dma")

    def tok(self):
        return (self.sem, self.val, "dma")


def emit_norm_tile(kb, x_rows, grep, xbuf, xn, ss, rs, hT, tpps, tok0, st):
    nc = kb.nc
    kb.wait("sp", st.get("xbuf_free"))
    ld = st["xsem"].inc(nc.sync.dma_start(out=xbuf[:], in_=x_rows))
    kb.wait("act", ld, st.get("xn_free"))
    t_sq = kb.mark(nc.scalar.activation(out=xn[:], in_=xbuf[:], func=AF.Square, accum_out=ss[:]), "act")
    kb.wait("act", t_sq)
    t_act = kb.mark(nc.scalar.activation(out=rs[:], in_=ss[:], func=AF.Sqrt, scale=1.0 / D, bias=st["eps"][:]), "act")
    kb.wait("dve", t_act)
    t_rc = kb.mark(nc.vector.reciprocal(out=rs[:], in_=rs[:]), "dve")
    kb.wait("dve", t_rc)
    t_xn = kb.mark(nc.vector.scalar_tensor_tensor(out=xn[:], in0=xbuf[:], scalar=rs[:, 0:1], in1=grep[:],
                                                  op0=ALU.mult, op1=ALU.mult), "dve")
    st["xbuf_free"] = t_xn
    kb.wait("pe", t_xn)
    last_pe = None
    for c8 in range(NCH // 8):
        slot = st["tp_i"] % len(tpps)
        st["tp_i"] += 1
        kb.wait("pe", st["tp_free"][slot])
        for j in range(8):
            c = c8 * 8 + j
            ins = nc.tensor.transpose(out=tpps[slot][:, j, :], in_=xn[:, c * 128:(c + 1) * 128], identity=st["ident"][:])
        t_pe = kb.mark(ins, "pe")
        last_pe = t_pe
        kb.wait("dve", t_pe, st.get("hT_free"))
        t_ev = kb.mark(nc.vector.tensor_copy(out=hT[:, c8 * 8:(c8 + 1) * 8, tok0:tok0 + 128], in_=tpps[slot][:, :, :]), "dve")
        st["tp_free"][slot] = t_ev
        st["hT_ready"] = t_ev
    st["xn_free"] = last_pe


def make_ident(kb, ident_f, ident):
    nc = kb.nc
    t = kb.mark(nc.gpsimd.memset(ident_f[:], 0.0), "pool")
    kb.wait("pool", t)
    t = kb.mark(nc.gpsimd.affine_select(out=ident_f[:], in_=ident_f[:], pattern=[[-1, 128]], compare_op=ALU.not_equal,
                                        fill=1.0, base=0, channel_multiplier=1), "pool")
    kb.wait("pool", t)
    return kb.mark(nc.gpsimd.tensor_copy(out=ident[:], in_=ident_f[:]), "pool")


def build_phase_a(TPC=TPC):
    nc = bass.Bass("TRN2", target_bir_lowering=False)
    x = nc.dram_tensor("x", [TPC, D], F32, kind="ExternalInput").ap()
    g = nc.dram_tensor("g", [128, D], F32, kind="ExternalInput").ap()
    wa = nc.dram_tensor("wa", [12, 128, NCH * 512], BF16, kind="ExternalInput").ap()
    wf = nc.dram_tensor("wf", [128, NCH * NH], BF16, kind="ExternalInput").ap()
    qk = nc.dram_tensor("qk", [4, NH, 128, TPC], F32, kind="ExternalOutput").ap()
    v = nc.dram_tensor("v", [2, TPC, W], F32, kind="ExternalOutput").ap()
    fo = nc.dram_tensor("fo", [NH, TPC], F32, kind="ExternalOutput").ap()
    with ExitStack() as es:
        kb = KB(nc, es)
        grep = kb.sb("grep", [128, D], F32)
        xbuf = kb.sb("xbuf", [128, D], F32)
        xn = kb.sb("xn", [128, D], BF16)
        ss = kb.sb("ss", [128, 1], F32)
        rs = kb.sb("rs", [128, 1], F32)
        epsT = kb.sb("epsT", [128, 1], F32)
        ident_f = kb.sb("ident_f", [128, 128], F32)
        ident = kb.sb("ident", [128, 128], BF16)
        hT = kb.sb("hT", [128, NCH, TT], BF16)
        NWA = 3
        wblk = [kb.sb(f"wblk{i}", [128, NCH, 512], BF16) for i in range(NWA)]
        wf_b = kb.sb("wf_b", [128, NCH, NH], BF16)
        NOST = 4
        ost = [kb.sb(f"ost{i}", [128, 512], F32) for i in range(NOST)]
        tpps = [kb.ps(f"tpps{i}", [128, 8, 128], BF16) for i in range(2)]
        NPS = 4
        mps = [kb.ps(f"mps{i}", [128, 512], F32) for i in range(NPS)]

        st = {"xsem": DmaSem(kb, "xsem"), "tp_i": 0, "tp_free": [None, None], "ident": ident, "eps": epsT}
        csem = DmaSem(kb, "csem")
        wl_sem = [DmaSem(kb, f"wlsem{i}") for i in range(NWA)]
        wblk_free = [None] * NWA
        ost_sem = [DmaSem(kb, f"ostsem{i}") for i in range(NOST)]
        mps_free = [None] * NPS
        fin = []

        t_id = make_ident(kb, ident_f, ident)
        t_eps = kb.mark(nc.vector.memset(epsT[:], EPS), "dve")
        c1 = csem.inc(nc.sync.dma_start(out=grep[:], in_=g[:, :]))
        c2 = csem.inc(nc.sync.dma_start(out=wf_b[:].rearrange("p c n -> p (c n)"), in_=wf[:, :]))
        kb.wait("dve", c2)
        kb.wait("pe", t_id, c2)
        kb.wait("act", t_eps)

        piece_i = 0
        blk_i = 0
        mps_i = 0
        ost_i = 0
        last_mm = None
        for tt in range(TPC // TT):
            st["hT_free"] = last_mm
            for s in range(TT // 128):
                r0 = tt * TT + s * 128
                emit_norm_tile(kb, x[r0:r0 + 128, :], grep, xbuf, xn, ss, rs, hT, tpps, s * 128, st)
            kb.wait("pe", st["hT_ready"])
            ms = mps_i % NPS
            mps_i += 1
            kb.wait("pe", mps_free[ms])
            for c in range(NCH):
                ins = nc.tensor.matmul(mps[ms][0:NH, :], lhsT=wf_b[:, c, :], rhs=hT[:, c, :], start=(c == 0), stop=(c == NCH - 1))
            t_mm = kb.mark(ins, "pe")
            os_ = ost_i % NOST
            ost_i += 1
            kb.wait("act", t_mm, ost_sem[os_].tok())
            t_ev = kb.mark(nc.scalar.copy(out=ost[os_][0:NH, :], in_=mps[ms][0:NH, :]), "act")
            mps_free[ms] = t_ev
            kb.wait("act", t_ev)
            ost_sem[os_].inc(nc.scalar.dma_start(out=fo[:, tt * TT:(tt + 1) * TT], in_=ost[os_][0:NH, :]))
            for blk in range(12):
                kind = blk // 2
                half = blk % 2
                wslot = blk_i % NWA
                blk_i += 1
                kb.wait("sp", wblk_free[wslot])
                t_cast = wl_sem[wslot].inc(nc.sync.dma_start(out=wblk[wslot][:].rearrange("p c n -> p (c n)"), in_=wa[blk, :, :]))
                kb.wait("pe", t_cast)
                for j in range(4):
                    ms = mps_i % NPS
                    mps_i += 1
                    kb.wait("pe", mps_free[ms])
                    for c in range(NCH):
                        if kind in (2, 5):
                            ins = nc.tensor.matmul(mps[ms][:], lhsT=hT[:, c, j * 128:(j + 1) * 128], rhs=wblk[wslot][:, c, :],
                                                   start=(c == 0), stop=(c == NCH - 1))
                        else:
                            ins = nc.tensor.matmul(mps[ms][:], lhsT=wblk[wslot][:, c, j * 128:(j + 1) * 128], rhs=hT[:, c, :],
                                                   start=(c == 0), stop=(c == NCH - 1))
                    t_mm = kb.mark(ins, "pe")
                    last_mm = t_mm
                    os_ = ost_i % NOST
                    ost_i += 1
                    kb.wait("act", t_mm, ost_sem[os_].tok())
                    t_ev = kb.mark(nc.scalar.copy(out=ost[os_][:], in_=mps[ms][:]), "act")
                    mps_free[ms] = t_ev
                    if kind in (2, 5):
                        dst = v[kind // 3, tt * TT + j * 128: tt * TT + (j + 1) * 128, half * 512:(half + 1) * 512]
                    else:
                        which = {0: 0, 1: 1, 3: 2, 4: 3}[kind]
                        dst = qk[which, half * 4 + j, :, tt * TT:(tt + 1) * TT]
                    kb.wait("act", t_ev)
                    ost_sem[os_].inc(nc.scalar.dma_start(out=dst, in_=ost[os_][:]))
                wblk_free[wslot] = last_mm
        for s_ in ost_sem:
            kb.wait("act", s_.tok())
    return nc


_CACHE = {}


def get_prog(name):
    if name not in _CACHE:
        _CACHE[name] = {"a": build_phase_a, "b": build_phase_b, "c": build_phase_c, "d": build_phase_d}[name]()
    return _CACHE[name]


def prep_a_f32(w_in):
    cols = np.concatenate([np.arange(0, 3 * W), np.arange(4 * W, 7 * W)])
    wa = w_in[:, cols].reshape(NCH, 128, 12, 512).transpose(2, 1, 0, 3).reshape(12, 128, NCH * 512)
    wf = w_in[:, 8 * W:8 * W + NH].reshape(NCH, 128, NH).transpose(1, 0, 2).reshape(128, NCH * NH)
    return {"wa": np.ascontiguousarray(wa), "wf": np.ascontiguousarray(wf)}


def run_phase_a(x_flat, g, wq):
    nc = get_prog("a")
    grep = np.ascontiguousarray(np.broadcast_to(g[None, :], (128, D)))
    in_maps = [{"x": np.ascontiguousarray(x_flat[c * TPC:(c + 1) * TPC]), "g": grep, "wa": wq["wa"], "wf": wq["wf"]} for c in range(NCORES)]
    res = run_bass_kernel_spmd(nc, in_maps, core_ids=list(range(NCORES)))
    qk = np.concatenate([r["qk"] for r in res.results], axis=3)
    v = np.concatenate([r["v"] for r in res.results], axis=1)
    fo = np.concatenate([r["fo"] for r in res.results], axis=1)
    return qk, v, fo


def build_phase_b(SEQ=SEQ, NB=BATCH):
    nc = bass.Bass("TRN2", target_bir_lowering=False)
    NKC = SEQ // 128
    NQT = SEQ // 512
    NBLK = SEQ // 256
    NPC = SEQ // 2048
    qk = nc.dram_tensor("qk", [NB, 4, 128, SEQ], F32, kind="ExternalInput").ap()
    vv = nc.dram_tensor("vv", [NB, 2, SEQ, 128], F32, kind="ExternalInput").ap()
    ff = nc.dram_tensor("ff", [NB, 1, SEQ], F32, kind="ExternalInput").ap()
    bf = nc.dram_tensor("bf", [1, 1], F32, kind="ExternalInput").ap()
    oo = nc.dram_tensor("oo", [NB, 2, 128, SEQ], F32, kind="ExternalOutput").ap()
    with ExitStack() as es:
        kb = KB(nc, es)
        qTs = [kb.sb(f"qT{i}", [128, SEQ], BF16) for i in range(2)]
        kTs = [kb.sb(f"kT{i}", [128, SEQ], BF16) for i in range(2)]
        vSs = [kb.sb(f"vS{i}", [128, NKC, 128], BF16) for i in range(2)]
        crep = kb.sb("crep", [128, SEQ], F32)
        negc = kb.sb("negc", [128, NKC], F32)
        NSTG = 2
        stg = [kb.sb(f"stg{i}", [128, 2048], F32) for i in range(NSTG)]
        biasT = kb.sb("biasT", [32, SEQ], BF16)
        gate_all = kb.sb("gate_all", [128, SEQ // 128, 32], F32)
        ksum = kb.sb("ksum", [128, 32], F32)
        gate_m = kb.sb("gate_m", [128, 2, 32], F32)
        top8 = kb.sb("top8", [128, 2, 8], F32)
        selb = kb.sb("selb", [128, 2, 32], F32)
        E_all = kb.sb("E_all", [32, 32, 128], BF16)
        ident_f = kb.sb("ident_f", [128, 128], F32)
        ident = kb.sb("ident", [128, 128], BF16)
        tri_f = kb.sb("tri_f", [128, 128], F32)
        tri = kb.sb("tri", [128, 128], BF16)
        ones_b = kb.sb("ones_b", [128, 128], BF16)
        onesrow = kb.sb("onesrow", [1, 128], F32)
        negone = kb.sb("negone", [1, 2], F32)
        one1 = kb.sb("one1", [1, 1], F32)
        negb = kb.sb("negb", [1, 1], F32)
        lrow = kb.sb("lrow", [1, 2048], F32)
        carry = kb.sb("carry", [1, 1], F32)
        NPT = 5
        PT = [kb.sb(f"PT{i}", [128, 512], BF16) for i in range(NPT)]
        tmpS = [kb.sb(f"tmpS{i}", [128, 512], F32) for i in range(NPT)]
        NOS = 2
        oS = [kb.sb(f"oS{i}", [128, 512], F32) for i in range(NOS)]
        Sps = [kb.ps(f"Sps{i}", [128, 512], F32) for i in range(NPT)]
        accps = kb.ps("accps", [128, 512], F32)
        denps = kb.ps("denps", [128, 512], F32)
        mscps = [kb.ps(f"mscps{i}", [128, 512], F32) for i in range(1)]

        ldsem = [DmaSem(kb, f"ldsem{i}") for i in range(NSTG)]
        csem = DmaSem(kb, "csem")
        osem = [DmaSem(kb, f"osem{i}") for i in range(NOS)]
        stg_free = [None] * NSTG
        st = {"stg_i": 0, "pt_i": 0, "os_i": 0, "msc_i": 0}
        S_free = [None] * NPT
        PT_free = [None] * NPT
        tmp_free = [None] * NPT
        msc_free = [None]
        last = {"pe": None, "act": None, "dve": None, "pool": None}

        def chain(e, ins, *deps):
            kb.wait(e, last[e], *deps)
            t = kb.mark(ins(), e)
            last[e] = t
            return t

        t = chain("pool", lambda: nc.gpsimd.memset(ident_f[:], 0.0))
        t = chain("pool", lambda: nc.gpsimd.affine_select(out=ident_f[:], in_=ident_f[:], pattern=[[-1, 128]],
                                                          compare_op=ALU.not_equal, fill=1.0, base=0, channel_multiplier=1))
        t = chain("pool", lambda: nc.gpsimd.tensor_copy(out=ident[:], in_=ident_f[:]))
        t = chain("pool", lambda: nc.gpsimd.memset(tri_f[:], 0.0))
        t = chain("pool", lambda: nc.gpsimd.affine_select(out=tri_f[:], in_=tri_f[:], pattern=[[1, 128]],
                                                          compare_op=ALU.is_ge, fill=NEG, base=0, channel_multiplier=-1))
        t = chain("pool", lambda: nc.gpsimd.tensor_copy(out=tri[:], in_=tri_f[:]))
        t = chain("pool", lambda: nc.gpsimd.memset(E_all[:], 0.0))
        t = chain("pool", lambda: nc.gpsimd.affine_select(out=E_all[:], in_=E_all[:], pattern=[[-1, 32], [0, 128]],
                                                          compare_op=ALU.not_equal, fill=1.0, base=0, channel_multiplier=1))
        t = chain("pool", lambda: nc.gpsimd.memset(ones_b[:], 1.0))
        t = chain("pool", lambda: nc.gpsimd.memset(onesrow[:], 1.0))
        t = chain("pool", lambda: nc.gpsimd.memset(negone[:], -1.0))
        t = chain("pool", lambda: nc.gpsimd.memset(one1[:], 1.0))
        t_const = chain("pool", lambda: nc.gpsimd.memset(carry[:], 0.0))
        cb = csem.inc(nc.sync.dma_start(out=negb[:], in_=bf[:, :]))
        t_negb = chain("dve", lambda: nc.vector.tensor_scalar(out=negb[:], in0=negb[:], scalar1=-1.0, scalar2=None, op0=ALU.mult), cb)
        chain("dve", lambda: nc.vector.memset(ksum[:], 0.0))
        chain("dve", lambda: nc.vector.memset(gate_all[:], 0.0))
        kb.wait("pe", t_const)
        kb.wait("act", t_const)
        kb.wait("dve", t_const)

        def load_piece(src_ap, shape_view=None):
            s_ = st["stg_i"] % NSTG
            st["stg_i"] += 1
            kb.wait("sp", stg_free[s_])
            dst = stg[s_][:] if shape_view is None else shape_view(stg[s_])
            tok = ldsem[s_].inc(nc.sync.dma_start(out=dst, in_=src_ap))
            return s_, tok

        def setup(b, which, attn_done, out):
            moba = (which == 0)
            qT, kT, vS = qTs[which], kTs[which], vSs[which]
            t_k = None
            for p in range(NPC):
                s_, tok = load_piece(qk[b, 2 * which + 1, :, p * 2048:(p + 1) * 2048])
                kb.wait("pool", tok, attn_done)
                t_k = kb.mark(nc.gpsimd.tensor_copy(out=kT[:, p * 2048:(p + 1) * 2048], in_=stg[s_][:]), "pool")
                fr = [t_k]
                if moba:
                    t_ks = chain("dve", lambda: nc.vector.tensor_reduce(
                        out=ksum[:, p * 8:(p + 1) * 8], in_=stg[s_][:].rearrange("p (n t) -> p n t", t=256), axis=AX.X, op=ALU.add),
                        tok, attn_done)
                    fr.append(t_ks)
                stg_free[s_] = fr
                yield
            t_v = None
            for p in range(NPC):
                s_, tok = load_piece(vv[b, which, p * 2048:(p + 1) * 2048, :].rearrange("(c p) d -> p c d", p=128),
                                     lambda tl: tl[:].rearrange("p (c d) -> p c d", d=128))
                kb.wait("pool", tok, attn_done)
                t_v = kb.mark(nc.gpsimd.tensor_copy(out=vS[:, p * 16:(p + 1) * 16, :],
                                                   in_=stg[s_][:].rearrange("p (c d) -> p c d", d=128)), "pool")
                stg_free[s_] = [t_v]
                yield
            t_q = None
            t_gate = None
            for p in range(NPC):
                s_, tok = load_piece(qk[b, 2 * which, :, p * 2048:(p + 1) * 2048])
                kb.wait("pool", tok, attn_done)
                t_q = kb.mark(nc.gpsimd.tensor_copy(out=qT[:, p * 2048:(p + 1) * 2048], in_=stg[s_][:]), "pool")
                fr = [t_q]
                if moba:
                    m_ = st["msc_i"] % 1
                    st["msc_i"] += 1
                    kb.wait("pe", tok, last["dve"], msc_free[m_])
                    gv = mscps[m_][:].rearrange("p (t n) -> p t n", n=32)
                    for j in range(16):
                        ins = nc.tensor.matmul(gv[:, j, :], lhsT=stg[s_][:, j * 128:(j + 1) * 128], rhs=ksum[:, :], start=True, stop=True)
                    t_g = kb.mark(ins, "pe")
                    last["pe"] = t_g
                    fr.append(t_g)
                    t_gate = chain("dve", lambda: nc.vector.tensor_copy(out=gate_all[:, p * 16:(p + 1) * 16, :], in_=gv), t_g)
                    msc_free[m_] = t_gate
                stg_free[s_] = fr
                yield
            t_bias = None
            if moba:
                for qb in range(NBLK):
                    chain("dve", lambda: nc.vector.memset(gate_m[:], -1e30))
                    if qb > 0:
                        chain("dve", lambda: nc.vector.tensor_copy(out=gate_m[:, :, 0:qb], in_=gate_all[:, 2 * qb:2 * qb + 2, 0:qb]))
                    for j in range(2):
                        chain("dve", lambda: nc.vector.max(out=top8[:, j, :], in_=gate_m[:, j, :]))
                        chain("dve", lambda: nc.vector.tensor_scalar(out=selb[:, j, :], in0=gate_m[:, j, :], scalar1=top8[:, j, 2:3],
                                                                    scalar2=None, op0=ALU.is_ge))
                    chain("dve", lambda: nc.vector.tensor_scalar(out=selb[:], in0=selb[:], scalar1=-1.0, scalar2=-NEG,
                                                                op0=ALU.add, op1=ALU.mult))
                    if qb + 1 < 32:
                        chain("dve", lambda: nc.vector.memset(selb[:, :, qb + 1:32], NEG))
                    t_sel = chain("dve", lambda: nc.vector.memset(selb[:, :, qb:qb + 1], 0.0))
                    m_ = st["msc_i"] % 1
                    st["msc_i"] += 1
                    kb.wait("pe", t_sel, msc_free[m_])
                    for j in range(2):
                        ins = nc.tensor.transpose(out=mscps[m_][0:32, j * 128:(j + 1) * 128], in_=selb[:, j, :], identity=ident_f[:])
                    t_tr = kb.mark(ins, "pe")
                    last["pe"] = t_tr
                    t_bias = chain("dve", lambda: nc.vector.tensor_copy(out=biasT[:, qb * 256:(qb + 1) * 256], in_=mscps[m_][0:32, 0:256]),
                                   t_tr, attn_done)
                    msc_free[m_] = t_bias
                    yield
            else:
                for p in range(NPC):
                    s_, tok = load_piece(ff[b, :, p * 2048:(p + 1) * 2048], lambda tl: tl[0:1, :])
                    fr_ = stg[s_][0:1, :]
                    chain("act", lambda: nc.scalar.activation(out=fr_, in_=fr_, func=AF.Exp, scale=-1.0, bias=negb[:]), tok, t_negb)
                    t_l = chain("act", lambda: nc.scalar.activation(out=fr_, in_=fr_, func=AF.Ln, scale=1.0, bias=one1[:]))
                    chain("dve", lambda: nc.vector.tensor_scalar(out=fr_, in0=fr_, scalar1=-1.0 / SCALE, scalar2=None, op0=ALU.mult), t_l)
                    init = 0.0 if p == 0 else carry[:, 0:1]
                    kb.wait("dve", last["pe"])
                    t_sc = chain("dve", lambda: nc.vector.tensor_tensor_scan(out=lrow[:], data0=one1[:, 0:1].to_broadcast([1, 2048]), data1=fr_,
                                                                           initial=init, op0=ALU.mult, op1=ALU.add))
                    stg_free[s_] = [t_sc]
                    t_sc = chain("dve", lambda: nc.vector.tensor_copy(out=carry[:], in_=lrow[:, 2047:2048]))
                    for i in range(4):
                        m_ = st["msc_i"] % 1
                        st["msc_i"] += 1
                        kb.wait("pe", t_sc, msc_free[m_])
                        t_mm = kb.mark(nc.tensor.matmul(mscps[m_][:], lhsT=onesrow[:], rhs=lrow[:, i * 512:(i + 1) * 512], start=True, stop=True), "pe")
                        last["pe"] = t_mm
                        t_cr = chain("dve", lambda: nc.vector.tensor_copy(out=crep[:, p * 2048 + i * 512:p * 2048 + (i + 1) * 512], in_=mscps[m_][:]),
                                     t_mm, attn_done)
                        msc_free[m_] = t_cr
                    m_ = st["msc_i"] % 1
                    st["msc_i"] += 1
                    kb.wait("pe", t_sc, msc_free[m_])
                    for kc in range(16):
                        ins = nc.tensor.matmul(mscps[m_][:, 2 * kc:2 * kc + 2], lhsT=lrow[:, kc * 128:(kc + 1) * 128], rhs=negone[:], start=True, stop=True)
                    t_mm = kb.mark(ins, "pe")
                    last["pe"] = t_mm
                    t_bias = chain("dve", lambda: nc.vector.tensor_copy(
                        out=negc[:, p * 16:(p + 1) * 16], in_=mscps[m_][:, 0:32].rearrange("p (c two) -> p c two", two=2)[:, :, 0]), t_mm, attn_done)
                    msc_free[m_] = t_bias
                    yield
            out["toks"] = [t_k, t_v, t_q, t_bias]
            yield

        def main(b, which, toks, bg):
            moba = (which == 0)
            qT, kT, vS = qTs[which], kTs[which], vSs[which]
            t_k, t_v, t_q, t_bias = toks
            kb.wait("pe", t_k, t_v, t_q, t_bias)
            units = [(Q, kc) for Q in range(NQT) for kc in range(4 * Q + 4)]
            res = {"last_pv": None}

            def emit_qk(Q, kc):
                q0 = Q * 512
                c0 = max(0, kc * 128 - q0)
                diag = kc * 128 >= q0
                n = kc // 2
                sl = st["pt_i"] % NPT
                st["pt_i"] += 1
                kb.wait("pe", S_free[sl])
                ins = nc.tensor.matmul(Sps[sl][:, c0:512], lhsT=kT[:, kc * 128:(kc + 1) * 128], rhs=qT[:, q0 + c0:q0 + 512],
                                       start=True, stop=not (moba or diag))
                if moba:
                    ins = nc.tensor.matmul(Sps[sl][:, c0:512], lhsT=E_all[:, n, :], rhs=biasT[:, q0 + c0:q0 + 512],
                                           start=False, stop=not diag)
                if diag:
                    ins = nc.tensor.matmul(Sps[sl][:, c0:c0 + 128], lhsT=ident[:], rhs=tri[:], start=False, stop=True)
                t_s = kb.mark(ins, "pe")
                if moba:
                    kb.wait("act", t_s, PT_free[sl])
                    t_p = kb.mark(nc.scalar.activation(out=PT[sl][:, c0:512], in_=Sps[sl][:, c0:512], func=AF.Exp, scale=SCALE), "act")
                    S_free[sl] = t_p
                else:
                    kb.wait("dve", t_s, tmp_free[sl])
                    t_t = kb.mark(nc.vector.scalar_tensor_tensor(out=tmpS[sl][:, c0:512], in0=Sps[sl][:, c0:512], scalar=negc[:, kc:kc + 1],
                                                                 in1=crep[:, q0 + c0:q0 + 512], op0=ALU.add, op1=ALU.add), "dve")
                    S_free[sl] = t_t
                    kb.wait("act", t_t, PT_free[sl])
                    t_p = kb.mark(nc.scalar.activation(out=PT[sl][:, c0:512], in_=tmpS[sl][:, c0:512], func=AF.Exp, scale=SCALE), "act")
                    tmp_free[sl] = t_p
                return (Q, kc, sl, c0, t_p)

            def emit_pv(info):
                Q, kc, sl, c0, t_p = info
                q0 = Q * 512
                nkc = 4 * Q + 4
                first = (kc == 0)
                kb.wait("pe", t_p)
                if first:
                    kb.wait("pe", st.get("acc_free"))
                nc.tensor.matmul(accps[:, c0:512], lhsT=vS[:, kc, :], rhs=PT[sl][:, c0:512], start=first, stop=(kc == nkc - 1))
                ins = nc.tensor.matmul(denps[:, c0:512], lhsT=ones_b[:], rhs=PT[sl][:, c0:512], start=first, stop=(kc == nkc - 1))
                t_pv = kb.mark(ins, "pe")
                PT_free[sl] = t_pv
                res["last_pv"] = t_pv
                if kc == nkc - 1:
                    o_ = st["os_i"] % NOS
                    st["os_i"] += 1
                    t_r = chain("dve", lambda: nc.vector.reciprocal(out=oS[o_][:], in_=denps[:]), t_pv, osem[o_].tok())
                    t_o = chain("dve", lambda: nc.vector.tensor_tensor(out=oS[o_][:], in0=accps[:], in1=oS[o_][:], op=ALU.mult))
                    st["acc_free"] = t_o
                    kb.wait("sp", t_o)
                    osem[o_].inc(nc.sync.dma_start(out=oo[b, which, :, q0:q0 + 512], in_=oS[o_][:]))

            LAG = 4
            pend = []
            for ui, (Q, kc) in enumerate(units):
                pend.append(emit_qk(Q, kc))
                if len(pend) > LAG:
                    emit_pv(pend.pop(0))
                if bg is not None and ui % 6 == 5:
                    next(bg, None)
            while pend:
                emit_pv(pend.pop(0))
            t_last_pv = res["last_pv"]
            return t_last_pv

        order = [(b, which) for b in range(NB) for which in range(2)]
        outs = [dict() for _ in order]
        dones = [None] * len(order)
        gens = []
        for a, (b, which) in enumerate(order):
            gens.append(None)
        g0 = setup(order[0][0], order[0][1], None, outs[0])
        for _ in g0:
            pass
        for a, (b, which) in enumerate(order):
            bg = None
            if a + 1 < len(order):
                nb_, nw_ = order[a + 1]
                prev_same = dones[a - 1] if a - 1 >= 0 else None
                bg = setup(nb_, nw_, prev_same, outs[a + 1])
            dones[a] = main(b, which, outs[a]["toks"], bg)
            if bg is not None:
                for _ in bg:
                    pass
        for s_ in osem:
            kb.wait("sp", s_.tok())
    return nc


def run_phase_b(qk, v, fo, b_forget):
    nc = get_prog("b")
    in_maps = []
    for h in range(NCORES):
        qk_h = np.ascontiguousarray(qk[:, h].reshape(4, 128, BATCH, SEQ).transpose(2, 0, 1, 3))
        v_h = np.ascontiguousarray(v[:, :, h * 128:(h + 1) * 128].reshape(2, BATCH, SEQ, 128).transpose(1, 0, 2, 3))
        f_h = np.ascontiguousarray(fo[h].reshape(BATCH, 1, SEQ))
        in_maps.append({"qk": qk_h, "vv": v_h, "ff": f_h, "bf": np.ascontiguousarray(b_forget[h].reshape(1, 1))})
    res = run_bass_kernel_spmd(nc, in_maps, core_ids=list(range(NCORES)))
    return np.stack([r["oo"] for r in res.results], axis=0)


HAL = 16


def build_phase_c(TPC=TPC, T=512):
    nc = bass.Bass("TRN2", target_bir_lowering=False)
    NT = TPC // T
    x = nc.dram_tensor("x", [TPC, D], F32, kind="ExternalInput").ap()
    xh = nc.dram_tensor("xh", [128, D], F32, kind="ExternalInput").ap()
    gcol = nc.dram_tensor("gcol", [128, NCH], F32, kind="ExternalInput").ap()
    NBT = 32 + 80 + 16
    wq = nc.dram_tensor("wq", [NBT, 128, NCH * 256], BF16, kind="ExternalInput").ap()
    oa = nc.dram_tensor("oa", [2, W, TPC], F32, kind="ExternalInput").ap()
    icnt = nc.dram_tensor("icnt", [128, 4, TPC], F32, kind="ExternalInput").ap()
    wpool = nc.dram_tensor("wpool", [128, 8, 256], F32, kind="ExternalInput").ap()
    pscale = nc.dram_tensor("pscale", [128, 8], F32, kind="ExternalInput").ap()
    wconv = nc.dram_tensor("wconv", [128, 3, 8], F32, kind="ExternalInput").ap()
    bmerge = nc.dram_tensor("bmerge", [128, 4, NCH], F32, kind="ExternalInput").ap()
    y = nc.dram_tensor("y", [TPC, D], F32, kind="ExternalOutput").ap()
    TE = T + HAL
    with ExitStack() as es:
        kb = KB(nc, es)
        hT = kb.sb("hT", [128, NCH, TE], BF16)
        brT = kb.sb("brT", [128, NCH, T], BF16)
        assert NCH * T >= 12288
        mg_raw = kb.sb("mg_raw", [128, NCH * T], BF16)
        mgT = mg_raw[:].rearrange("p (c t) -> p c t", t=T)
        xbuf = mg_raw[:, 0:8192].bitcast(F32)
        xn = mg_raw[:, 8192:12288]
        NW = 3
        wblk = [kb.sb(f"wblk{i}", [128, NCH, 256], BF16) for i in range(NW)]
        ss = kb.sb("ss", [128, 1], F32)
        rs = kb.sb("rs", [128, 1], F32)
        epsT = kb.sb("epsT", [128, 1], F32)
        gcolS = kb.sb("gcolS", [128, NCH], F32)
        ident_f = kb.sb("ident_f", [128, 128], F32)
        ident = kb.sb("ident", [128, 128], BF16)
        icS = kb.sb("icS", [128, 4, T], F32)
        assert T == 512
        wpool_f = icS[:].rearrange("p a (b c) -> p (a b) c", c=256)
        wpool_b = kb.sb("wpool_b", [128, 8, 256], BF16)
        pscS = kb.sb("pscS", [128, 8], F32)
        wcvS = kb.sb("wcvS", [128, 3, 8], F32)
        bmS = kb.sb("bmS", [128, 4, NCH], F32)
        NTF = 2
        tmpf = [kb.sb(f"tmpf{i}", [128, T], F32) for i in range(NTF)]
        otile = [kb.sb(f"otile{i}", [128, T], F32) for i in range(2)]
        uext = kb.sb("uext", [128, 2, TE], F32)
        pa = kb.sb("pa", [128, 2, TE], F32)
        pb = kb.sb("pb", [128, 2, TE], F32)
        pooledT = kb.sb("pooledT", [128, 2, T], BF16)
        yS = kb.sb("yS", [128, 2, T], F32)
        bS = kb.sb("bS", [128, T], F32)
        cext = kb.sb("cext", [128, TE], F32)
        zext = kb.sb("zext", [128, TE], F32)
        ycv = kb.sb("ycv", [128, T], F32)
        sgT = kb.sb("sgT", [128, 4, 2, T], BF16)
        accm = kb.sb("accm", [128, T], F32)
        xres = [kb.sb(f"xres{i}", [128, 256], F32) for i in range(2)]
        orow = [kb.sb(f"orow{i}", [128, 256], F32) for i in range(2)]
        tpps = [kb.ps(f"tpps{i}", [128, 8, 128], BF16) for i in range(2)]
        NPS = 4
        mps = [kb.ps(f"mps{i}", [128, 512], F32) for i in range(NPS)]
        hps = kb.ps("hps", [128, 512], F32)

        last = {"pe": None, "act": None, "dve": None, "pool": None}

        def chain(e, ins, *deps):
            kb.wait(e, last[e], *deps)
            t = kb.mark(ins(), e)
            last[e] = t
            return t

        def pe_mark(ins):
            t = kb.mark(ins, "pe")
            last["pe"] = t
            return t

        csem = DmaSem(kb, "csem")
        xsem = DmaSem(kb, "xsem")
        wblk_free = [None] * NW
        wl_sem = [DmaSem(kb, f"wlsem{i}") for i in range(NW)]
        osem = [DmaSem(kb, f"osem{i}") for i in range(2)]
        otile_free = [None, None]
        xrsem = [DmaSem(kb, f"xrsem{i}") for i in range(2)]
        xres_free = [None, None]
        orsem = [DmaSem(kb, f"orsem{i}") for i in range(2)]
        icsem = DmaSem(kb, "icsem")
        mps_free = [None] * NPS
        hps_free = [None]
        tp_free = [None, None]
        cnt = {"piece": 0, "blk": 0, "mps": 0, "tp": 0, "tf": 0, "ot": 0, "xr": 0}

        t = chain("pool", lambda: nc.gpsimd.memset(ident_f[:], 0.0))
        t = chain("pool", lambda: nc.gpsimd.affine_select(out=ident_f[:], in_=ident_f[:], pattern=[[-1, 128]],
                                                          compare_op=ALU.not_equal, fill=1.0, base=0, channel_multiplier=1))
        t_id = chain("pool", lambda: nc.gpsimd.tensor_copy(out=ident[:], in_=ident_f[:]))
        c_all = None
        for dst, src in ((gcolS, gcol), (wpool_f, wpool), (pscS, pscale), (wcvS, wconv), (bmS, bmerge)):
            c_all = csem.inc(nc.sync.dma_start(out=dst[:], in_=src))
        chain("dve", lambda: nc.vector.memset(epsT[:], EPS))
        t_c = chain("dve", lambda: nc.vector.tensor_copy(out=wpool_b[:], in_=wpool_f[:]), c_all)
        kb.wait("pe", t_id, t_c)
        kb.wait("act", t_c)

        def norm_tile(x_rows, dst_col0, src_c0, ncols):
            kb.wait("sp", last["dve"], last["pe"])
            ld = xsem.inc(nc.sync.dma_start(out=xbuf[:], in_=x_rows))
            chain("act", lambda: nc.scalar.activation(out=xn[:], in_=xbuf[:], func=AF.Square, accum_out=ss[:]), ld, last["pe"], last["dve"])
            t_a = chain("act", lambda: nc.scalar.activation(out=rs[:], in_=ss[:], func=AF.Sqrt, scale=1.0 / D, bias=epsT[:]))
            chain("dve", lambda: nc.vector.reciprocal(out=rs[:], in_=rs[:]), t_a)
            t_xn = chain("dve", lambda: nc.vector.tensor_scalar(out=xn[:], in0=xbuf[:], scalar1=rs[:, 0:1], scalar2=None, op0=ALU.mult))
            for c8 in range(NCH // 8):
                sl = cnt["tp"] % 2
                cnt["tp"] += 1
                kb.wait("pe", t_xn, tp_free[sl])
                for j in range(8):
                    c = c8 * 8 + j
                    ins = nc.tensor.transpose(out=tpps[sl][:, j, :], in_=xn[:, c * 128:(c + 1) * 128], identity=ident[:])
                t_pe = pe_mark(ins)
                for j in range(8):
                    c = c8 * 8 + j
                    t_ev = chain("dve", lambda: nc.vector.tensor_scalar(out=hT[:, c, dst_col0:dst_col0 + ncols],
                                                                       in0=tpps[sl][:, j, src_c0:src_c0 + ncols],
                                                                       scalar1=gcolS[:, c:c + 1], scalar2=None, op0=ALU.mult), t_pe)
                tp_free[sl] = t_ev

        def load_block(view=None, col0=None):
            slot = cnt["blk"] % NW
            bi = cnt["blk"] % NBT
            cnt["blk"] += 1
            kb.wait("sp", wblk_free[slot])
            ld = wl_sem[slot].inc(nc.sync.dma_start(out=wblk[slot][:].rearrange("p c n -> p (c n)"), in_=wq[bi, :, :]))
            kb.wait("pe", ld)
            return slot

        def release(slot):
            wblk_free[slot] = last["pe"]

        def mm_feat(slot, j, halo=False):
            ms = cnt["mps"] % NPS
            cnt["mps"] += 1
            kb.wait("pe", mps_free[ms])
            for c in range(NCH):
                ins = nc.tensor.matmul(mps[ms][:, 0:T], lhsT=wblk[slot][:, c, j * 128:(j + 1) * 128], rhs=hT[:, c, HAL:TE],
                                       start=(c == 0), stop=(c == NCH - 1))
            if halo:
                kb.wait("pe", hps_free[0])
                for c in range(NCH):
                    ins = nc.tensor.matmul(hps[:, 0:HAL], lhsT=wblk[slot][:, c, j * 128:(j + 1) * 128], rhs=hT[:, c, 0:HAL],
                                           start=(c == 0), stop=(c == NCH - 1))
            return ms, pe_mark(ins)

        def get_tmpf():
            i = cnt["tf"] % NTF
            cnt["tf"] += 1
            return tmpf[i]

        pending = []

        for tt in range(NT):
            t0 = tt * T
            if tt == 0:
                norm_tile(xh[:, :], 0, 128 - HAL, HAL)
            else:
                chain("dve", lambda: nc.vector.tensor_copy(out=hT[:, :, 0:HAL], in_=hT[:, :, T:TE]), last["pe"])
            for s in range(T // 128):
                norm_tile(x[t0 + s * 128:t0 + (s + 1) * 128, :], HAL + s * 128, 0, 128)
            kb.wait("pe", last["dve"])
            kb.wait("sp", last["dve"])
            t_ic = icsem.inc(nc.sync.dma_start(out=icS[:], in_=icnt[:, :, t0:t0 + T]))
            kb.wait("dve", t_ic)

            for blk in range(32):
                slot = load_block()
                if blk < 8:
                    br = blk // 4
                    for j in range(2):
                        ch = 2 * (blk % 4) + j
                        ms, t_mm = mm_feat(slot, j)
                        tf = get_tmpf()
                        t_a = chain("act", lambda: nc.scalar.activation(out=tf[:], in_=mps[ms][:, 0:T], func=AF.Silu), t_mm, last["dve"])
                        mps_free[ms] = t_a
                        oi = cnt["ot"] % 2
                        cnt["ot"] += 1
                        kb.wait("sp", otile_free[oi])
                        t_o = osem[oi].inc(nc.sync.dma_start(out=otile[oi][:], in_=oa[br, ch * 128:(ch + 1) * 128, t0:t0 + T]))
                        t_d = chain("dve", lambda: nc.vector.tensor_tensor(out=brT[:, br * 8 + ch, :], in0=tf[:], in1=otile[oi][:], op=ALU.mult), t_a, t_o)
                        otile_free[oi] = t_d
                elif blk < 16:
                    g = (blk - 8) // 2
                    if (blk - 8) % 2 == 0:
                        for j in range(2):
                            ms, t_mm = mm_feat(slot, j, halo=True)
                            chain("act", lambda: nc.scalar.copy(out=uext[:, j, HAL:TE], in_=mps[ms][:, 0:T]), t_mm, last["dve"])
                            t_a = chain("act", lambda: nc.scalar.copy(out=uext[:, j, 0:HAL], in_=hps[:, 0:HAL]))
                            mps_free[ms] = t_a
                            hps_free[0] = t_a
                        src = uext
                        bufs = [pa, pb]
                        for k in range(g + 1):
                            sh = 2 ** k
                            dst = bufs[k % 2]
                            lo = 2 * sh - 1
                            chain("dve", lambda: nc.vector.tensor_tensor(out=dst[:, :, lo:TE], in0=src[:, :, lo:TE], in1=src[:, :, lo - sh:TE - sh], op=ALU.add), last["act"])
                            src = dst
                        for j in range(2):
                            tf = get_tmpf()
                            chain("dve", lambda: nc.vector.tensor_tensor(out=tf[:], in0=src[:, j, HAL:TE], in1=icS[:, g, :], op=ALU.mult), last["act"])
                            t_p = chain("dve", lambda: nc.vector.tensor_tensor(out=pooledT[:, j, :], in0=tf[:], in1=uext[:, j, HAL:TE], op=ALU.subtract), last["pe"])
                        for oc in range(2):
                            ms = cnt["mps"] % NPS
                            cnt["mps"] += 1
                            kb.wait("pe", mps_free[ms], t_p)
                            for j in range(2):
                                ins = nc.tensor.matmul(mps[ms][:, 0:T], lhsT=wpool_b[:, 2 * g + j, oc * 128:(oc + 1) * 128], rhs=pooledT[:, j, :],
                                                       start=(j == 0), stop=(j == 1))
                            t_mm = pe_mark(ins)
                            t_y = chain("dve", lambda: nc.vector.tensor_scalar(out=yS[:, oc, :], in0=mps[ms][:, 0:T], scalar1=pscS[:, 2 * g + oc:2 * g + oc + 1],
                                                                              scalar2=None, op0=ALU.mult), t_mm)
                            mps_free[ms] = t_y
                    else:
                        for j in range(2):
                            ms, t_mm = mm_feat(slot, j)
                            tf = get_tmpf()
                            t_a = chain("act", lambda: nc.scalar.activation(out=tf[:], in_=mps[ms][:, 0:T], func=AF.Silu), t_mm, last["dve"])
                            mps_free[ms] = t_a
                            chain("dve", lambda: nc.vector.tensor_tensor(out=brT[:, 16 + 2 * g + j, :], in0=tf[:], in1=yS[:, j, :], op=ALU.mult), t_a)
                else:
                    i = (blk - 16) // 2
                    if (blk - 16) % 2 == 0:
                        ms, t_mm = mm_feat(slot, 0)
                        t_a = chain("act", lambda: nc.scalar.copy(out=bS[:], in_=mps[ms][:, 0:T]), t_mm, last["dve"])
                        mps_free[ms] = t_a
                        ms, t_mm = mm_feat(slot, 1, halo=True)
                        chain("act", lambda: nc.scalar.copy(out=cext[:, HAL:TE], in_=mps[ms][:, 0:T]), t_mm, last["dve"])
                        t_a = chain("act", lambda: nc.scalar.copy(out=cext[:, 0:HAL], in_=hps[:, 0:HAL]))
                        mps_free[ms] = t_a
                        hps_free[0] = t_a
                    else:
                        ms, t_mm = mm_feat(slot, 0, halo=True)
                        chain("dve", lambda: nc.vector.tensor_tensor(out=zext[:, HAL:TE], in0=mps[ms][:, 0:T], in1=cext[:, HAL:TE], op=ALU.mult), t_mm, last["act"])
                        t_d = chain("dve", lambda: nc.vector.tensor_tensor(out=zext[:, 0:HAL], in0=hps[:, 0:HAL], in1=cext[:, 0:HAL], op=ALU.mult))
                        mps_free[ms] = t_d
                        hps_free[0] = t_d
                        chain("dve", lambda: nc.vector.tensor_scalar(out=ycv[:], in0=zext[:, HAL - 2:TE - 2], scalar1=wcvS[:, 0, i:i + 1], scalar2=None, op0=ALU.mult))
                        chain("dve", lambda: nc.vector.scalar_tensor_tensor(out=ycv[:], in0=zext[:, HAL - 1:TE - 1], scalar=wcvS[:, 1, i:i + 1], in1=ycv[:],
                                                                           op0=ALU.mult, op1=ALU.add))
                        chain("dve", lambda: nc.vector.scalar_tensor_tensor(out=ycv[:], in0=zext[:, HAL:TE], scalar=wcvS[:, 2, i:i + 1], in1=ycv[:],
                                                                           op0=ALU.mult, op1=ALU.add))
                        chain("dve", lambda: nc.vector.tensor_tensor(out=ycv[:], in0=ycv[:], in1=bS[:], op=ALU.mult))
                        ms, t_mm = mm_feat(slot, 1)
                        tf = get_tmpf()
                        t_a = chain("act", lambda: nc.scalar.activation(out=tf[:], in_=mps[ms][:, 0:T], func=AF.Silu), t_mm, last["dve"])
                        mps_free[ms] = t_a
                        chain("dve", lambda: nc.vector.tensor_tensor(out=brT[:, 24 + i, :], in0=tf[:], in1=ycv[:], op=ALU.mult), t_a)
                release(slot)

            for dp in range(NCH // 2):
                for mb in range(4):
                    slot = load_block()
                    dcl = mb // 2
                    dc = 2 * dp + dcl
                    for j in range(2):
                        i = 2 * (mb % 2) + j
                        ms, t_mm = mm_feat(slot, j)
                        t_a = chain("act", lambda: nc.scalar.activation(out=sgT[:, i, dcl, :], in_=mps[ms][:, 0:T], func=AF.Sigmoid,
                                                                       bias=bmS[:, i, dc:dc + 1]), t_mm, last["dve"])
                        mps_free[ms] = t_a
                    release(slot)
                slot = load_block()
                kb.wait("pe", last["dve"])
                for dcl in range(2):
                    dc = 2 * dp + dcl
                    for i in range(4):
                        ms = cnt["mps"] % NPS
                        cnt["mps"] += 1
                        kb.wait("pe", mps_free[ms])
                        for wcn in range(8):
                            ins = nc.tensor.matmul(mps[ms][:, 0:T], lhsT=wblk[slot][:, 8 * i + wcn, dcl * 128:(dcl + 1) * 128], rhs=brT[:, 8 * i + wcn, :],
                                                   start=(wcn == 0), stop=(wcn == 7))
                        t_mm = pe_mark(ins)
                        if i == 0:
                            t_d = chain("dve", lambda: nc.vector.tensor_tensor(out=accm[:], in0=mps[ms][:, 0:T], in1=sgT[:, i, dcl, :], op=ALU.mult), t_mm, last["act"])
                        else:
                            tf = get_tmpf()
                            t_d = chain("dve", lambda: nc.vector.tensor_tensor(out=tf[:], in0=mps[ms][:, 0:T], in1=sgT[:, i, dcl, :], op=ALU.mult), t_mm, last["act"])
                            if i < 3:
                                chain("dve", lambda: nc.vector.tensor_tensor(out=accm[:], in0=accm[:], in1=tf[:], op=ALU.add))
                            else:
                                chain("dve", lambda: nc.vector.tensor_tensor(out=mgT[:, dc, :], in0=accm[:], in1=tf[:], op=ALU.add), last["pe"])
                        mps_free[ms] = t_d
                release(slot)

            kb.wait("pe", last["dve"])
            for ob in range(16):
                slot = load_block()
                for st_ in pending:
                    st_()
                pending = []
                for s in range(T // 128):
                    ms = cnt["mps"] % NPS
                    cnt["mps"] += 1
                    kb.wait("pe", mps_free[ms])
                    for c in range(NCH):
                        ins = nc.tensor.matmul(mps[ms][:, 0:256], lhsT=mgT[:, c, s * 128:(s + 1) * 128], rhs=wblk[slot][:, c, :],
                                               start=(c == 0), stop=(c == NCH - 1))
                    t_mm = pe_mark(ins)
                    xi = cnt["xr"] % 2
                    cnt["xr"] += 1
                    kb.wait("sp", xres_free[xi])
                    r0 = t0 + s * 128
                    t_x = xrsem[xi].inc(nc.sync.dma_start(out=xres[xi][:], in_=x[r0:r0 + 128, ob * 256:(ob + 1) * 256]))
                    t_d = chain("dve", lambda: nc.vector.tensor_tensor(out=orow[xi][:], in0=mps[ms][:, 0:256], in1=xres[xi][:], op=ALU.add),
                                t_mm, t_x, orsem[xi].tok())
                    mps_free[ms] = t_d
                    xres_free[xi] = t_d

                    def mk_store(xi=xi, r0=r0, ob=ob, t_d=t_d):
                        kb.wait("sp", t_d)
                        orsem[xi].inc(nc.sync.dma_start(out=y[r0:r0 + 128, ob * 256:(ob + 1) * 256], in_=orow[xi][:]))
                    mk_store()
                release(slot)
        for s_ in orsem:
            kb.wait("sp", s_.tok())
    return nc


def prep_c_consts(w_in, w_pool, pool_scale, w_conv, w_branch, b_merge, w_out, g):
    def blk(mat, cols):
        return mat[:, cols].reshape(NCH, 128, 256).transpose(1, 0, 2).reshape(128, NCH * 256)
    blocks = []
    for b in range(4):
        blocks.append(blk(w_in, np.arange(3072 + b * 256, 3072 + (b + 1) * 256)))
    for b in range(4):
        blocks.append(blk(w_in, np.arange(7168 + b * 256, 7168 + (b + 1) * 256)))
    for gg in range(4):
        blocks.append(blk(w_in, np.arange(8200 + gg * 256, 8200 + (gg + 1) * 256)))
        blocks.append(blk(w_in, np.arange(9224 + gg * 256, 9224 + (gg + 1) * 256)))
    for i in range(8):
        blocks.append(blk(w_in, np.concatenate([np.arange(10248 + i * 128, 10248 + (i + 1) * 128), np.arange(11272 + i * 128, 11272 + (i + 1) * 128)])))
        blocks.append(blk(w_in, np.concatenate([np.arange(12296 + i * 128, 12296 + (i + 1) * 128), np.arange(13320 + i * 128, 13320 + (i + 1) * 128)])))
    wbr = w_branch.reshape(D, D)
    for dp in range(16):
        for mb in range(4):
            dc = 2 * dp + mb // 2
            cc = []
            for j in range(2):
                i = 2 * (mb % 2) + j
                cc.append(np.arange(14344 + i * 4096 + dc * 128, 14344 + i * 4096 + (dc + 1) * 128))
            blocks.append(blk(w_in, np.concatenate(cc)))
        blocks.append(blk(wbr, np.arange(dp * 256, (dp + 1) * 256)))
    for ob in range(16):
        blocks.append(blk(w_out, np.arange(ob * 256, (ob + 1) * 256)))
    assert len(blocks) == 128
    consts = {
        "gcol": np.ascontiguousarray(g.reshape(NCH, 128).T),
        "wpool": np.ascontiguousarray(w_pool.reshape(4, 2, 128, 256).transpose(2, 0, 1, 3).reshape(128, 8, 256)),
        "pscale": np.ascontiguousarray(pool_scale.reshape(8, 128).T),
        "wconv": np.ascontiguousarray(w_conv.reshape(3, 8, 128).transpose(2, 0, 1)),
        "bmerge": np.ascontiguousarray(b_merge.reshape(4, NCH, 128).transpose(2, 0, 1)),
    }
    return consts, np.stack(blocks, axis=0)


def icnt_table(pos0, n):
    pos = np.arange(pos0, pos0 + n)
    tab = np.stack([1.0 / np.minimum(pos + 1, w) for w in (2, 4, 8, 16)], axis=0).astype(np.float32)
    return np.ascontiguousarray(np.broadcast_to(tab[None], (128, 4, n)))


def run_phase_c(h, oo, consts, wq):
    nc = get_prog("c")
    in_maps = []
    for c in range(NCORES):
        b = c // (NCORES // BATCH)
        off = (c % (NCORES // BATCH)) * TPC
        r0 = c * TPC
        xh = np.zeros((128, D), np.float32) if off == 0 else np.ascontiguousarray(h[r0 - 128:r0])
        oa = np.ascontiguousarray(oo[:, b, :, :, off:off + TPC].transpose(1, 0, 2, 3).reshape(2, W, TPC))
        m = dict(consts)
        m.update({"x": np.ascontiguousarray(h[r0:r0 + TPC]), "xh": xh, "oa": oa, "icnt": icnt_table(off, TPC), "wq": wq})
        in_maps.append(m)
    res = run_bass_kernel_spmd(nc, in_maps, core_ids=list(range(NCORES)))
    return np.concatenate([r["y"] for r in res.results], axis=0)


def build_phase_d(TPC=TPC):
    nc = bass.Bass("TRN2", target_bir_lowering=False)
    x = nc.dram_tensor("x", [TPC, D], F32, kind="ExternalInput").ap()
    g = nc.dram_tensor("g", [128, D], F32, kind="ExternalInput").ap()
    y = nc.dram_tensor("y", [TPC, D], F32, kind="ExternalOutput").ap()
    with ExitStack() as es:
        kb = KB(nc, es)
        grep = kb.sb("grep", [128, D], F32)
        NB_ = 2
        xb = [kb.sb(f"xb{i}", [128, D], F32) for i in range(NB_)]
        yb = [kb.sb(f"yb{i}", [128, D], F32) for i in range(NB_)]
        junk = kb.sb("junk", [128, D], BF16)
        ss = kb.sb("ss", [128, 1], F32)
        rs = kb.sb("rs", [128, 1], F32)
        epsT = kb.sb("epsT", [128, 1], F32)
        last = {"act": None, "dve": None}

        def chain(e, ins, *deps):
            kb.wait(e, last[e], *deps)
            t = kb.mark(ins(), e)
            last[e] = t
            return t

        csem = DmaSem(kb, "csem")
        xs = [DmaSem(kb, f"xs{i}") for i in range(NB_)]
        ys = [DmaSem(kb, f"ys{i}") for i in range(NB_)]
        xfree = [None] * NB_
        c1 = csem.inc(nc.sync.dma_start(out=grep[:], in_=g[:, :]))
        t_e = chain("dve", lambda: nc.vector.memset(epsT[:], EPS), c1)
        kb.wait("act", t_e)
        for i in range(TPC // 128):
            s_ = i % NB_
            kb.wait("sp", xfree[s_])
            ld = xs[s_].inc(nc.sync.dma_start(out=xb[s_][:], in_=x[i * 128:(i + 1) * 128, :]))
            chain("act", lambda: nc.scalar.activation(out=junk[:], in_=xb[s_][:], func=AF.Square, accum_out=ss[:]), ld, last["dve"])
            t_a = chain("act", lambda: nc.scalar.activation(out=rs[:], in_=ss[:], func=AF.Sqrt, scale=1.0 / D, bias=epsT[:]))
            chain("dve", lambda: nc.vector.reciprocal(out=rs[:], in_=rs[:]), t_a)
            t_y = chain("dve", lambda: nc.vector.scalar_tensor_tensor(out=yb[s_][:], in0=xb[s_][:], scalar=rs[:, 0:1], in1=grep[:],
                                                                     op0=ALU.mult, op1=ALU.mult), ys[s_].tok())
            xfree[s_] = t_y
            kb.wait("sp", t_y)
            ys[s_].inc(nc.sync.dma_start(out=y[i * 128:(i + 1) * 128, :], in_=yb[s_][:]))
        for s_ in ys:
            kb.wait("sp", s_.tok())
    return nc


def run_phase_d(h, final_g):
    nc = get_prog("d")
    grep = np.ascontiguousarray(np.broadcast_to(final_g[None, :], (128, D)))
    in_maps = [{"x": np.ascontiguousarray(h[c * TPC:(c + 1) * TPC]), "g": grep} for c in range(NCORES)]
    res = run_bass_kernel_spmd(nc, in_maps, core_ids=list(range(NCORES)))
    return np.concatenate([r["y"] for r in res.results], axis=0)


WCH = 4096


def build_cast(NP):
    nc = bass.Bass("TRN2", target_bir_lowering=False)
    x = nc.dram_tensor("x", [NP, 128, WCH], F32, kind="ExternalInput").ap()
    y = nc.dram_tensor("y", [NP, 128, WCH], BF16, kind="ExternalOutput").ap()
    with ExitStack() as es:
        kb = KB(nc, es)
        NB_ = 4
        xb = [kb.sb(f"xb{i}", [128, WCH], F32) for i in range(NB_)]
        yb = [kb.sb(f"yb{i}", [128, WCH], BF16) for i in range(NB_)]
        xs = [DmaSem(kb, f"xs{i}") for i in range(NB_)]
        ys = [DmaSem(kb, f"ys{i}") for i in range(NB_)]
        xfree = [None] * NB_
        engs = ["dve", "pool", "act"]
        for i in range(NP):
            s_ = i % NB_
            e = engs[i % 3]
            kb.wait("sp", xfree[s_])
            ld = xs[s_].inc(nc.sync.dma_start(out=xb[s_][:], in_=x[i, :, :]))
            kb.wait(e, ld, ys[s_].tok())
            if e == "dve":
                ins = nc.vector.tensor_copy(out=yb[s_][:], in_=xb[s_][:])
            elif e == "pool":
                ins = nc.gpsimd.tensor_copy(out=yb[s_][:], in_=xb[s_][:])
            else:
                ins = nc.scalar.copy(out=yb[s_][:], in_=xb[s_][:])
            t = kb.mark(ins, e)
            xfree[s_] = t
            kb.wait("sp", t)
            ys[s_].inc(nc.sync.dma_start(out=y[i, :, :], in_=yb[s_][:]))
        for s_ in ys:
            kb.wait("sp", s_.tok())
    return nc


def run_cast(arrays):
    sizes = [a.size for a in arrays]
    total = sum(sizes)
    unit = NCORES * 128 * WCH
    npc = -(-total // unit)
    flat = np.zeros(npc * unit, np.float32)
    off = 0
    for a in arrays:
        flat[off:off + a.size] = a.reshape(-1)
        off += a.size
    flat = flat.reshape(NCORES, npc, 128, WCH)
    key = ("w", npc)
    if key not in _CACHE:
        _CACHE[key] = build_cast(npc)
    res = run_bass_kernel_spmd(_CACHE[key], [{"x": flat[c]} for c in range(NCORES)], core_ids=list(range(NCORES)))
    out = np.concatenate([np.asarray(r["y"]).reshape(-1) for r in res.results])
    outs = []
    off = 0
    for a in arrays:
        outs.append(out[off:off + a.size].reshape(a.shape))
        off += a.size
    return outs


def kernel(x, norm_g, w_in, b_forget, w_pool, pool_scale, w_conv, w_branch, b_merge, w_out, final_g):
    f32 = lambda a: np.asarray(a, dtype=np.float32)
    x, norm_g, w_in, b_forget, w_pool, pool_scale, w_conv, w_branch, b_merge, w_out, final_g = map(
        f32, (x, norm_g, w_in, b_forget, w_pool, pool_scale, w_conv, w_branch, b_merge, w_out, final_g))
    cc, fl = [], []
    for l in range(2):
        pa = prep_a_f32(w_in[l])
        consts, wblocks = prep_c_consts(w_in[l], w_pool[l], pool_scale[l], w_conv[l], w_branch[l], b_merge[l], w_out[l], norm_g[l])
        cc.append(consts)
        fl += [pa["wa"], pa["wf"], wblocks]
    q = run_cast(fl)
    del fl
    h = np.ascontiguousarray(x.reshape(TOK, D))
    for l in range(2):
        wq_a = {"wa": q[3 * l], "wf": q[3 * l + 1]}
        qk, v, fo = run_phase_a(h, norm_g[l], wq_a)
        oo = run_phase_b(qk, v, fo, b_forget[l])
        del qk, v, fo
        h = run_phase_c(h, oo, cc[l], q[3 * l + 2])
        del oo
    out = run_phase_d(h, final_g)
    return out.reshape(BATCH, SEQ, D).astype(np.float32)
```

```python
import numpy as np
from contextlib import ExitStack
import concourse.bass as bass
import concourse.mybir as mybir
from concourse.bass_utils import run_bass_kernel_spmd

F32 = mybir.dt.float32
BF16 = mybir.dt.bfloat16
AF = mybir.ActivationFunctionType
ALU = mybir.AluOpType
AX = mybir.AxisListType

NCORES = 8
D = 4096
NCH = D // 128
SEQ = 8192
BATCH = 2
TOK = BATCH * SEQ
TPC = TOK // NCORES
TT = 512
W = 1024
NH = 8
EPS = 1e-6
SCALE = 128 ** -0.5
NEG = -60000.0


class KB:
    def __init__(self, nc, es):
        self.nc = nc
        self.es = es
        self.eng = {"pe": nc.tensor, "act": nc.scalar, "dve": nc.vector, "pool": nc.gpsimd, "sp": nc.sync}
        self.psem = {}
        self.pcnt = {}
        for e in ("pe", "act", "dve", "pool"):
            self.psem[e] = es.enter_context(nc.semaphore("prog_" + e))
            self.pcnt[e] = 0
        self.waited = {}
        self.nsem = 0

    def sb(self, name, shape, dt):
        return self.es.enter_context(self.nc.sbuf_tensor(name, shape, dt))

    def ps(self, name, shape, dt):
        return self.es.enter_context(self.nc.psum_tensor(name, shape, dt))

    def sem(self, name):
        self.nsem += 1
        return self.es.enter_context(self.nc.semaphore(name))

    def mark(self, instr, e):
        self.pcnt[e] += 1
        instr.then_inc(self.psem[e], 1)
        return (self.psem[e], self.pcnt[e], e)

    def wait(self, e, *toks):
        flat = []
        for tok in toks:
            if isinstance(tok, list):
                flat.extend(tok)
            else:
                flat.append(tok)
        for tok in flat:
            if tok is None:
                continue
            sem, val, src = tok
            key = (e, id(sem))
            if self.waited.get(key, 0) >= val:
                continue
            self.waited[key] = val
            self.eng[e].wait_ge(sem, val)


class DmaSem:
    def __init__(self, kb, name):
        self.sem = kb.sem(name)
        self.val = 0

    def inc(self, instr, n=1):
        self.val += 16
        instr.then_inc(self.sem, 16)
        return (self.sem, self.val, "dma")

    def tok(self):
        return (self.sem, self.val, "dma")


def emit_norm_tile(kb, x_rows, grep, xbuf, xn, ss, rs, hT, tpps, tok0, st):
    nc = kb.nc
    kb.wait("sp", st.get("xbuf_free"))
    ld = st["xsem"].inc(nc.sync.dma_start(out=xbuf[:], in_=x_rows))
    kb.wait("act", ld, st.get("xn_free"))
    t_sq = kb.mark(nc.scalar.activation(out=xn[:], in_=xbuf[:], func=AF.Square, accum_out=ss[:]), "act")
    kb.wait("act", t_sq)
    t_act = kb.mark(nc.scalar.activation(out=rs[:], in_=ss[:], func=AF.Sqrt, scale=1.0 / D, bias=st["eps"][:]), "act")
    kb.wait("dve", t_act)
    t_rc = kb.mark(nc.vector.reciprocal(out=rs[:], in_=rs[:]), "dve")
    kb.wait("dve", t_rc)
    t_xn = kb.mark(nc.vector.scalar_tensor_tensor(out=xn[:], in0=xbuf[:], scalar=rs[:, 0:1], in1=grep[:],
                                                  op0=ALU.mult, op1=ALU.mult), "dve")
    st["xbuf_free"] = t_xn
    kb.wait("pe", t_xn)
    last_pe = None
    for c8 in range(NCH // 8):
        slot = st["tp_i"] % len(tpps)
        st["tp_i"] += 1
        kb.wait("pe", st["tp_free"][slot])
        for j in range(8):
            c = c8 * 8 + j
            ins = nc.tensor.transpose(out=tpps[slot][:, j, :], in_=xn[:, c * 128:(c + 1) * 128], identity=st["ident"][:])
        t_pe = kb.mark(ins, "pe")
        last_pe = t_pe
        kb.wait("dve", t_pe, st.get("hT_free"))
        t_ev = kb.mark(nc.vector.tensor_copy(out=hT[:, c8 * 8:(c8 + 1) * 8, tok0:tok0 + 128], in_=tpps[slot][:, :, :]), "dve")
        st["tp_free"][slot] = t_ev
        st["hT_ready"] = t_ev
    st["xn_free"] = last_pe


def make_ident(kb, ident_f, ident):
    nc = kb.nc
    t = kb.mark(nc.gpsimd.memset(ident_f[:], 0.0), "pool")
    kb.wait("pool", t)
    t = kb.mark(nc.gpsimd.affine_select(out=ident_f[:], in_=ident_f[:], pattern=[[-1, 128]], compare_op=ALU.not_equal,
                                        fill=1.0, base=0, channel_multiplier=1), "pool")
    kb.wait("pool", t)
    return kb.mark(nc.gpsimd.tensor_copy(out=ident[:], in_=ident_f[:]), "pool")


def build_phase_a(TPC=TPC):
    nc = bass.Bass("TRN2", target_bir_lowering=False)
    x = nc.dram_tensor("x", [TPC, D], F32, kind="ExternalInput").ap()
    g = nc.dram_tensor("g", [128, D], F32, kind="ExternalInput").ap()
    wa = nc.dram_tensor("wa", [12, 128, NCH * 512], BF16, kind="ExternalInput").ap()
    wf = nc.dram_tensor("wf", [128, NCH * NH], BF16, kind="ExternalInput").ap()
    qk = nc.dram_tensor("qk", [4, NH, 128, TPC], F32, kind="ExternalOutput").ap()
    v = nc.dram_tensor("v", [2, TPC, W], F32, kind="ExternalOutput").ap()
    fo = nc.dram_tensor("fo", [NH, TPC], F32, kind="ExternalOutput").ap()
    with ExitStack() as es:
        kb = KB(nc, es)
        grep = kb.sb("grep", [128, D], F32)
        xbuf = kb.sb("xbuf", [128, D], F32)
        xn = kb.sb("xn", [128, D], BF16)
        ss = kb.sb("ss", [128, 1], F32)
        rs = kb.sb("rs", [128, 1], F32)
        epsT = kb.sb("epsT", [128, 1], F32)
        ident_f = kb.sb("ident_f", [128, 128], F32)
        ident = kb.sb("ident", [128, 128], BF16)
        hT = kb.sb("hT", [128, NCH, TT], BF16)
        NWA = 3
        wblk = [kb.sb(f"wblk{i}", [128, NCH, 512], BF16) for i in range(NWA)]
        wf_b = kb.sb("wf_b", [128, NCH, NH], BF16)
        NOST = 4
        ost = [kb.sb(f"ost{i}", [128, 512], F32) for i in range(NOST)]
        tpps = [kb.ps(f"tpps{i}", [128, 8, 128], BF16) for i in range(2)]
        NPS = 4
        mps = [kb.ps(f"mps{i}", [128, 512], F32) for i in range(NPS)]

        st = {"xsem": DmaSem(kb, "xsem"), "tp_i": 0, "tp_free": [None, None], "ident": ident, "eps": epsT}
        csem = DmaSem(kb, "csem")
        wl_sem = [DmaSem(kb, f"wlsem{i}") for i in range(NWA)]
        wblk_free = [None] * NWA
        ost_sem = [DmaSem(kb, f"ostsem{i}") for i in range(NOST)]
        mps_free = [None] * NPS
        fin = []

        t_id = make_ident(kb, ident_f, ident)
        t_eps = kb.mark(nc.vector.memset(epsT[:], EPS), "dve")
        c1 = csem.inc(nc.sync.dma_start(out=grep[:], in_=g[:, :]))
        c2 = csem.inc(nc.sync.dma_start(out=wf_b[:].rearrange("p c n -> p (c n)"), in_=wf[:, :]))
        kb.wait("dve", c2)
        kb.wait("pe", t_id, c2)
        kb.wait("act", t_eps)

        piece_i = 0
        blk_i = 0
        mps_i = 0
        ost_i = 0
        last_mm = None
        for tt in range(TPC // TT):
            st["hT_free"] = last_mm
            for s in range(TT // 128):
                r0 = tt * TT + s * 128
                emit_norm_tile(kb, x[r0:r0 + 128, :], grep, xbuf, xn, ss, rs, hT, tpps, s * 128, st)
            kb.wait("pe", st["hT_ready"])
            ms = mps_i % NPS
            mps_i += 1
            kb.wait("pe", mps_free[ms])
            for c in range(NCH):
                ins = nc.tensor.matmul(mps[ms][0:NH, :], lhsT=wf_b[:, c, :], rhs=hT[:, c, :], start=(c == 0), stop=(c == NCH - 1))
            t_mm = kb.mark(ins, "pe")
            os_ = ost_i % NOST
            ost_i += 1
            kb.wait("act", t_mm, ost_sem[os_].tok())
            t_ev = kb.mark(nc.scalar.copy(out=ost[os_][0:NH, :], in_=mps[ms][0:NH, :]), "act")
            mps_free[ms] = t_ev
            kb.wait("act", t_ev)
            ost_sem[os_].inc(nc.scalar.dma_start(out=fo[:, tt * TT:(tt + 1) * TT], in_=ost[os_][0:NH, :]))
            for blk in range(12):
                kind = blk // 2
                half = blk % 2
                wslot = blk_i % NWA
                blk_i += 1
                kb.wait("sp", wblk_free[wslot])
                t_cast = wl_sem[wslot].inc(nc.sync.dma_start(out=wblk[wslot][:].rearrange("p c n -> p (c n)"), in_=wa[blk, :, :]))
                kb.wait("pe", t_cast)
                for j in range(4):
                    ms = mps_i % NPS
                    mps_i += 1
                    kb.wait("pe", mps_free[ms])
                    for c in range(NCH):
                        if kind in (2, 5):
                            ins = nc.tensor.matmul(mps[ms][:], lhsT=hT[:, c, j * 128:(j + 1) * 128], rhs=wblk[wslot][:, c, :],
                                                   start=(c == 0), stop=(c == NCH - 1))
                        else:
                            ins = nc.tensor.matmul(mps[ms][:], lhsT=wblk[wslot][:, c, j * 128:(j + 1) * 128], rhs=hT[:, c, :],
                                                   start=(c == 0), stop=(c == NCH - 1))
                    t_mm = kb.mark(ins, "pe")
                    last_mm = t_mm
                    os_ = ost_i % NOST
                    ost_i += 1
                    kb.wait("act", t_mm, ost_sem[os_].tok())
                    t_ev = kb.mark(nc.scalar.copy(out=ost[os_][:], in_=mps[ms][:]), "act")
                    mps_free[ms] = t_ev
                    if kind in (2, 5):
                        dst = v[kind // 3, tt * TT + j * 128: tt * TT + (j + 1) * 128, half * 512:(half + 1) * 512]
                    else:
                        which = {0: 0, 1: 1, 3: 2, 4: 3}[kind]
                        dst = qk[which, half * 4 + j, :, tt * TT:(tt + 1) * TT]
                    kb.wait("act", t_ev)
                    ost_sem[os_].inc(nc.scalar.dma_start(out=dst, in_=ost[os_][:]))
                wblk_free[wslot] = last_mm
        for s_ in ost_sem:
            kb.wait("act", s_.tok())
    return nc


_CACHE = {}


def get_prog(name):
    if name not in _CACHE:
        _CACHE[name] = {"a": build_phase_a, "b": build_phase_b, "c": build_phase_c, "d": build_phase_d}[name]()
    return _CACHE[name]


def prep_a_f32(w_in):
    cols = np.concatenate([np.arange(0, 3 * W), np.arange(4 * W, 7 * W)])
    wa = w_in[:, cols].reshape(NCH, 128, 12, 512).transpose(2, 1, 0, 3).reshape(12, 128, NCH * 512)
    wf = w_in[:, 8 * W:8 * W + NH].reshape(NCH, 128, NH).transpose(1, 0, 2).reshape(128, NCH * NH)
    return {"wa": np.ascontiguousarray(wa), "wf": np.ascontiguousarray(wf)}


def run_phase_a(x_flat, g, wq):
    nc = get_prog("a")
    grep = np.ascontiguousarray(np.broadcast_to(g[None, :], (128, D)))
    in_maps = [{"x": np.ascontiguousarray(x_flat[c * TPC:(c + 1) * TPC]), "g": grep, "wa": wq["wa"], "wf": wq["wf"]} for c in range(NCORES)]
    res = run_bass_kernel_spmd(nc, in_maps, core_ids=list(range(NCORES)))
    qk = np.concatenate([r["qk"] for r in res.results], axis=3)
    v = np.concatenate([r["v"] for r in res.results], axis=1)
    fo = np.concatenate([r["fo"] for r in res.results], axis=1)
    return qk, v, fo


def build_phase_b(SEQ=SEQ, NB=BATCH):
    nc = bass.Bass("TRN2", target_bir_lowering=False)
    NKC = SEQ // 128
    NQT = SEQ // 512
    NBLK = SEQ // 256
    NPC = SEQ // 2048
    qk = nc.dram_tensor("qk", [NB, 4, 128, SEQ], F32, kind="ExternalInput").ap()
    vv = nc.dram_tensor("vv", [NB, 2, SEQ, 128], F32, kind="ExternalInput").ap()
    ff = nc.dram_tensor("ff", [NB, 1, SEQ], F32, kind="ExternalInput").ap()
    bf = nc.dram_tensor("bf", [1, 1], F32, kind="ExternalInput").ap()
    oo = nc.dram_tensor("oo", [NB, 2, 128, SEQ], F32, kind="ExternalOutput").ap()
    with ExitStack() as es:
        kb = KB(nc, es)
        qTs = [kb.sb(f"qT{i}", [128, SEQ], BF16) for i in range(2)]
        kTs = [kb.sb(f"kT{i}", [128, SEQ], BF16) for i in range(2)]
        vSs = [kb.sb(f"vS{i}", [128, NKC, 128], BF16) for i in range(2)]
        crep = kb.sb("crep", [128, SEQ], F32)
        negc = kb.sb("negc", [128, NKC], F32)
        NSTG = 2
        stg = [kb.sb(f"stg{i}", [128, 2048], F32) for i in range(NSTG)]
        biasT = kb.sb("biasT", [32, SEQ], BF16)
        gate_all = kb.sb("gate_all", [128, SEQ // 128, 32], F32)
        ksum = kb.sb("ksum", [128, 32], F32)
        gate_m = kb.sb("gate_m", [128, 2, 32], F32)
        top8 = kb.sb("top8", [128, 2, 8], F32)
        selb = kb.sb("selb", [128, 2, 32], F32)
        E_all = kb.sb("E_all", [32, 32, 128], BF16)
        ident_f = kb.sb("ident_f", [128, 128], F32)
        ident = kb.sb("ident", [128, 128], BF16)
        tri_f = kb.sb("tri_f", [128, 128], F32)
        tri = kb.sb("tri", [128, 128], BF16)
        ones_b = kb.sb("ones_b", [128, 128], BF16)
        onesrow = kb.sb("onesrow", [1, 128], F32)
        negone = kb.sb("negone", [1, 2], F32)
        one1 = kb.sb("one1", [1, 1], F32)
        negb = kb.sb("negb", [1, 1], F32)
        lrow = kb.sb("lrow", [1, 2048], F32)
        carry = kb.sb("carry", [1, 1], F32)
        NPT = 5
        PT = [kb.sb(f"PT{i}", [128, 512], BF16) for i in range(NPT)]
        tmpS = [kb.sb(f"tmpS{i}", [128, 512], F32) for i in range(NPT)]
        NOS = 2
        oS = [kb.sb(f"oS{i}", [128, 512], F32) for i in range(NOS)]
        Sps = [kb.ps(f"Sps{i}", [128, 512], F32) for i in range(NPT)]
        accps = kb.ps("accps", [128, 512], F32)
        denps = kb.ps("denps", [128, 512], F32)
        mscps = [kb.ps(f"mscps{i}", [128, 512], F32) for i in range(1)]

        ldsem = [DmaSem(kb, f"ldsem{i}") for i in range(NSTG)]
        csem = DmaSem(kb, "csem")
        osem = [DmaSem(kb, f"osem{i}") for i in range(NOS)]
        stg_free = [None] * NSTG
        st = {"stg_i": 0, "pt_i": 0, "os_i": 0, "msc_i": 0}
        S_free = [None] * NPT
        PT_free = [None] * NPT
        tmp_free = [None] * NPT
        msc_free = [None]
        last = {"pe": None, "act": None, "dve": None, "pool": None}

        def chain(e, ins, *deps):
            kb.wait(e, last[e], *deps)
            t = kb.mark(ins(), e)
            last[e] = t
            return t

        t = chain("pool", lambda: nc.gpsimd.memset(ident_f[:], 0.0))
        t = chain("pool", lambda: nc.gpsimd.affine_select(out=ident_f[:], in_=ident_f[:], pattern=[[-1, 128]],
                                                          compare_op=ALU.not_equal, fill=1.0, base=0, channel_multiplier=1))
        t = chain("pool", lambda: nc.gpsimd.tensor_copy(out=ident[:], in_=ident_f[:]))
        t = chain("pool", lambda: nc.gpsimd.memset(tri_f[:], 0.0))
        t = chain("pool", lambda: nc.gpsimd.affine_select(out=tri_f[:], in_=tri_f[:], pattern=[[1, 128]],
                                                          compare_op=ALU.is_ge, fill=NEG, base=0, channel_multiplier=-1))
        t = chain("pool", lambda: nc.gpsimd.tensor_copy(out=tri[:], in_=tri_f[:]))
        t = chain("pool", lambda: nc.gpsimd.memset(E_all[:], 0.0))
        t = chain("pool", lambda: nc.gpsimd.affine_select(out=E_all[:], in_=E_all[:], pattern=[[-1, 32], [0, 128]],
                                                          compare_op=ALU.not_equal, fill=1.0, base=0, channel_multiplier=1))
        t = chain("pool", lambda: nc.gpsimd.memset(ones_b[:], 1.0))
        t = chain("pool", lambda: nc.gpsimd.memset(onesrow[:], 1.0))
        t = chain("pool", lambda: nc.gpsimd.memset(negone[:], -SCALE))
        t = chain("pool", lambda: nc.gpsimd.memset(one1[:], 1.0))
        t_const = chain("pool", lambda: nc.gpsimd.memset(carry[:], 0.0))
        cb = csem.inc(nc.sync.dma_start(out=negb[:], in_=bf[:, :]))
        t_negb = chain("dve", lambda: nc.vector.tensor_scalar(out=negb[:], in0=negb[:], scalar1=-1.0, scalar2=None, op0=ALU.mult), cb)
        chain("dve", lambda: nc.vector.memset(ksum[:], 0.0))
        chain("dve", lambda: nc.vector.memset(gate_all[:], 0.0))
        kb.wait("pe", t_const)
        kb.wait("act", t_const)
        kb.wait("dve", t_const)

        def load_piece(src_ap, shape_view=None):
            s_ = st["stg_i"] % NSTG
            st["stg_i"] += 1
            kb.wait("sp", stg_free[s_])
            dst = stg[s_][:] if shape_view is None else shape_view(stg[s_])
            tok = ldsem[s_].inc(nc.sync.dma_start(out=dst, in_=src_ap))
            return s_, tok

        def setup(b, which, attn_done, out):
            moba = (which == 0)
            qT, kT, vS = qTs[which], kTs[which], vSs[which]
            t_k = None
            for p in range(NPC):
                s_, tok = load_piece(qk[b, 2 * which + 1, :, p * 2048:(p + 1) * 2048])
                kb.wait("pool", tok, attn_done)
                t_k = kb.mark(nc.gpsimd.tensor_copy(out=kT[:, p * 2048:(p + 1) * 2048], in_=stg[s_][:]), "pool")
                fr = [t_k]
                if moba:
                    t_ks = chain("dve", lambda: nc.vector.tensor_reduce(
                        out=ksum[:, p * 8:(p + 1) * 8], in_=stg[s_][:].rearrange("p (n t) -> p n t", t=256), axis=AX.X, op=ALU.add),
                        tok, attn_done)
                    fr.append(t_ks)
                stg_free[s_] = fr
                yield
            t_v = None
            for p in range(NPC):
                s_, tok = load_piece(vv[b, which, p * 2048:(p + 1) * 2048, :].rearrange("(c p) d -> p c d", p=128),
                                     lambda tl: tl[:].rearrange("p (c d) -> p c d", d=128))
                kb.wait("pool", tok, attn_done)
                t_v = kb.mark(nc.gpsimd.tensor_copy(out=vS[:, p * 16:(p + 1) * 16, :],
                                                   in_=stg[s_][:].rearrange("p (c d) -> p c d", d=128)), "pool")
                stg_free[s_] = [t_v]
                yield
            t_q = None
            t_gate = None
            for p in range(NPC):
                s_, tok = load_piece(qk[b, 2 * which, :, p * 2048:(p + 1) * 2048])
                kb.wait("pool", tok, attn_done)
                t_q = kb.mark(nc.gpsimd.tensor_copy(out=qT[:, p * 2048:(p + 1) * 2048], in_=stg[s_][:]), "pool")
                fr = [t_q]
                if moba:
                    m_ = st["msc_i"] % 1
                    st["msc_i"] += 1
                    kb.wait("pe", tok, last["dve"], msc_free[m_])
                    gv = mscps[m_][:].rearrange("p (t n) -> p t n", n=32)
                    for j in range(16):
                        ins = nc.tensor.matmul(gv[:, j, :], lhsT=stg[s_][:, j * 128:(j + 1) * 128], rhs=ksum[:, :], start=True, stop=True)
                    t_g = kb.mark(ins, "pe")
                    last["pe"] = t_g
                    fr.append(t_g)
                    t_gate = chain("dve", lambda: nc.vector.tensor_copy(out=gate_all[:, p * 16:(p + 1) * 16, :], in_=gv), t_g)
                    msc_free[m_] = t_gate
                stg_free[s_] = fr
                yield
            t_bias = None
            if moba:
                for qb in range(NBLK):
                    chain("dve", lambda: nc.vector.memset(gate_m[:], -1e30))
                    if qb > 0:
                        chain("dve", lambda: nc.vector.tensor_copy(out=gate_m[:, :, 0:qb], in_=gate_all[:, 2 * qb:2 * qb + 2, 0:qb]))
                    for j in range(2):
                        chain("dve", lambda: nc.vector.max(out=top8[:, j, :], in_=gate_m[:, j, :]))
                        chain("dve", lambda: nc.vector.tensor_scalar(out=selb[:, j, :], in0=gate_m[:, j, :], scalar1=top8[:, j, 2:3],
                                                                    scalar2=None, op0=ALU.is_ge))
                    chain("dve", lambda: nc.vector.tensor_scalar(out=selb[:], in0=selb[:], scalar1=-1.0, scalar2=-NEG,
                                                                op0=ALU.add, op1=ALU.mult))
                    if qb + 1 < 32:
                        chain("dve", lambda: nc.vector.memset(selb[:, :, qb + 1:32], NEG))
                    t_sel = chain("dve", lambda: nc.vector.memset(selb[:, :, qb:qb + 1], 0.0))
                    m_ = st["msc_i"] % 1
                    st["msc_i"] += 1
                    kb.wait("pe", t_sel, msc_free[m_])
                    for j in range(2):
                        ins = nc.tensor.transpose(out=mscps[m_][0:32, j * 128:(j + 1) * 128], in_=selb[:, j, :], identity=ident_f[:])
                    t_tr = kb.mark(ins, "pe")
                    last["pe"] = t_tr
                    t_bias = chain("dve", lambda: nc.vector.tensor_copy(out=biasT[:, qb * 256:(qb + 1) * 256], in_=mscps[m_][0:32, 0:256]),
                                   t_tr, attn_done)
                    msc_free[m_] = t_bias
                    yield
            else:
                for p in range(NPC):
                    s_, tok = load_piece(ff[b, :, p * 2048:(p + 1) * 2048], lambda tl: tl[0:1, :])
                    fr_ = stg[s_][0:1, :]
                    chain("act", lambda: nc.scalar.activation(out=fr_, in_=fr_, func=AF.Exp, scale=-1.0, bias=negb[:]), tok, t_negb)
                    t_l = chain("act", lambda: nc.scalar.activation(out=fr_, in_=fr_, func=AF.Ln, scale=1.0, bias=one1[:]))
                    chain("dve", lambda: nc.vector.tensor_scalar(out=fr_, in0=fr_, scalar1=-1.0 / SCALE, scalar2=None, op0=ALU.mult), t_l)
                    init = 0.0 if p == 0 else carry[:, 0:1]
                    kb.wait("dve", last["pe"])
                    t_sc = chain("dve", lambda: nc.vector.tensor_tensor_scan(out=lrow[:], data0=one1[:, 0:1].to_broadcast([1, 2048]), data1=fr_,
                                                                           initial=init, op0=ALU.mult, op1=ALU.add))
                    stg_free[s_] = [t_sc]
                    t_sc = chain("dve", lambda: nc.vector.tensor_copy(out=carry[:], in_=lrow[:, 2047:2048]))
                    for i in range(4):
                        m_ = st["msc_i"] % 1
                        st["msc_i"] += 1
                        kb.wait("pe", t_sc, msc_free[m_])
                        t_mm = kb.mark(nc.tensor.matmul(mscps[m_][:], lhsT=onesrow[:], rhs=lrow[:, i * 512:(i + 1) * 512], start=True, stop=True), "pe")
                        last["pe"] = t_mm
                        t_cr = chain("dve", lambda: nc.vector.tensor_copy(out=crep[:, p * 2048 + i * 512:p * 2048 + (i + 1) * 512], in_=mscps[m_][:]),
                                     t_mm, attn_done)
                        msc_free[m_] = t_cr
                    m_ = st["msc_i"] % 1
                    st["msc_i"] += 1
                    kb.wait("pe", t_sc, msc_free[m_])
                    for kc in range(16):
                        ins = nc.tensor.matmul(mscps[m_][:, 2 * kc:2 * kc + 2], lhsT=lrow[:, kc * 128:(kc + 1) * 128], rhs=negone[:], start=True, stop=True)
                    t_mm = kb.mark(ins, "pe")
                    last["pe"] = t_mm
                    t_bias = chain("dve", lambda: nc.vector.tensor_copy(
                        out=negc[:, p * 16:(p + 1) * 16], in_=mscps[m_][:, 0:32].rearrange("p (c two) -> p c two", two=2)[:, :, 0]), t_mm, attn_done)
                    msc_free[m_] = t_bias
                    yield
            out["toks"] = [t_k, t_v, t_q, t_bias]
            yield

        def main(b, which, toks, bg):
            moba = (which == 0)
            qT, kT, vS = qTs[which], kTs[which], vSs[which]
            t_k, t_v, t_q, t_bias = toks
            kb.wait("pe", t_k, t_v, t_q, t_bias)
            units = [(Q, kc) for Q in range(NQT) for kc in range(4 * Q + 4)]
            res = {"last_pv": None}

            def emit_qk(Q, kc):
                q0 = Q * 512
                c0 = max(0, kc * 128 - q0)
                diag = kc * 128 >= q0
                n = kc // 2
                sl = st["pt_i"] % NPT
                st["pt_i"] += 1
                kb.wait("pe", S_free[sl])
                ins = nc.tensor.matmul(Sps[sl][:, c0:512], lhsT=kT[:, kc * 128:(kc + 1) * 128], rhs=qT[:, q0 + c0:q0 + 512],
                                       start=True, stop=not (moba or diag))
                if moba:
                    ins = nc.tensor.matmul(Sps[sl][:, c0:512], lhsT=E_all[:, n, :], rhs=biasT[:, q0 + c0:q0 + 512],
                                           start=False, stop=not diag)
                if diag:
                    ins = nc.tensor.matmul(Sps[sl][:, c0:c0 + 128], lhsT=ident[:], rhs=tri[:], start=False, stop=True)
                t_s = kb.mark(ins, "pe")
                if moba:
                    kb.wait("act", t_s, PT_free[sl])
                    t_p = kb.mark(nc.scalar.activation(out=PT[sl][:, c0:512], in_=Sps[sl][:, c0:512], func=AF.Exp, scale=SCALE), "act")
                    S_free[sl] = t_p
                else:
                    kb.wait("dve", t_s, tmp_free[sl])
                    t_t = kb.mark(nc.vector.tensor_tensor(out=tmpS[sl][:, c0:512], in0=Sps[sl][:, c0:512],
                                                          in1=crep[:, q0 + c0:q0 + 512], op=ALU.add), "dve")
                    S_free[sl] = t_t
                    kb.wait("act", t_t, PT_free[sl])
                    t_p = kb.mark(nc.scalar.activation(out=PT[sl][:, c0:512], in_=tmpS[sl][:, c0:512], func=AF.Exp, scale=SCALE,
                                                       bias=negc[:, kc:kc + 1]), "act")
                    tmp_free[sl] = t_p
                return (Q, kc, sl, c0, t_p)

            def emit_pv(info):
                Q, kc, sl, c0, t_p = info
                q0 = Q * 512
                nkc = 4 * Q + 4
                first = (kc == 0)
                kb.wait("pe", t_p)
                if first:
                    kb.wait("pe", st.get("acc_free"))
                nc.tensor.matmul(accps[:, c0:512], lhsT=vS[:, kc, :], rhs=PT[sl][:, c0:512], start=first, stop=(kc == nkc - 1))
                ins = nc.tensor.matmul(denps[:, c0:512], lhsT=ones_b[:], rhs=PT[sl][:, c0:512], start=first, stop=(kc == nkc - 1))
                t_pv = kb.mark(ins, "pe")
                PT_free[sl] = t_pv
                res["last_pv"] = t_pv
                if kc == nkc - 1:
                    o_ = st["os_i"] % NOS
                    st["os_i"] += 1
                    t_r = chain("dve", lambda: nc.vector.reciprocal(out=oS[o_][:], in_=denps[:]), t_pv, osem[o_].tok())
                    t_o = chain("dve", lambda: nc.vector.tensor_tensor(out=oS[o_][:], in0=accps[:], in1=oS[o_][:], op=ALU.mult))
                    st["acc_free"] = t_o
                    kb.wait("sp", t_o)
                    osem[o_].inc(nc.sync.dma_start(out=oo[b, which, :, q0:q0 + 512], in_=oS[o_][:]))

            LAG = 4
            pend = []
            for ui, (Q, kc) in enumerate(units):
                pend.append(emit_qk(Q, kc))
                if len(pend) > LAG:
                    emit_pv(pend.pop(0))
                if bg is not None and ui % 6 == 5:
                    next(bg, None)
            while pend:
                emit_pv(pend.pop(0))
            t_last_pv = res["last_pv"]
            return t_last_pv

        order = [(b, which) for b in range(NB) for which in range(2)]
        outs = [dict() for _ in order]
        dones = [None] * len(order)
        gens = []
        for a, (b, which) in enumerate(order):
            gens.append(None)
        g0 = setup(order[0][0], order[0][1], None, outs[0])
        for _ in g0:
            pass
        for a, (b, which) in enumerate(order):
            bg = None
            if a + 1 < len(order):
                nb_, nw_ = order[a + 1]
                prev_same = dones[a - 1] if a - 1 >= 0 else None
                bg = setup(nb_, nw_, prev_same, outs[a + 1])
            dones[a] = main(b, which, outs[a]["toks"], bg)
            if bg is not None:
                for _ in bg:
                    pass
        for s_ in osem:
            kb.wait("sp", s_.tok())
    return nc


def run_phase_b(qk, v, fo, b_forget):
    nc = get_prog("b")
    in_maps = []
    for h in range(NCORES):
        qk_h = np.ascontiguousarray(qk[:, h].reshape(4, 128, BATCH, SEQ).transpose(2, 0, 1, 3))
        v_h = np.ascontiguousarray(v[:, :, h * 128:(h + 1) * 128].reshape(2, BATCH, SEQ, 128).transpose(1, 0, 2, 3))
        f_h = np.ascontiguousarray(fo[h].reshape(BATCH, 1, SEQ))
        in_maps.append({"qk": qk_h, "vv": v_h, "ff": f_h, "bf": np.ascontiguousarray(b_forget[h].reshape(1, 1))})
    res = run_bass_kernel_spmd(nc, in_maps, core_ids=list(range(NCORES)))
    return np.stack([r["oo"] for r in res.results], axis=0)


HAL = 16


def build_phase_c(TPC=TPC, T=512):
    nc = bass.Bass("TRN2", target_bir_lowering=False)
    NT = TPC // T
    x = nc.dram_tensor("x", [TPC, D], F32, kind="ExternalInput").ap()
    xh = nc.dram_tensor("xh", [128, D], F32, kind="ExternalInput").ap()
    gcol = nc.dram_tensor("gcol", [128, NCH], F32, kind="ExternalInput").ap()
    NBT = 32 + 80 + 16
    wq = nc.dram_tensor("wq", [NBT, 128, NCH * 256], BF16, kind="ExternalInput").ap()
    oa = nc.dram_tensor("oa", [2, W, TPC], F32, kind="ExternalInput").ap()
    icnt = nc.dram_tensor("icnt", [128, 4, TPC], F32, kind="ExternalInput").ap()
    wpool = nc.dram_tensor("wpool", [128, 8, 256], F32, kind="ExternalInput").ap()
    pscale = nc.dram_tensor("pscale", [128, 8], F32, kind="ExternalInput").ap()
    wconv = nc.dram_tensor("wconv", [128, 3, 8], F32, kind="ExternalInput").ap()
    bmerge = nc.dram_tensor("bmerge", [128, 4, NCH], F32, kind="ExternalInput").ap()
    y = nc.dram_tensor("y", [TPC, D], F32, kind="ExternalOutput").ap()
    TE = T + HAL
    with ExitStack() as es:
        kb = KB(nc, es)
        hT = kb.sb("hT", [128, NCH, TE], BF16)
        br_raw = kb.sb("br_raw", [128, NCH * T], BF16)
        brT = br_raw[:].rearrange("p (c t) -> p c t", t=T)
        assert NCH * T >= 12288
        mg_raw = kb.sb("mg_raw", [128, NCH * T], BF16)
        mgT = mg_raw[:].rearrange("p (c t) -> p c t", t=T)
        xbuf = mg_raw[:, 0:8192].bitcast(F32)
        xn = mg_raw[:, 8192:12288]
        sqjunk = mg_raw[:, 12288:16384]
        xbufs = [xbuf, br_raw[:, 0:8192].bitcast(F32)]
        NW = 3
        wblk = [kb.sb(f"wblk{i}", [128, NCH, 256], BF16) for i in range(NW)]
        ssb = kb.sb("ssb", [128, 8], F32)
        rsb = kb.sb("rsb", [128, 8], F32)
        epsT = kb.sb("epsT", [128, 1], F32)
        gcolS = kb.sb("gcolS", [128, NCH], F32)
        ident_f = kb.sb("ident_f", [128, 128], F32)
        ident = kb.sb("ident", [128, 128], BF16)
        icS = kb.sb("icS", [128, 4, T], F32)
        assert T == 512
        wpool_f = icS[:].rearrange("p a (b c) -> p (a b) c", c=256)
        wpool_b = kb.sb("wpool_b", [128, 8, 256], BF16)
        pscS = kb.sb("pscS", [128, 8], F32)
        wcvS = kb.sb("wcvS", [128, 3, 8], F32)
        bmS = kb.sb("bmS", [128, 4, NCH], F32)
        NTF = 2
        tmpf = [kb.sb(f"tmpf{i}", [128, T], F32) for i in range(NTF)]
        otile = [kb.sb(f"otile{i}", [128, T], F32) for i in range(2)]
        uext = kb.sb("uext", [128, 2, TE], F32)
        pa = kb.sb("pa", [128, 2, TE], F32)
        pb = kb.sb("pb", [128, 2, TE], F32)
        pooledT = kb.sb("pooledT", [128, 2, T], BF16)
        yS = kb.sb("yS", [128, 2, T], F32)
        bS = kb.sb("bS", [128, T], F32)
        cext = kb.sb("cext", [128, TE], F32)
        zext = kb.sb("zext", [128, TE], F32)
        ycv = kb.sb("ycv", [128, T], F32)
        sgT = kb.sb("sgT", [128, 4, 2, T], BF16)
        accm = kb.sb("accm", [128, T], F32)
        xres = [kb.sb(f"xres{i}", [128, 256], F32) for i in range(2)]
        orow = [kb.sb(f"orow{i}", [128, 256], F32) for i in range(2)]
        tpps = [kb.ps(f"tpps{i}", [128, 8, 128], BF16) for i in range(2)]
        NPS = 4
        mps = [kb.ps(f"mps{i}", [128, 512], F32) for i in range(NPS)]
        hps = kb.ps("hps", [128, 512], F32)

        last = {"pe": None, "act": None, "dve": None, "pool": None}

        def chain(e, ins, *deps):
            kb.wait(e, last[e], *deps)
            t = kb.mark(ins(), e)
            last[e] = t
            return t

        def pe_mark(ins):
            t = kb.mark(ins, "pe")
            last["pe"] = t
            return t

        csem = DmaSem(kb, "csem")
        xsem = [DmaSem(kb, "xsem0"), DmaSem(kb, "xsem1")]
        wblk_free = [None] * NW
        wl_sem = [DmaSem(kb, f"wlsem{i}") for i in range(NW)]
        osem = [DmaSem(kb, f"osem{i}") for i in range(2)]
        otile_free = [None, None]
        xrsem = [DmaSem(kb, f"xrsem{i}") for i in range(2)]
        xres_free = [None, None]
        orsem = [DmaSem(kb, f"orsem{i}") for i in range(2)]
        icsem = DmaSem(kb, "icsem")
        mps_free = [None] * NPS
        hps_free = [None]
        tp_free = [None, None]
        cnt = {"piece": 0, "blk": 0, "mps": 0, "tp": 0, "tf": 0, "ot": 0, "xr": 0}

        t = chain("pool", lambda: nc.gpsimd.memset(ident_f[:], 0.0))
        t = chain("pool", lambda: nc.gpsimd.affine_select(out=ident_f[:], in_=ident_f[:], pattern=[[-1, 128]],
                                                          compare_op=ALU.not_equal, fill=1.0, base=0, channel_multiplier=1))
        t_id = chain("pool", lambda: nc.gpsimd.tensor_copy(out=ident[:], in_=ident_f[:]))
        c_all = None
        for dst, src in ((gcolS, gcol), (wpool_f, wpool), (pscS, pscale), (wcvS, wconv), (bmS, bmerge)):
            c_all = csem.inc(nc.sync.dma_start(out=dst[:], in_=src))
        chain("dve", lambda: nc.vector.memset(epsT[:], EPS))
        t_c = chain("dve", lambda: nc.vector.tensor_copy(out=wpool_b[:], in_=wpool_f[:]), c_all)
        kb.wait("pe", t_id, t_c)
        kb.wait("act", t_c)

        def norm_phase(items):
            start = [last["dve"], last["pe"]]
            lds = {}
            t_xns = {}

            def issue_load(i):
                kb.wait("sp", *start, t_xns.get(i - 2))
                lds[i] = xsem[i % 2].inc(nc.sync.dma_start(out=xbufs[i % 2][:], in_=items[i][0]))

            for i in range(min(2, len(items))):
                issue_load(i)
            for i, (x_rows, dst_col0, src_c0, ncols) in enumerate(items):
                xb = xbufs[i % 2]
                t_sq = chain("act", lambda: nc.scalar.activation(out=sqjunk, in_=xb[:], func=AF.Square, accum_out=ssb[:, i:i + 1]), lds[i], *start)
                t_a = chain("act", lambda: nc.scalar.activation(out=rsb[:, i:i + 1], in_=ssb[:, i:i + 1], func=AF.Sqrt, scale=1.0 / D, bias=epsT[:]))
                chain("dve", lambda: nc.vector.reciprocal(out=rsb[:, i:i + 1], in_=rsb[:, i:i + 1]), t_a)
                t_xn = chain("dve", lambda: nc.vector.tensor_scalar(out=xn, in0=xb[:], scalar1=rsb[:, i:i + 1], scalar2=None, op0=ALU.mult), last["pe"])
                t_xns[i] = t_xn
                if i + 2 < len(items):
                    issue_load(i + 2)
                for c8 in range(NCH // 8):
                    sl = cnt["tp"] % 2
                    cnt["tp"] += 1
                    kb.wait("pe", t_xn, tp_free[sl])
                    for j in range(8):
                        c = c8 * 8 + j
                        ins = nc.tensor.transpose(out=tpps[sl][:, j, :], in_=xn[:, c * 128:(c + 1) * 128], identity=ident[:])
                    t_pe = pe_mark(ins)
                    for j in range(8):
                        c = c8 * 8 + j
                        t_ev = chain("dve", lambda: nc.vector.tensor_scalar(out=hT[:, c, dst_col0:dst_col0 + ncols],
                                                                           in0=tpps[sl][:, j, src_c0:src_c0 + ncols],
                                                                           scalar1=gcolS[:, c:c + 1], scalar2=None, op0=ALU.mult), t_pe)
                    tp_free[sl] = t_ev

        def load_block(view=None, col0=None):
            slot = cnt["blk"] % NW
            bi = cnt["blk"] % NBT
            cnt["blk"] += 1
            kb.wait("sp", wblk_free[slot])
            ld = wl_sem[slot].inc(nc.sync.dma_start(out=wblk[slot][:].rearrange("p c n -> p (c n)"), in_=wq[bi, :, :]))
            kb.wait("pe", ld)
            return slot

        def release(slot):
            wblk_free[slot] = last["pe"]

        def mm_feat(slot, j, halo=False):
            ms = cnt["mps"] % NPS
            cnt["mps"] += 1
            kb.wait("pe", mps_free[ms])
            for c in range(NCH):
                ins = nc.tensor.matmul(mps[ms][:, 0:T], lhsT=wblk[slot][:, c, j * 128:(j + 1) * 128], rhs=hT[:, c, HAL:TE],
                                       start=(c == 0), stop=(c == NCH - 1))
            if halo:
                kb.wait("pe", hps_free[0])
                for c in range(NCH):
                    ins = nc.tensor.matmul(hps[:, 0:HAL], lhsT=wblk[slot][:, c, j * 128:(j + 1) * 128], rhs=hT[:, c, 0:HAL],
                                           start=(c == 0), stop=(c == NCH - 1))
            return ms, pe_mark(ins)

        def get_tmpf():
            i = cnt["tf"] % NTF
            cnt["tf"] += 1
            return tmpf[i]

        pending = []

        for tt in range(NT):
            t0 = tt * T
            items = []
            if tt == 0:
                items.append((xh[:, :], 0, 128 - HAL, HAL))
            else:
                chain("dve", lambda: nc.vector.tensor_copy(out=hT[:, :, 0:HAL], in_=hT[:, :, T:TE]), last["pe"])
            for s in range(T // 128):
                items.append((x[t0 + s * 128:t0 + (s + 1) * 128, :], HAL + s * 128, 0, 128))
            norm_phase(items)
            kb.wait("pe", last["dve"])
            kb.wait("sp", last["dve"])
            t_ic = icsem.inc(nc.sync.dma_start(out=icS[:], in_=icnt[:, :, t0:t0 + T]))
            kb.wait("dve", t_ic)

            for blk in range(32):
                slot = load_block()
                if blk < 8:
                    br = blk // 4
                    for j in range(2):
                        ch = 2 * (blk % 4) + j
                        ms, t_mm = mm_feat(slot, j)
                        tf = get_tmpf()
                        t_a = chain("act", lambda: nc.scalar.activation(out=tf[:], in_=mps[ms][:, 0:T], func=AF.Silu), t_mm, last["dve"])
                        mps_free[ms] = t_a
                        oi = cnt["ot"] % 2
                        cnt["ot"] += 1
                        kb.wait("sp", otile_free[oi])
                        t_o = osem[oi].inc(nc.sync.dma_start(out=otile[oi][:], in_=oa[br, ch * 128:(ch + 1) * 128, t0:t0 + T]))
                        t_d = chain("dve", lambda: nc.vector.tensor_tensor(out=brT[:, br * 8 + ch, :], in0=tf[:], in1=otile[oi][:], op=ALU.mult), t_a, t_o)
                        otile_free[oi] = t_d
                elif blk < 16:
                    g = (blk - 8) // 2
                    if (blk - 8) % 2 == 0:
                        for j in range(2):
                            ms, t_mm = mm_feat(slot, j, halo=True)
                            chain("act", lambda: nc.scalar.copy(out=uext[:, j, HAL:TE], in_=mps[ms][:, 0:T]), t_mm, last["dve"])
                            t_a = chain("act", lambda: nc.scalar.copy(out=uext[:, j, 0:HAL], in_=hps[:, 0:HAL]))
                            mps_free[ms] = t_a
                            hps_free[0] = t_a
                        src = uext
                        bufs = [pa, pb]
                        for k in range(g + 1):
                            sh = 2 ** k
                            dst = bufs[k % 2]
                            lo = 2 * sh - 1
                            chain("dve", lambda: nc.vector.tensor_tensor(out=dst[:, :, lo:TE], in0=src[:, :, lo:TE], in1=src[:, :, lo - sh:TE - sh], op=ALU.add), last["act"])
                            src = dst
                        for j in range(2):
                            tf = get_tmpf()
                            chain("dve", lambda: nc.vector.tensor_tensor(out=tf[:], in0=src[:, j, HAL:TE], in1=icS[:, g, :], op=ALU.mult), last["act"])
                            t_p = chain("dve", lambda: nc.vector.tensor_tensor(out=pooledT[:, j, :], in0=tf[:], in1=uext[:, j, HAL:TE], op=ALU.subtract), last["pe"])
                        for oc in range(2):
                            ms = cnt["mps"] % NPS
                            cnt["mps"] += 1
                            kb.wait("pe", mps_free[ms], t_p)
                            for j in range(2):
                                ins = nc.tensor.matmul(mps[ms][:, 0:T], lhsT=wpool_b[:, 2 * g + j, oc * 128:(oc + 1) * 128], rhs=pooledT[:, j, :],
                                                       start=(j == 0), stop=(j == 1))
                            t_mm = pe_mark(ins)
                            t_y = chain("dve", lambda: nc.vector.tensor_scalar(out=yS[:, oc, :], in0=mps[ms][:, 0:T], scalar1=pscS[:, 2 * g + oc:2 * g + oc + 1],
                                                                              scalar2=None, op0=ALU.mult), t_mm)
                            mps_free[ms] = t_y
                    else:
                        for j in range(2):
                            ms, t_mm = mm_feat(slot, j)
                            tf = get_tmpf()
                            t_a = chain("act", lambda: nc.scalar.activation(out=tf[:], in_=mps[ms][:, 0:T], func=AF.Silu), t_mm, last["dve"])
                            mps_free[ms] = t_a
                            chain("dve", lambda: nc.vector.tensor_tensor(out=brT[:, 16 + 2 * g + j, :], in0=tf[:], in1=yS[:, j, :], op=ALU.mult), t_a)
                else:
                    i = (blk - 16) // 2
                    if (blk - 16) % 2 == 0:
                        ms, t_mm = mm_feat(slot, 0)
                        t_a = chain("act", lambda: nc.scalar.copy(out=bS[:], in_=mps[ms][:, 0:T]), t_mm, last["dve"])
                        mps_free[ms] = t_a
                        ms, t_mm = mm_feat(slot, 1, halo=True)
                        chain("act", lambda: nc.scalar.copy(out=cext[:, HAL:TE], in_=mps[ms][:, 0:T]), t_mm, last["dve"])
                        t_a = chain("act", lambda: nc.scalar.copy(out=cext[:, 0:HAL], in_=hps[:, 0:HAL]))
                        mps_free[ms] = t_a
                        hps_free[0] = t_a
                    else:
                        ms, t_mm = mm_feat(slot, 0, halo=True)
                        chain("dve", lambda: nc.vector.tensor_tensor(out=zext[:, HAL:TE], in0=mps[ms][:, 0:T], in1=cext[:, HAL:TE], op=ALU.mult), t_mm, last["act"])
                        t_d = chain("dve", lambda: nc.vector.tensor_tensor(out=zext[:, 0:HAL], in0=hps[:, 0:HAL], in1=cext[:, 0:HAL], op=ALU.mult))
                        mps_free[ms] = t_d
                        hps_free[0] = t_d
                        chain("dve", lambda: nc.vector.tensor_scalar(out=ycv[:], in0=zext[:, HAL - 2:TE - 2], scalar1=wcvS[:, 0, i:i + 1], scalar2=None, op0=ALU.mult))
                        chain("dve", lambda: nc.vector.scalar_tensor_tensor(out=ycv[:], in0=zext[:, HAL - 1:TE - 1], scalar=wcvS[:, 1, i:i + 1], in1=ycv[:],
                                                                           op0=ALU.mult, op1=ALU.add))
                        chain("dve", lambda: nc.vector.scalar_tensor_tensor(out=ycv[:], in0=zext[:, HAL:TE], scalar=wcvS[:, 2, i:i + 1], in1=ycv[:],
                                                                           op0=ALU.mult, op1=ALU.add))
                        chain("dve", lambda: nc.vector.tensor_tensor(out=ycv[:], in0=ycv[:], in1=bS[:], op=ALU.mult))
                        ms, t_mm = mm_feat(slot, 1)
                        tf = get_tmpf()
                        t_a = chain("act", lambda: nc.scalar.activation(out=tf[:], in_=mps[ms][:, 0:T], func=AF.Silu), t_mm, last["dve"])
                        mps_free[ms] = t_a
                        chain("dve", lambda: nc.vector.tensor_tensor(out=brT[:, 24 + i, :], in0=tf[:], in1=ycv[:], op=ALU.mult), t_a)
                release(slot)

            for dp in range(NCH // 2):
                for mb in range(4):
                    slot = load_block()
                    dcl = mb // 2
                    dc = 2 * dp + dcl
                    for j in range(2):
                        i = 2 * (mb % 2) + j
                        ms, t_mm = mm_feat(slot, j)
                        t_a = chain("act", lambda: nc.scalar.activation(out=sgT[:, i, dcl, :], in_=mps[ms][:, 0:T], func=AF.Sigmoid,
                                                                       bias=bmS[:, i, dc:dc + 1]), t_mm, last["dve"])
                        mps_free[ms] = t_a
                    release(slot)
                slot = load_block()
                kb.wait("pe", last["dve"])
                for dcl in range(2):
                    dc = 2 * dp + dcl
                    for i in range(4):
                        ms = cnt["mps"] % NPS
                        cnt["mps"] += 1
                        kb.wait("pe", mps_free[ms])
                        for wcn in range(8):
                            ins = nc.tensor.matmul(mps[ms][:, 0:T], lhsT=wblk[slot][:, 8 * i + wcn, dcl * 128:(dcl + 1) * 128], rhs=brT[:, 8 * i + wcn, :],
                                                   start=(wcn == 0), stop=(wcn == 7))
                        t_mm = pe_mark(ins)
                        if i == 0:
                            t_d = chain("dve", lambda: nc.vector.tensor_tensor(out=accm[:], in0=mps[ms][:, 0:T], in1=sgT[:, i, dcl, :], op=ALU.mult), t_mm, last["act"])
                        else:
                            tf = get_tmpf()
                            t_d = chain("dve", lambda: nc.vector.tensor_tensor(out=tf[:], in0=mps[ms][:, 0:T], in1=sgT[:, i, dcl, :], op=ALU.mult), t_mm, last["act"])
                            if i < 3:
                                chain("dve", lambda: nc.vector.tensor_tensor(out=accm[:], in0=accm[:], in1=tf[:], op=ALU.add))
                            else:
                                chain("dve", lambda: nc.vector.tensor_tensor(out=mgT[:, dc, :], in0=accm[:], in1=tf[:], op=ALU.add), last["pe"])
                        mps_free[ms] = t_d
                release(slot)

            kb.wait("pe", last["dve"])
            for ob in range(16):
                slot = load_block()
                for st_ in pending:
                    st_()
                pending = []
                for s in range(T // 128):
                    ms = cnt["mps"] % NPS
                    cnt["mps"] += 1
                    kb.wait("pe", mps_free[ms])
                    for c in range(NCH):
                        ins = nc.tensor.matmul(mps[ms][:, 0:256], lhsT=mgT[:, c, s * 128:(s + 1) * 128], rhs=wblk[slot][:, c, :],
                                               start=(c == 0), stop=(c == NCH - 1))
                    t_mm = pe_mark(ins)
                    xi = cnt["xr"] % 2
                    cnt["xr"] += 1
                    kb.wait("pool", xres_free[xi])
                    r0 = t0 + s * 128
                    t_x = xrsem[xi].inc(nc.gpsimd.dma_start(out=xres[xi][:], in_=x[r0:r0 + 128, ob * 256:(ob + 1) * 256]))
                    t_d = chain("dve", lambda: nc.vector.tensor_tensor(out=orow[xi][:], in0=mps[ms][:, 0:256], in1=xres[xi][:], op=ALU.add),
                                t_mm, t_x, orsem[xi].tok())
                    mps_free[ms] = t_d
                    xres_free[xi] = t_d
                    kb.wait("pool", t_d)
                    orsem[xi].inc(nc.gpsimd.dma_start(out=y[r0:r0 + 128, ob * 256:(ob + 1) * 256], in_=orow[xi][:]))
                release(slot)
        for s_ in orsem:
            kb.wait("pool", s_.tok())
    return nc


def prep_c_consts(w_in, w_pool, pool_scale, w_conv, w_branch, b_merge, w_out, g):
    def blk(mat, cols):
        return mat[:, cols].reshape(NCH, 128, 256).transpose(1, 0, 2).reshape(128, NCH * 256)
    blocks = []
    for b in range(4):
        blocks.append(blk(w_in, np.arange(3072 + b * 256, 3072 + (b + 1) * 256)))
    for b in range(4):
        blocks.append(blk(w_in, np.arange(7168 + b * 256, 7168 + (b + 1) * 256)))
    for gg in range(4):
        blocks.append(blk(w_in, np.arange(8200 + gg * 256, 8200 + (gg + 1) * 256)))
        blocks.append(blk(w_in, np.arange(9224 + gg * 256, 9224 + (gg + 1) * 256)))
    for i in range(8):
        blocks.append(blk(w_in, np.concatenate([np.arange(10248 + i * 128, 10248 + (i + 1) * 128), np.arange(11272 + i * 128, 11272 + (i + 1) * 128)])))
        blocks.append(blk(w_in, np.concatenate([np.arange(12296 + i * 128, 12296 + (i + 1) * 128), np.arange(13320 + i * 128, 13320 + (i + 1) * 128)])))
    wbr = w_branch.reshape(D, D)
    for dp in range(16):
        for mb in range(4):
            dc = 2 * dp + mb // 2
            cc = []
            for j in range(2):
                i = 2 * (mb % 2) + j
                cc.append(np.arange(14344 + i * 4096 + dc * 128, 14344 + i * 4096 + (dc + 1) * 128))
            blocks.append(blk(w_in, np.concatenate(cc)))
        blocks.append(blk(wbr, np.arange(dp * 256, (dp + 1) * 256)))
    for ob in range(16):
        blocks.append(blk(w_out, np.arange(ob * 256, (ob + 1) * 256)))
    assert len(blocks) == 128
    consts = {
        "gcol": np.ascontiguousarray(g.reshape(NCH, 128).T),
        "wpool": np.ascontiguousarray(w_pool.reshape(4, 2, 128, 256).transpose(2, 0, 1, 3).reshape(128, 8, 256)),
        "pscale": np.ascontiguousarray(pool_scale.reshape(8, 128).T),
        "wconv": np.ascontiguousarray(w_conv.reshape(3, 8, 128).transpose(2, 0, 1)),
        "bmerge": np.ascontiguousarray(b_merge.reshape(4, NCH, 128).transpose(2, 0, 1)),
    }
    return consts, np.stack(blocks, axis=0)


def icnt_table(pos0, n):
    pos = np.arange(pos0, pos0 + n)
    tab = np.stack([1.0 / np.minimum(pos + 1, w) for w in (2, 4, 8, 16)], axis=0).astype(np.float32)
    return np.ascontiguousarray(np.broadcast_to(tab[None], (128, 4, n)))


def run_phase_c(h, oo, consts, wq):
    nc = get_prog("c")
    in_maps = []
    for c in range(NCORES):
        b = c // (NCORES // BATCH)
        off = (c % (NCORES // BATCH)) * TPC
        r0 = c * TPC
        xh = np.zeros((128, D), np.float32) if off == 0 else np.ascontiguousarray(h[r0 - 128:r0])
        oa = np.ascontiguousarray(oo[:, b, :, :, off:off + TPC].transpose(1, 0, 2, 3).reshape(2, W, TPC))
        m = dict(consts)
        m.update({"x": np.ascontiguousarray(h[r0:r0 + TPC]), "xh": xh, "oa": oa, "icnt": icnt_table(off, TPC), "wq": wq})
        in_maps.append(m)
    res = run_bass_kernel_spmd(nc, in_maps, core_ids=list(range(NCORES)))
    return np.concatenate([r["y"] for r in res.results], axis=0)


def build_phase_d(TPC=TPC):
    nc = bass.Bass("TRN2", target_bir_lowering=False)
    x = nc.dram_tensor("x", [TPC, D], F32, kind="ExternalInput").ap()
    g = nc.dram_tensor("g", [128, D], F32, kind="ExternalInput").ap()
    y = nc.dram_tensor("y", [TPC, D], F32, kind="ExternalOutput").ap()
    with ExitStack() as es:
        kb = KB(nc, es)
        grep = kb.sb("grep", [128, D], F32)
        NB_ = 2
        xb = [kb.sb(f"xb{i}", [128, D], F32) for i in range(NB_)]
        yb = [kb.sb(f"yb{i}", [128, D], F32) for i in range(NB_)]
        junk = kb.sb("junk", [128, D], BF16)
        ss = kb.sb("ss", [128, 1], F32)
        rs = kb.sb("rs", [128, 1], F32)
        epsT = kb.sb("epsT", [128, 1], F32)
        last = {"act": None, "dve": None}

        def chain(e, ins, *deps):
            kb.wait(e, last[e], *deps)
            t = kb.mark(ins(), e)
            last[e] = t
            return t

        csem = DmaSem(kb, "csem")
        xs = [DmaSem(kb, f"xs{i}") for i in range(NB_)]
        ys = [DmaSem(kb, f"ys{i}") for i in range(NB_)]
        xfree = [None] * NB_
        c1 = csem.inc(nc.sync.dma_start(out=grep[:], in_=g[:, :]))
        t_e = chain("dve", lambda: nc.vector.memset(epsT[:], EPS), c1)
        kb.wait("act", t_e)
        for i in range(TPC // 128):
            s_ = i % NB_
            kb.wait("sp", xfree[s_])
            ld = xs[s_].inc(nc.sync.dma_start(out=xb[s_][:], in_=x[i * 128:(i + 1) * 128, :]))
            chain("act", lambda: nc.scalar.activation(out=junk[:], in_=xb[s_][:], func=AF.Square, accum_out=ss[:]), ld, last["dve"])
            t_a = chain("act", lambda: nc.scalar.activation(out=rs[:], in_=ss[:], func=AF.Sqrt, scale=1.0 / D, bias=epsT[:]))
            chain("dve", lambda: nc.vector.reciprocal(out=rs[:], in_=rs[:]), t_a)
            t_y = chain("dve", lambda: nc.vector.scalar_tensor_tensor(out=yb[s_][:], in0=xb[s_][:], scalar=rs[:, 0:1], in1=grep[:],
                                                                     op0=ALU.mult, op1=ALU.mult), ys[s_].tok())
            xfree[s_] = t_y
            kb.wait("sp", t_y)
            ys[s_].inc(nc.sync.dma_start(out=y[i * 128:(i + 1) * 128, :], in_=yb[s_][:]))
        for s_ in ys:
            kb.wait("sp", s_.tok())
    return nc


def run_phase_d(h, final_g):
    nc = get_prog("d")
    grep = np.ascontiguousarray(np.broadcast_to(final_g[None, :], (128, D)))
    in_maps = [{"x": np.ascontiguousarray(h[c * TPC:(c + 1) * TPC]), "g": grep} for c in range(NCORES)]
    res = run_bass_kernel_spmd(nc, in_maps, core_ids=list(range(NCORES)))
    return np.concatenate([r["y"] for r in res.results], axis=0)


WCH = 4096


def build_cast(NP):
    nc = bass.Bass("TRN2", target_bir_lowering=False)
    x = nc.dram_tensor("x", [NP, 128, WCH], F32, kind="ExternalInput").ap()
    y = nc.dram_tensor("y", [NP, 128, WCH], BF16, kind="ExternalOutput").ap()
    with ExitStack() as es:
        kb = KB(nc, es)
        NB_ = 4
        xb = [kb.sb(f"xb{i}", [128, WCH], F32) for i in range(NB_)]
        yb = [kb.sb(f"yb{i}", [128, WCH], BF16) for i in range(NB_)]
        xs = [DmaSem(kb, f"xs{i}") for i in range(NB_)]
        ys = [DmaSem(kb, f"ys{i}") for i in range(NB_)]
        xfree = [None] * NB_
        engs = ["dve", "pool", "act"]
        for i in range(NP):
            s_ = i % NB_
            e = engs[i % 3]
            kb.wait("sp", xfree[s_])
            ld = xs[s_].inc(nc.sync.dma_start(out=xb[s_][:], in_=x[i, :, :]))
            kb.wait(e, ld, ys[s_].tok())
            if e == "dve":
                ins = nc.vector.tensor_copy(out=yb[s_][:], in_=xb[s_][:])
            elif e == "pool":
                ins = nc.gpsimd.tensor_copy(out=yb[s_][:], in_=xb[s_][:])
            else:
                ins = nc.scalar.copy(out=yb[s_][:], in_=xb[s_][:])
            t = kb.mark(ins, e)
            xfree[s_] = t
            kb.wait("sp", t)
            ys[s_].inc(nc.sync.dma_start(out=y[i, :, :], in_=yb[s_][:]))
        for s_ in ys:
            kb.wait("sp", s_.tok())
    return nc


def run_cast(arrays):
    sizes = [a.size for a in arrays]
    total = sum(sizes)
    unit = NCORES * 128 * WCH
    npc = -(-total // unit)
    flat = np.zeros(npc * unit, np.float32)
    off = 0
    for a in arrays:
        flat[off:off + a.size] = a.reshape(-1)
        off += a.size
    flat = flat.reshape(NCORES, npc, 128, WCH)
    key = ("w", npc)
    if key not in _CACHE:
        _CACHE[key] = build_cast(npc)
    res = run_bass_kernel_spmd(_CACHE[key], [{"x": flat[c]} for c in range(NCORES)], core_ids=list(range(NCORES)))
    out = np.concatenate([np.asarray(r["y"]).reshape(-1) for r in res.results])
    outs = []
    off = 0
    for a in arrays:
        outs.append(out[off:off + a.size].reshape(a.shape))
        off += a.size
    return outs


def kernel(x, norm_g, w_in, b_forget, w_pool, pool_scale, w_conv, w_branch, b_merge, w_out, final_g):
    f32 = lambda a: np.asarray(a, dtype=np.float32)
    x, norm_g, w_in, b_forget, w_pool, pool_scale, w_conv, w_branch, b_merge, w_out, final_g = map(
        f32, (x, norm_g, w_in, b_forget, w_pool, pool_scale, w_conv, w_branch, b_merge, w_out, final_g))
    cc, fl = [], []
    for l in range(2):
        pa = prep_a_f32(w_in[l])
        consts, wblocks = prep_c_consts(w_in[l], w_pool[l], pool_scale[l], w_conv[l], w_branch[l], b_merge[l], w_out[l], norm_g[l])
        cc.append(consts)
        fl += [pa["wa"], pa["wf"], wblocks]
    q = run_cast(fl)
    del fl
    h = np.ascontiguousarray(x.reshape(TOK, D))
    for l in range(2):
        wq_a = {"wa": q[3 * l], "wf": q[3 * l + 1]}
        qk, v, fo = run_phase_a(h, norm_g[l], wq_a)
        oo = run_phase_b(qk, v, fo, b_forget[l])
        del qk, v, fo
        h = run_phase_c(h, oo, cc[l], q[3 * l + 2])
        del oo
    out = run_phase_d(h, final_g)
    return out.reshape(BATCH, SEQ, D).astype(np.float32)
```
